# Optimizing a Trainium2 kernel written in Bass

```python
import jax, jax.numpy as jnp
from jax import lax
import numpy as np

D_MODEL = 1024
BATCH = 8
SEQ = 2048
DEPTH = 1

ML_HEADS = 4
ML_DIM = D_MODEL
ML_HEAD_DIM = ML_DIM // ML_HEADS
ML_CHUNK = 64
CONV_WIDTH = 4
FOX_HEADS = 16
FOX_HEAD_DIM = 64
FOX_DIM = FOX_HEADS * FOX_HEAD_DIM
Q_BLOCK = 128
D_FF = 2816
EPS = 1e-6
SPLIT_SIZES = (ML_DIM, ML_DIM, ML_DIM, ML_DIM, ML_HEADS, ML_HEADS,
               FOX_DIM, FOX_DIM, FOX_DIM, FOX_HEADS, D_MODEL, D_MODEL)
N_IN = sum(SPLIT_SIZES)

kernel_name = 'macaron_mlstm_fox_gated_hybrid'


def _split_points():
    pts, acc = [], 0
    for s in SPLIT_SIZES[:-1]:
        acc += s
        pts.append(acc)
    return pts


def rmsnorm(x, g):
    xf = x.astype(jnp.float32)
    y = xf * lax.rsqrt(jnp.mean(xf * xf, axis=-1, keepdims=True) + EPS)
    return (y * g.astype(jnp.float32)).astype(x.dtype)


def swiglu(h, w_gate, w_up, w_down):
    return (jax.nn.silu(h @ w_gate) * (h @ w_up)) @ w_down


def causal_depthwise_conv(u, w, b):
    T = u.shape[1]
    up = jnp.pad(u, ((0, 0), (CONV_WIDTH - 1, 0), (0, 0)))
    y = b
    for j in range(CONV_WIDTH):
        y = y + up[:, j:j + T] * w[j]
    return y


def to_heads(u, n_heads):
    B, T, C = u.shape
    return u.reshape(B, T, n_heads, C // n_heads).transpose(0, 2, 1, 3)


def from_heads(u):
    B, H, T, Dh = u.shape
    return u.transpose(0, 2, 1, 3).reshape(B, T, H * Dh)


def mlstm_chunkwise(q, k, v, log_i, log_f):
    B, H, T, Dk = q.shape
    Dv = v.shape[-1]
    L = ML_CHUNK
    nc = T // L

    def to_chunks(a):
        return jnp.moveaxis(a.reshape(B, H, nc, L, *a.shape[3:]), 2, 0)

    xs = tuple(to_chunks(a) for a in (q, k, v, log_i, log_f))
    causal = jnp.tril(jnp.ones((L, L), dtype=bool))

    def step(carry, chunk):
        C, n, m = carry
        qx, kx, vx, ix, fx = chunk
        b = jnp.cumsum(fx, axis=-1)
        dmat = jnp.where(causal, b[..., :, None] - b[..., None, :] + ix[..., None, :], -jnp.inf)
        inter = b + m[..., None]
        m_t = jnp.maximum(jnp.max(dmat, axis=-1), inter)
        w_intra = jnp.exp(dmat - m_t[..., None])
        w_inter = jnp.exp(inter - m_t)
        s = jnp.einsum('bhtd,bhsd->bhts', qx, kx) * w_intra
        num = jnp.einsum('bhts,bhsv->bhtv', s, vx) + w_inter[..., None] * jnp.einsum('bhvd,bhtd->bhtv', C, qx)
        den = jnp.sum(s, axis=-1) + w_inter * jnp.einsum('bhd,bhtd->bht', n, qx)
        h = num / jnp.maximum(jnp.abs(den), jnp.exp(-m_t))[..., None]
        b_last = b[..., -1]
        log_w = b_last[..., None] - b + ix
        m_new = jnp.maximum(b_last + m, jnp.max(log_w, axis=-1))
        decay = jnp.exp(b_last + m - m_new)
        w = jnp.exp(log_w - m_new[..., None])
        C_new = decay[..., None, None] * C + jnp.einsum('bhsv,bhsd->bhvd', vx * w[..., None], kx)
        n_new = decay[..., None] * n + jnp.einsum('bhs,bhsd->bhd', w, kx)
        return (C_new, n_new, m_new), h

    init = (jnp.zeros((B, H, Dv, Dk), jnp.float32),
            jnp.zeros((B, H, Dk), jnp.float32),
            jnp.zeros((B, H), jnp.float32))
    _, hc = lax.scan(step, init, xs)
    return jnp.moveaxis(hc, 0, 2).reshape(B, H, T, Dv)


def forgetting_attention(q, k, v, log_f):
    T = q.shape[2]
    c = jnp.cumsum(log_f, axis=-1)
    scale = FOX_HEAD_DIM ** -0.5
    outs = []
    for blk in range(T // Q_BLOCK):
        q0 = blk * Q_BLOCK
        q1 = q0 + Q_BLOCK
        logits = (jnp.einsum('bhtd,bhsd->bhts', q[:, :, q0:q1], k[:, :, :q1]) * scale
                  + c[:, :, q0:q1, None] - c[:, :, None, :q1])
        mask = (q0 + jnp.arange(Q_BLOCK))[:, None] >= jnp.arange(q1)[None, :]
        p = jax.nn.softmax(jnp.where(mask, logits, -jnp.inf), axis=-1)
        outs.append(jnp.einsum('bhts,bhsd->bhtd', p, v[:, :, :q1]))
    return jnp.concatenate(outs, axis=2)


def headwise_layernorm(h, g):
    mu = jnp.mean(h, axis=-1, keepdims=True)
    var = jnp.mean(jnp.square(h - mu), axis=-1, keepdims=True)
    hn = (h - mu) * lax.rsqrt(var + EPS)
    return from_heads(hn) * g.astype(jnp.float32)


def setup_inputs(seed: int = 0) -> dict:
    key = jax.random.key(seed)
    ks = jax.random.split(key, 20)
    f32 = jnp.float32
    d = D_MODEL

    def w(k, shape, fan_in):
        return jax.random.normal(k, shape, f32) * fan_in ** -0.5

    def gain(k, shape):
        return 1.0 + 0.02 * jax.random.normal(k, shape, f32)

    x = jax.random.normal(ks[0], (BATCH, SEQ, d), f32)
    pts = [0] + _split_points()
    ml_f0 = pts[5]
    fx_f0 = pts[9]
    b_in = 0.02 * jax.random.normal(ks[1], (DEPTH, N_IN), f32)
    b_in = b_in.at[:, ml_f0:ml_f0 + ML_HEADS].add(jnp.linspace(3.0, 6.0, ML_HEADS, dtype=f32))
    b_in = b_in.at[:, fx_f0:fx_f0 + FOX_HEADS].add(jnp.linspace(1.0, 4.0, FOX_HEADS, dtype=f32))
    return {
        'x': x,
        'ffn1_norm': gain(ks[2], (DEPTH, d)),
        'ffn1_w_gate': w(ks[3], (DEPTH, d, D_FF), d),
        'ffn1_w_up': w(ks[4], (DEPTH, d, D_FF), d),
        'ffn1_w_down': w(ks[5], (DEPTH, D_FF, d), D_FF),
        'mix_norm': gain(ks[6], (DEPTH, d)),
        'w_in': w(ks[7], (DEPTH, d, N_IN), d),
        'b_in': b_in,
        'conv_w': w(ks[8], (DEPTH, CONV_WIDTH, 2 * ML_DIM), CONV_WIDTH),
        'conv_b': 0.02 * jax.random.normal(ks[9], (DEPTH, 2 * ML_DIM), f32),
        'ml_head_norm': gain(ks[10], (DEPTH, ML_DIM)),
        'w_out': w(ks[11], (DEPTH, d, d), d),
        'ffn2_norm': gain(ks[12], (DEPTH, d)),
        'ffn2_w_gate': w(ks[13], (DEPTH, d, D_FF), d),
        'ffn2_w_up': w(ks[14], (DEPTH, d, D_FF), d),
        'ffn2_w_down': w(ks[15], (DEPTH, D_FF, d), D_FF),
        'final_norm': gain(ks[16], (d,)),
    }


def reference(x, ffn1_norm, ffn1_w_gate, ffn1_w_up, ffn1_w_down, mix_norm, w_in, b_in,
              conv_w, conv_b, ml_head_norm, w_out, ffn2_norm, ffn2_w_gate, ffn2_w_up,
              ffn2_w_down, final_norm):
    f32 = jnp.float32
    for l in range(DEPTH):
        x = x + 0.5 * swiglu(rmsnorm(x, ffn1_norm[l]), ffn1_w_gate[l], ffn1_w_up[l], ffn1_w_down[l])

        h = rmsnorm(x, mix_norm[l])
        proj = (h @ w_in[l] + b_in[l]).astype(f32)
        (ml_q, ml_k, ml_v, ml_o, ml_i, ml_f,
         fx_q, fx_k, fx_v, fx_f, g_a, g_b) = jnp.split(proj, _split_points(), axis=-1)

        qk = jax.nn.silu(causal_depthwise_conv(jnp.concatenate([ml_q, ml_k], axis=-1),
                                               conv_w[l].astype(f32), conv_b[l].astype(f32)))
        q_a = to_heads(qk[..., :ML_DIM], ML_HEADS)
        k_a = to_heads(qk[..., ML_DIM:], ML_HEADS) * ML_HEAD_DIM ** -0.5
        v_a = to_heads(ml_v, ML_HEADS)
        log_i = ml_i.transpose(0, 2, 1)
        log_f_a = jax.nn.log_sigmoid(ml_f).transpose(0, 2, 1)
        h_a = mlstm_chunkwise(q_a, k_a, v_a, log_i, log_f_a)
        y_a = jax.nn.sigmoid(ml_o) * headwise_layernorm(h_a, ml_head_norm[l])

        log_f_b = jax.nn.log_sigmoid(fx_f).transpose(0, 2, 1)
        h_b = forgetting_attention(to_heads(fx_q, FOX_HEADS), to_heads(fx_k, FOX_HEADS),
                                   to_heads(fx_v, FOX_HEADS), log_f_b)
        y_b = from_heads(h_b)

        y = jax.nn.sigmoid(g_a) * y_a + jax.nn.sigmoid(g_b) * y_b
        x = x + y.astype(x.dtype) @ w_out[l]

        x = x + 0.5 * swiglu(rmsnorm(x, ffn2_norm[l]), ffn2_w_gate[l], ffn2_w_up[l], ffn2_w_down[l])
    return rmsnorm(x, final_norm)
```

```python
import numpy as np
from contextlib import ExitStack
import concourse.bass as bass
import concourse.mybir as mybir
from concourse.bass_utils import run_bass_kernel_spmd

F32 = mybir.dt.float32
BF16 = mybir.dt.bfloat16
AF = mybir.ActivationFunctionType
ALU = mybir.AluOpType

D = 1024
T = 2048
DFF = 2816
NIN = 9240
NDC = 8
NFC = 22
NG = 4
NTI = 16
EPS = 1e-6
O_MLQ, O_MLK, O_MLV, O_MLO, O_MLI, O_MLF = 0, 1024, 2048, 3072, 4096, 4100
O_FXQ, O_FXK, O_FXV, O_FXF, O_GA, O_GB = 4104, 5128, 6152, 7176, 7192, 8216
LN16 = 2.772588722239781


class Buf:
    __slots__ = ("name", "w", "r", "dsem", "dcnt")

    def __init__(self, name):
        self.name = name
        self.w = None
        self.r = {}
        self.dsem = None
        self.dcnt = 0


class Prog:
    ENG = ("pe", "act", "dve", "pool", "sp")

    def __init__(self, nc, stack):
        self.nc = nc
        self.stack = stack
        self.ops = {e: [] for e in self.ENG}
        self.tick = {e: 0 for e in self.ENG}
        self.waited = {e: {} for e in self.ENG}
        self.sems = {}
        self.nbuf = 0

    def buf(self, name):
        self.nbuf += 1
        return Buf(f"{name}_{self.nbuf}")

    def sem(self, name):
        if name not in self.sems:
            self.sems[name] = self.stack.enter_context(self.nc.semaphore(name))
        return self.sems[name]

    def _waits(self, eng, reads, writes, extra):
        ws = list(extra)
        for b in reads:
            if b.w is not None:
                ws.append(b.w)
        for b in writes:
            if b.w is not None:
                ws.append(b.w)
            ws.extend(b.r.items())
        out = []
        for (s, v) in ws:
            if eng == "pe" and s == "pe":
                continue
            if self.waited[eng].get(s, 0) >= v:
                continue
            self.waited[eng][s] = v
            out.append((s, v))
        return out

    def _update(self, tick, reads, writes):
        for b in reads:
            b.r[tick[0]] = max(b.r.get(tick[0], 0), tick[1])
        for b in writes:
            b.w = tick
            b.r = {}

    def op(self, eng, fn, reads=(), writes=(), signal=True, waits=()):
        w = self._waits(eng, reads, writes, waits)
        if signal:
            self.tick[eng] += 1
            tick = (eng, self.tick[eng])
            self.ops[eng].append((w, fn, ("inc", eng)))
        else:
            assert eng == "pe"
            tick = (eng, self.tick[eng] + 1)
            self.ops[eng].append((w, fn, None))
        self._update(tick, reads, writes)
        return tick

    def dma(self, eng, out, in_, owner, reads=(), writes=(), **kw):
        w = self._waits(eng, reads, writes, ())
        if owner.dsem is None:
            owner.dsem = "d_" + owner.name
            self.sem(owner.dsem)
        owner.dcnt += 1
        tick = (owner.dsem, 16 * owner.dcnt)
        self.ops[eng].append((w, (lambda e: e.dma_start(out=out, in_=in_, **kw)), ("dma", owner.dsem)))
        self._update(tick, reads, writes)
        return tick

    def wait_only(self, eng, ticks):
        w = self._waits(eng, (), (), ticks)
        if w:
            self.ops[eng].append((w, None, None))

    def barrier(self, engines=("pe", "act", "dve", "sp", "pool")):
        ticks = [(e, self.tick[e]) for e in ("pe", "act", "dve", "pool") if self.tick[e] > 0]
        for e in engines:
            self.wait_only(e, [t for t in ticks if t[0] != e])

    def replay(self):
        nc = self.nc
        for e in self.ENG:
            self.sem(e)
        sems = self.sems
        ops = self.ops

        def run(engname, engobj):
            for (w, fn, sig) in ops[engname]:
                for (s, v) in w:
                    engobj.wait_ge(sems[s], v)
                if fn is None:
                    continue
                inst = fn(engobj)
                if sig is not None:
                    inst.then_inc(sems[sig[1]], 16 if sig[0] == "dma" else 1)

        with nc.Block() as block:
            @block.tensor
            def _(e):
                run("pe", e)

            @block.scalar
            def _(e):
                run("act", e)

            @block.vector
            def _(e):
                run("dve", e)

            @block.gpsimd
            def _(e):
                run("pool", e)

            @block.sync
            def _(e):
                run("sp", e)


def build(stop_after="all", do_fox=True, do_mlstm=True):
    nc = bass.Bass("TRN2", target_bir_lowering=False)
    dt_in = lambda name, shape: nc.dram_tensor(name, shape, F32, kind="ExternalInput").ap()
    x_d = dt_in("x", [T, D])
    g1_d = dt_in("ffn1_norm", [D])
    wg1_d = dt_in("ffn1_w_gate", [D, DFF])
    wu1_d = dt_in("ffn1_w_up", [D, DFF])
    wd1_d = dt_in("ffn1_w_down", [DFF, D])
    gm_d = dt_in("mix_norm", [D])
    win_d = dt_in("w_in", [D, NIN])
    bin_d = dt_in("b_in", [NIN])
    cw_d = dt_in("conv_w", [4, 2 * D])
    cb_d = dt_in("conv_b", [2 * D])
    hn_d = dt_in("ml_head_norm", [D])
    wo_d = dt_in("w_out", [D, D])
    g2_d = dt_in("ffn2_norm", [D])
    wg2_d = dt_in("ffn2_w_gate", [D, DFF])
    wu2_d = dt_in("ffn2_w_up", [D, DFF])
    wd2_d = dt_in("ffn2_w_down", [DFF, D])
    gf_d = dt_in("final_norm", [D])
    out_d = nc.dram_tensor("out", [T, D], F32, kind="ExternalOutput").ap()
    scr_d = nc.dram_tensor("fox_scr", [6, 16, T], BF16, kind="Internal").ap()

    with ExitStack() as st:
        P = Prog(nc, st)
        _uid = [0]

        def sbt(stack, name, shape, dt):
            _uid[0] += 1
            return stack.enter_context(nc.sbuf_tensor(f"{name}_{_uid[0]}", shape, dt))

        XT = sbt(st, "XT", [128, NDC, T], F32)
        XTb = [P.buf(f"xt{g}") for g in range(NG)]
        CF = sbt(st, "CF", [128, 1232], F32)
        CB = sbt(st, "CB", [128, 512], BF16)
        identf = CF[:, 0:128]
        tri128 = CF[:, 128:256]
        tri64 = CF[:, 256:384]
        ones64 = CF[:, 384:512]
        oneslo = CF[:, 512:640]
        oneshi = CF[:, 640:768]
        allones = CF[:, 768:896]
        mask2 = CF[:, 896:960]
        gcols = CF[:, 960:992].rearrange("p (a c) -> p a c", c=NDC)
        bfm = CF[:, 992:1024].rearrange("p (a c) -> p a c", c=NDC)
        cwc = CF[:, 1024:1088].rearrange("p (a c) -> p a c", c=16)
        cbc = CF[:, 1088:1104]
        epsc = CF[:, 1104:1105]
        identb = CB[:, 0:128]
        onesms = CB[:, 128:256]
        onesb = CB[:, 256:384]
        maskneg = CB[:, 384:512]
        RSL = 2048
        NRING = 4
        ring_t = [sbt(st, f"ring{i}", [128, RSL], BF16) for i in range(NRING)]
        ring_b = [P.buf(f"ring{i}") for i in range(NRING)]
        ring_i = [0]
        banks = [st.enter_context(nc.psum_tensor(f"bank{i}", [128, 512], F32)) for i in range(6)]
        bank67 = st.enter_context(nc.psum_tensor("bank67", [128, 1024], F32))
        banks.append(bank67[:, 0:512])
        banks.append(bank67[:, 512:1024])
        bankb = [P.buf(f"bank{i}") for i in range(8)]
        cbuf = P.buf("consts")
        scrb = P.buf("scr")

        def ring_next():
            i = ring_i[0] % NRING
            ring_i[0] += 1
            return ring_t[i], ring_b[i]

        def load_slab(w_d, r0, nrow_chunks, c0, ncols, tens, tb, dst_off=0):
            src = w_d[r0:r0 + nrow_chunks * 128, c0:c0 + ncols].rearrange("(c p) n -> p c n", p=128)
            dst = tens[:, dst_off:dst_off + nrow_chunks * ncols].rearrange("p (c n) -> p c n", n=ncols)
            P.dma("pool", dst, src, owner=tb, writes=[tb])
            return dst

        def setup():
            P.dma("sp", gcols[:, 0, :], g1_d.rearrange("(c p) -> p c", p=128), owner=cbuf, allow_slow_non_contiguous=True)
            P.dma("sp", gcols[:, 1, :], gm_d.rearrange("(c p) -> p c", p=128), owner=cbuf, allow_slow_non_contiguous=True)
            P.dma("sp", gcols[:, 2, :], g2_d.rearrange("(c p) -> p c", p=128), owner=cbuf, allow_slow_non_contiguous=True)
            P.dma("sp", gcols[:, 3, :], gf_d.rearrange("(c p) -> p c", p=128), owner=cbuf, allow_slow_non_contiguous=True)
            for k, o in enumerate((O_MLQ, O_MLK, O_FXQ, O_FXK)):
                P.dma("sp", bfm[:, k, :], bin_d[o:o + 1024].rearrange("(c p) -> p c", p=128), owner=cbuf, allow_slow_non_contiguous=True)
            for j in range(4):
                P.dma("sp", cwc[:, j, :], cw_d[j, :].rearrange("(c p) -> p c", p=128), owner=cbuf, allow_slow_non_contiguous=True)
            P.dma("sp", cbc[:, :], cb_d.rearrange("(c p) -> p c", p=128), owner=cbuf, allow_slow_non_contiguous=True)
            V = lambda fn: P.op("dve", fn)
            t = V(lambda e: e.memset(identb[:], 0.0))
            t2 = V(lambda e: e.memset(identf[:], 0.0))
            t3 = V(lambda e: e.memset(tri128[:], 1.0))
            V(lambda e: e.memset(onesms[:], 1.0 / D))
            V(lambda e: e.memset(onesb[:], 1.0))
            V(lambda e: e.memset(allones[:], 1.0))
            V(lambda e: e.memset(epsc[:], EPS))
            V(lambda e: e.memset(ones64[:], 0.0))
            V(lambda e: e.memset(oneslo[:], 0.0))
            V(lambda e: e.memset(oneshi[:], 0.0))
            t4 = V(lambda e: e.memset(ones64[0:64, 0:64], 1.0))
            t4 = V(lambda e: e.memset(ones64[64:128, 64:128], 1.0))
            V(lambda e: e.memset(oneslo[0:64, :], 1.0))
            t5 = V(lambda e: e.memset(oneshi[64:128, :], 1.0))
            aff = lambda tens, cm, pat, cmp: (lambda e: e.affine_select(
                out=tens[:], in_=tens[:], pattern=[[pat, 128]], compare_op=cmp, fill=(1.0 if cmp == ALU.not_equal else 0.0),
                base=0, channel_multiplier=cm))
            p1 = P.op("pool", aff(identb, 1, -1, ALU.not_equal), waits=[t])
            p2 = P.op("pool", aff(identf, 1, -1, ALU.not_equal), waits=[t2])
            p3 = P.op("pool", aff(tri128, -1, 1, ALU.is_ge), waits=[t3])
            v = P.op("dve", lambda e: e.tensor_copy(out=tri64[:], in_=tri128[:]), waits=[p3])
            v = P.op("dve", lambda e: e.memset(tri64[0:64, 64:128], 0.0), waits=[v])
            v = P.op("dve", lambda e: e.tensor_copy(out=mask2[0:64, :], in_=tri128[0:64, 0:64]), waits=[v])
            v = P.op("dve", lambda e: e.tensor_copy(out=mask2[64:128, :], in_=tri128[64:128, 64:128]), waits=[v])
            v = P.op("dve", lambda e: e.tensor_scalar(out=maskneg[:], in0=tri128[:], scalar1=-1.0, scalar2=30000.0,
                                                      op0=ALU.add, op1=ALU.mult), waits=[v])
            for e_ in ("pe", "act", "dve", "sp"):
                P.wait_only(e_, [(cbuf.dsem, 16 * cbuf.dcnt)])
            P.barrier()

        bank_rr = [0]

        def rms_to_hT(gi, HT, hTb, ph, groups=range(NG), out_fn=None):
            if out_fn is None:
                with ExitStack() as tmp:
                    _rms(gi, HT, hTb, tmp, groups, None)
                    P.barrier()
            else:
                _rms(gi, HT, hTb, ph, groups, out_fn)

        def _rms(gi, HT, hTb, ph, groups, out_fn):
            sqt = sbt(ph, "sqt", [128, 2, NDC, 512], BF16)
            sq = [sqt[:, j, :, :] for j in range(2)]
            sqb = [P.buf("sq") for _ in range(2)]
            rst = sbt(ph, "rst", [128, 2, 512], F32)
            rs = [rst[:, j, :] for j in range(2)]
            rsb = [P.buf("rs") for _ in range(2)]
            for g in groups:
                j = g % 2
                tok = slice(g * 512, (g + 1) * 512)
                P.op("act", lambda e, j=j, tok=tok: e.activation(out=sq[j][:], in_=XT[:, :, tok], func=AF.Square),
                     reads=[XTb[g]], writes=[sqb[j]])
                bk = bank_rr[0] % 8
                bank_rr[0] += 1
                for dc in range(NDC):
                    P.op("pe", lambda e, j=j, dc=dc, bk=bk: e.matmul(banks[bk][:, :], lhsT=onesms[:], rhs=sq[j][:, dc, :],
                                                                        start=(dc == 0), stop=(dc == NDC - 1)),
                         reads=[sqb[j]], writes=[bankb[bk]], signal=(dc == NDC - 1))
                P.op("act", lambda e, j=j, bk=bk: e.activation(out=rs[j][:], in_=banks[bk][:, :], func=AF.Sqrt, bias=epsc[:]),
                     reads=[bankb[bk]], writes=[rsb[j]])
                P.op("dve", lambda e, j=j: e.reciprocal(out=rs[j][:], in_=rs[j][:]), reads=[rsb[j]], writes=[rsb[j]])
                if out_fn is not None:
                    out_fn(g, rs[j], rsb[j])
                    continue
                for dc in range(NDC):
                    P.op("dve", lambda e, j=j, dc=dc, tok=tok: e.scalar_tensor_tensor(
                        out=HT[:, dc, tok], in0=XT[:, dc, tok], scalar=gcols[:, gi, dc:dc + 1], in1=rs[j][:],
                        op0=ALU.mult, op1=ALU.mult), reads=[XTb[g], rsb[j]], writes=[hTb[g]])

        def phase_load():
            with ExitStack() as ph:
                xs = [sbt(ph, f"xs{j}", [128, D], F32) for j in range(4)]
                xsb = [P.buf("xs") for _ in range(4)]
                for i in range(NTI):
                    j = i % 4
                    P.dma("sp", xs[j][:], x_d[i * 128:(i + 1) * 128, :], owner=xsb[j], writes=[xsb[j]])
                    for h in range(2):
                        bk = bank_rr[0] % 8
                        bank_rr[0] += 1
                        for k in range(4):
                            dc = 4 * h + k
                            P.op("pe", lambda e, j=j, dc=dc, k=k, bk=bk: e.transpose(
                                out=banks[bk][:, k * 128:(k + 1) * 128], in_=xs[j][:, dc * 128:(dc + 1) * 128], identity=identf[:]),
                                reads=[xsb[j]], writes=[bankb[bk]], signal=(k == 3))
                        eng = "act" if (2 * i + h) % 2 == 0 else "dve"
                        src = lambda bk=bk: banks[bk][:, :].rearrange("p (c n) -> p c n", n=128)
                        dst = lambda i=i, h=h: XT[:, 4 * h:4 * h + 4, i * 128:(i + 1) * 128]
                        if eng == "act":
                            P.op("act", lambda e, src=src, dst=dst: e.copy(out=dst(), in_=src()), reads=[bankb[bk]], writes=[XTb[i // 4]])
                        else:
                            P.op("dve", lambda e, src=src, dst=dst: e.tensor_copy(out=dst(), in_=src()), reads=[bankb[bk]], writes=[XTb[i // 4]])
                P.barrier()

        def phase_ffn(gi, wg_d, wu_d, wd_d):
            with ExitStack() as ph:
                HT = sbt(ph, "ffn_hT", [128, NDC, T], BF16)
                hTb = [P.buf("hT") for _ in range(NG)]
                rms_to_hT(gi, HT, hTb, ph)
                TH = T // 2
                AT = sbt(ph, "ffn_aT", [128, NFC, TH], BF16)
                aTb = [P.buf("aT") for _ in range(2)]
                sgt = sbt(ph, "sgt", [128, 2, 512], BF16)
                sg = [sgt[:, j, :] for j in range(2)]
                sgb = [P.buf("sg") for _ in range(2)]
                cnt = 0
                for th in range(2):
                    for fp in range(NFC // 2):
                        tg, tgb = ring_next()
                        wgs = load_slab(wg_d, 0, NDC, fp * 256, 256, tg, tgb)
                        tu, tub = ring_next()
                        wus = load_slab(wu_d, 0, NDC, fp * 256, 256, tu, tub)
                        for fi in range(2):
                            fc = 2 * fp + fi
                            fcol = slice(fi * 128, (fi + 1) * 128)
                            bs = 4 * (cnt % 2)
                            cnt += 1
                            for (ws, wb, boff) in ((wgs, tgb, 0), (wus, tub, 2)):
                                for dc in range(NDC):
                                    for q in range(2):
                                        g = 2 * th + q
                                        bk = bs + boff + q
                                        P.op("pe", lambda e, ws=ws, dc=dc, fcol=fcol, g=g, bk=bk: e.matmul(
                                            banks[bk][:, :], lhsT=ws[:, dc, fcol], rhs=HT[:, dc, g * 512:(g + 1) * 512],
                                            start=(dc == 0), stop=(dc == NDC - 1)),
                                            reads=[wb, hTb[g]], writes=[bankb[bk]], signal=(dc == NDC - 1))
                            for q in range(2):
                                P.op("act", lambda e, q=q, bk=bs + q: e.activation(out=sg[q][:], in_=banks[bk][:, :], func=AF.Silu),
                                     reads=[bankb[bs + q]], writes=[sgb[q]])
                                P.op("dve", lambda e, q=q, bk=bs + 2 + q, fc=fc: e.tensor_tensor(
                                    out=AT[:, fc, q * 512:(q + 1) * 512], in0=sg[q][:], in1=banks[bk][:, :], op=ALU.mult),
                                    reads=[sgb[q], bankb[bs + 2 + q]], writes=[aTb[q]])
                    for dc in range(NDC):
                        t0, t0b = ring_next()
                        s0 = load_slab(wd_d, 0, 11, dc * 128, 128, t0, t0b)
                        t1, t1b = ring_next()
                        s1 = load_slab(wd_d, 11 * 128, 11, dc * 128, 128, t1, t1b)
                        bs = 2 * (dc % 4)
                        for fc in range(NFC):
                            ws, wb = (s0, t0b) if fc < 11 else (s1, t1b)
                            for q in range(2):
                                bk = bs + q
                                P.op("pe", lambda e, ws=ws, fc=fc, q=q, bk=bk: e.matmul(
                                    banks[bk][:, :], lhsT=ws[:, fc % 11, :], rhs=AT[:, fc, q * 512:(q + 1) * 512],
                                    start=(fc == 0), stop=(fc == NFC - 1)),
                                    reads=[wb, aTb[q]], writes=[bankb[bk]], signal=(fc == NFC - 1))
                        for q in range(2):
                            bk = bs + q
                            g = 2 * th + q
                            P.op("dve", lambda e, dc=dc, g=g, bk=bk: e.scalar_tensor_tensor(
                                out=XT[:, dc, g * 512:(g + 1) * 512], in0=banks[bk][:, :], scalar=0.5,
                                in1=XT[:, dc, g * 512:(g + 1) * 512], op0=ALU.mult, op1=ALU.add),
                                reads=[bankb[bk], XTb[g]], writes=[XTb[g]])
                P.barrier()

        def phase_final():
            with ExitStack() as ph:
                FT = sbt(ph, "fT", [128, NDC, 512], F32)
                FTb = P.buf("fT")
                ost = [sbt(ph, f"ost{j}", [128, D], F32) for j in range(2)]
                ostb = [P.buf("ost") for _ in range(2)]
                stores = []

                def out_fn(g, rs, rsb):
                    tok = slice(g * 512, (g + 1) * 512)
                    for dc in range(NDC):
                        P.op("dve", lambda e, dc=dc, tok=tok, rs=rs: e.scalar_tensor_tensor(
                            out=FT[:, dc, :], in0=XT[:, dc, tok], scalar=gcols[:, 3, dc:dc + 1], in1=rs[:],
                            op0=ALU.mult, op1=ALU.mult), reads=[XTb[g], rsb], writes=[FTb])
                    for it in range(4):
                        i = 4 * g + it
                        j = i % 2
                        for h in range(2):
                            bk = bank_rr[0] % 8
                            bank_rr[0] += 1
                            for k in range(4):
                                dc = 4 * h + k
                                P.op("pe", lambda e, dc=dc, k=k, bk=bk, it=it: e.transpose(
                                    out=banks[bk][:, k * 128:(k + 1) * 128], in_=FT[:, dc, it * 128:(it + 1) * 128], identity=identf[:]),
                                    reads=[FTb], writes=[bankb[bk]], signal=(k == 3))
                            if h == 0:
                                P.op("act", lambda e, j=j, bk=bk: e.copy(out=ost[j][:, 0:512], in_=banks[bk][:, :]),
                                     reads=[bankb[bk]], writes=[ostb[j]])
                            else:
                                P.op("dve", lambda e, j=j, bk=bk: e.tensor_copy(out=ost[j][:, 512:1024], in_=banks[bk][:, :]),
                                     reads=[bankb[bk]], writes=[ostb[j]])
                        stores.append(P.dma("sp", out_d[i * 128:(i + 1) * 128, :], ost[j][:], owner=ostb[j], reads=[ostb[j]]))

                rms_to_hT(3, None, None, ph, out_fn=out_fn)
                P.wait_only("sp", stores)

        def phase_mixer():
            with ExitStack() as ph:
                HT = sbt(ph, "mx_hT", [128, NDC, T], BF16)
                hTb = [P.buf("hT") for _ in range(NG)]
                Y = sbt(ph, "mx_Y", [128, NTI, D], BF16)
                Yb = [P.buf("Y") for _ in range(NTI)]
                rms_to_hT(1, HT, hTb, ph)
                GF = sbt(ph, "GF", [128, 2592], F32)
                GFW = 2592
                gpre = GF[:, 0:384].rearrange("p (j i) -> p j i", i=NTI)
                lf = GF[:, 384:704].rearrange("p (j i) -> p j i", i=NTI)
                tmpa = GF[:, 704:1024].rearrange("p (j i) -> p j i", i=NTI)
                tmpb = GF[:, 1024:1344].rearrange("p (j i) -> p j i", i=NTI)
                bsb = GF[:, 1344:1408]
                wk = GF[:, 1408:1472]
                wv = GF[:, 1472:1536]
                eb = GF[:, 1536:1600]
                ebL = GF[:, 1600:1728]
                tot = GF[:, 1728:1984].rearrange("p (h i) -> p h i", i=NTI)
                OFF_O = 1984
                off = GF[:, 1984:2240].rearrange("p (h i) -> p h i", i=NTI)
                cc_ = GF[:, 2240:2496].rearrange("p (h i) -> p h i", i=NTI)
                GB_O = 2496
                gbias = GF[:, 2496:2520]
                ieb = GF[:, 2528:2592]
                gb = P.buf("gates")

                tg, tgb = ring_next()
                gsl = tg[:, 0:NDC * 24].rearrange("p (c n) -> p c n", n=24)
                P.dma("pool", gsl[:, :, 0:8], win_d[:, O_MLI:O_MLI + 8].rearrange("(c p) n -> p c n", p=128), owner=tgb, writes=[tgb])
                P.dma("pool", gsl[:, :, 8:24], win_d[:, O_FXF:O_FXF + 16].rearrange("(c p) n -> p c n", p=128), owner=tgb, writes=[tgb])
                P.dma("sp", gbias[:, 0:8], bin_d[O_MLI:O_MLI + 8].partition_broadcast(128), owner=gb, writes=[gb])
                P.dma("sp", gbias[:, 8:24], bin_d[O_FXF:O_FXF + 16].partition_broadcast(128), owner=gb, writes=[gb])
                bk = 0
                for i in range(NTI):
                    for dc in range(NDC):
                        last = (i == NTI - 1 and dc == NDC - 1)
                        P.op("pe", lambda e, i=i, dc=dc: e.matmul(
                            banks[0][:, i * 24:(i + 1) * 24], lhsT=HT[:, dc, i * 128:(i + 1) * 128], rhs=gsl[:, dc, :],
                            start=(i == 0 and dc == 0), stop=(dc == NDC - 1), skip_group_check=True),
                            reads=[tgb, hTb[i // 4]], writes=[bankb[0]], signal=last)
                P.op("dve", lambda e: e.tensor_tensor(
                    out=gpre.rearrange("p j i -> p i j"), in0=banks[0][:, 0:NTI * 24].rearrange("p (i j) -> p i j", j=24),
                    in1=bass.AP(GF, GB_O, [[GFW, 128], [0, NTI], [1, 24]]), op=ALU.add), reads=[bankb[0], gb], writes=[gb])
                xg = gpre[:, 4:24, :]
                DV = lambda fn: P.op("dve", fn, reads=[gb], writes=[gb])
                AC = lambda fn: P.op("act", fn, reads=[gb], writes=[gb])
                DV(lambda e: e.tensor_scalar(out=tmpa[:], in0=xg, scalar1=-1.0, scalar2=None, op0=ALU.mult))
                DV(lambda e: e.tensor_tensor(out=tmpa[:], in0=tmpa[:], in1=xg, op=ALU.max))
                AC(lambda e: e.activation(out=tmpa[:], in_=tmpa[:], func=AF.Exp, scale=-1.0))
                AC(lambda e: e.activation(out=tmpa[:], in_=tmpa[:], func=AF.Ln, bias=1.0))
                DV(lambda e: e.tensor_scalar(out=tmpb[:], in0=xg, scalar1=0.0, scalar2=None, op0=ALU.min))
                DV(lambda e: e.tensor_tensor(out=lf[:], in0=tmpb[:], in1=tmpa[:], op=ALU.subtract))
                lfa = GF[:, 384:448]
                lfb = GF[:, 448:704]
                li = GF[:, 0:64]
                for k, m in enumerate((tri64, ones64, oneslo, oneshi)):
                    P.op("pe", lambda e, k=k, m=m: e.matmul(banks[1][:, k * 64:(k + 1) * 64], lhsT=m[:], rhs=lfa,
                                                             start=(k == 0), stop=True, skip_group_check=True),
                         reads=[gb], writes=[bankb[1]], signal=(k == 3))
                P.op("pe", lambda e: e.matmul(banks[2][:, 0:256], lhsT=tri128[:], rhs=lfb, start=True, stop=True, skip_group_check=True),
                     reads=[gb], writes=[bankb[2]], signal=False)
                P.op("pe", lambda e: e.matmul(banks[2][:, 256:512], lhsT=allones[:], rhs=lfb, start=False, stop=True, skip_group_check=True),
                     reads=[gb], writes=[bankb[2]], signal=True)
                A1 = lambda fn: P.op("act", fn, reads=[gb, bankb[1]], writes=[gb])
                D1 = lambda fn: P.op("dve", fn, reads=[gb, bankb[1]], writes=[gb])
                A1(lambda e: e.copy(out=bsb[:], in_=banks[1][:, 0:64]))
                D1(lambda e: e.tensor_tensor(out=wk[:], in0=li, in1=bsb[:], op=ALU.subtract))
                D1(lambda e: e.tensor_tensor(out=wv[:], in0=banks[1][:, 64:128], in1=bsb[:], op=ALU.subtract))
                D1(lambda e: e.tensor_tensor(out=wv[:], in0=wv[:], in1=li, op=ALU.add))
                A1(lambda e: e.activation(out=wk[:], in_=wk[:], func=AF.Exp, bias=-LN16))
                A1(lambda e: e.activation(out=wv[:], in_=wv[:], func=AF.Exp, bias=-LN16))
                A1(lambda e: e.activation(out=eb[:], in_=bsb[:], func=AF.Exp))
                A1(lambda e: e.activation(out=ieb[:], in_=bsb[:], func=AF.Exp, scale=-1.0))
                A1(lambda e: e.activation(out=ebL[:], in_=banks[1][:, 128:256], func=AF.Exp))
                A2 = lambda fn: P.op("act", fn, reads=[gb, bankb[2]], writes=[gb])
                D2 = lambda fn: P.op("dve", fn, reads=[gb, bankb[2]], writes=[gb])
                A2(lambda e: e.copy(out=GF[:, 1728:1984], in_=banks[2][:, 256:512]))
                D2(lambda e: e.memset(off[:, :, 0:1], 0.0))
                for i in range(1, NTI):
                    D2(lambda e, i=i: e.tensor_tensor(out=off[:, :, i:i + 1], in0=off[:, :, i - 1:i], in1=tot[:, :, i - 1:i], op=ALU.add))
                D2(lambda e: e.tensor_tensor(out=GF[:, 2240:2496], in0=banks[2][:, 0:256], in1=GF[:, 1984:2240], op=ALU.add))

                if not do_fox:
                    for i in range(NTI):
                        P.op("dve", lambda e, i=i: e.memset(Y[:, i, :], 0.0), writes=[Yb[i]])
                with ExitStack() as pf:
                    qh = [sbt(pf, f"qh{j}", [128, T], BF16) for j in range(2)]
                    kh = [sbt(pf, f"kh{j}", [128, T], BF16) for j in range(2)]
                    qkb = [P.buf("qk2a"), P.buf("qk2b")]
                    VAf = sbt(pf, "VAf", [128, NTI, 2, 65], BF16)
                    VAb = P.buf("VA")
                    sgbt = sbt(pf, "sgb", [128, NTI, 128], BF16)
                    sgbb = P.buf("sgb")
                    BRf = sbt(pf, "BRf", [1, 512], BF16)
                    brb = P.buf("brow")
                    ones512 = sbt(pf, "ones512", [1, 512], BF16)
                    pTt = sbt(pf, "pTft", [128, 4, 512], BF16)
                    pTf = [pTt[:, j, :] for j in range(4)]
                    pTb = [P.buf("pT") for _ in range(4)]
                    rect = sbt(pf, "rect", [128, 2, 4], F32)
                    recb = [P.buf("rec") for _ in range(2)]
                    RW = sbt(pf, "RW", [128, 6, 256], F32)
                    PCS = sbt(pf, "PCS", [128, 6, 256], BF16)
                    rwb = P.buf("rw")
                    P.op("dve", lambda e: e.memset(VAf[:, :, :, 64:65], 1.0), writes=[VAb])
                    P.op("dve", lambda e: e.memset(ones512[:], 1.0), writes=[brb])
                    if do_fox:
                        for j in range(2):
                            P.op("dve", lambda e, j=j: e.memset(qh[j][64:70, :], 1.0), writes=[qkb[j]])
                            P.op("dve", lambda e, j=j: e.memset(kh[j][64:70, :], 1.0), writes=[qkb[j]])
                    def fox_rows():
                        for (k, src) in ((0, cc_), (1, off)):
                            for par in range(2):
                                P.op("dve", lambda e, k=k, src=src, par=par: e.tensor_copy(
                                    out=RW[:, 2 + k, par * 128:(par + 1) * 128].rearrange("p (h g) -> p h g", g=8),
                                    in_=src[:, :, par::2]), reads=[gb], writes=[rwb])
                            for par in range(2):
                                P.op("pe", lambda e, k=k, par=par: e.transpose(
                                    out=banks[6][:, k * 256 + par * 128:k * 256 + (par + 1) * 128],
                                    in_=RW[:, 2 + k, par * 128:(par + 1) * 128], identity=identf),
                                    reads=[rwb], writes=[bankb[6]], signal=(par == 1))
                            P.op("act", lambda e, k=k: e.copy(out=RW[:, k, :], in_=banks[6][:, k * 256:(k + 1) * 256]),
                                 reads=[bankb[6]], writes=[rwb])
                        DR = lambda fn: P.op("dve", fn, reads=[rwb], writes=[rwb])
                        for (k, pc0, mul) in ((1, 0, 8.0), (0, 3, -8.0)):
                            x8 = RW[:, 2, :]
                            r1 = RW[:, 3, :]
                            DR(lambda e, k=k, mul=mul, x8=x8: e.tensor_scalar(out=x8, in0=RW[:, k, :], scalar1=mul, scalar2=None, op0=ALU.mult))
                            DR(lambda e, pc0=pc0, x8=x8: e.tensor_copy(out=PCS[:, pc0, :], in_=x8))
                            DR(lambda e, pc0=pc0, x8=x8, r1=r1: e.tensor_tensor(out=r1, in0=x8, in1=PCS[:, pc0, :], op=ALU.subtract))
                            DR(lambda e, pc0=pc0, r1=r1: e.tensor_copy(out=PCS[:, pc0 + 1, :], in_=r1))
                            DR(lambda e, pc0=pc0, r1=r1: e.tensor_tensor(out=r1, in0=r1, in1=PCS[:, pc0 + 1, :], op=ALU.subtract))
                            DR(lambda e, pc0=pc0, r1=r1: e.tensor_copy(out=PCS[:, pc0 + 2, :], in_=r1))
                        for j in range(6):
                            P.dma("sp", scr_d[j].rearrange("h (g n) -> (h g) n", n=256), PCS[:, j, :], owner=rwb, reads=[rwb], writes=[scrb])
                    scnt = 0
                    for p in (range(8) if do_fox else ()):
                        tq, tqb = ring_next()
                        wq = load_slab(win_d, 0, NDC, O_FXQ + p * 128, 128, tq, tqb, 0)
                        wkk = load_slab(win_d, 0, NDC, O_FXK + p * 128, 128, tq, tqb, NDC * 128)
                        tv, tvb = ring_next()
                        wvg = tv[:, 0:NDC * 256].rearrange("p (c n) -> p c n", n=256)
                        P.dma("pool", wvg[:, :, 0:128], win_d[:, O_FXV + p * 128:O_FXV + (p + 1) * 128].rearrange("(c p) n -> p c n", p=128),
                              owner=tvb, writes=[tvb])
                        P.dma("pool", wvg[:, :, 128:256], win_d[:, O_GB + p * 128:O_GB + (p + 1) * 128].rearrange("(c p) n -> p c n", p=128),
                              owner=tvb, writes=[tvb])
                        for k, o in enumerate((O_FXV, O_GB, O_FXQ, O_FXK)):
                            P.dma("pool", BRf[:, k * 128:(k + 1) * 128], bin_d[o + p * 128:o + (p + 1) * 128].rearrange("(o n) -> o n", o=1),
                                  owner=brb, writes=[brb])
                        for (ws, dst, bseg) in ((wq, qh, 2), (wkk, kh, 3)):
                            for g in range(NG):
                                bk = 6 + (scnt % 2)
                                scnt += 1
                                for dc in range(NDC):
                                    P.op("pe", lambda e, ws=ws, dc=dc, g=g, bk=bk: e.matmul(
                                        banks[bk][:, :], lhsT=ws[:, dc, :], rhs=HT[:, dc, g * 512:(g + 1) * 512],
                                        start=(dc == 0), stop=False),
                                        reads=[tqb, hTb[g]], writes=[bankb[bk]], signal=False)
                                P.op("pe", lambda e, bk=bk, bseg=bseg: e.matmul(
                                    banks[bk][:, :], lhsT=BRf[:, bseg * 128:(bseg + 1) * 128], rhs=ones512[:], start=False, stop=True),
                                    reads=[brb], writes=[bankb[bk]], signal=True)
                                P.op("act", lambda e, dst=dst, g=g, bk=bk: e.copy(out=dst[0][0:64, g * 512:(g + 1) * 512], in_=banks[bk][0:64, :]),
                                     reads=[bankb[bk]], writes=[qkb[0]])
                                P.op("dve", lambda e, dst=dst, g=g, bk=bk: e.tensor_copy(out=dst[1][0:64, g * 512:(g + 1) * 512], in_=banks[bk][64:128, :]),
                                     reads=[bankb[bk]], writes=[qkb[1]])
                        for i in range(NTI):
                            bk = 6 + (scnt % 2)
                            scnt += 1
                            for dc in range(NDC):
                                P.op("pe", lambda e, wvg=wvg, dc=dc, i=i, bk=bk: e.matmul(
                                    banks[bk][:, 0:256], lhsT=HT[:, dc, i * 128:(i + 1) * 128], rhs=wvg[:, dc, :],
                                    start=(dc == 0), stop=False, skip_group_check=True),
                                    reads=[tvb, hTb[i // 4]], writes=[bankb[bk]], signal=False)
                            P.op("pe", lambda e, bk=bk: e.matmul(banks[bk][:, 0:256], lhsT=onesb[0:1, :], rhs=BRf[:, 0:256], start=False, stop=True,
                                                                 skip_group_check=True), reads=[brb], writes=[bankb[bk]], signal=True)
                            P.op("act", lambda e, i=i, bk=bk: e.copy(out=VAf[:, i, :, 0:64], in_=banks[bk][:, 0:128].rearrange("p (h d) -> p h d", d=64)),
                                 reads=[bankb[bk]], writes=[VAb])
                            P.op("act", lambda e, i=i, bk=bk: e.activation(out=sgbt[:, i, :], in_=banks[bk][:, 128:256], func=AF.Sigmoid),
                                 reads=[bankb[bk]], writes=[sgbb])
                        if p == 0:
                            fox_rows()
                        for hl in range(2):
                            h = 2 * p + hl
                            P.dma("sp", qh[hl][67:70, :], scr_d[0:3, h, :], owner=qkb[hl], reads=[scrb], writes=[qkb[hl]])
                            P.dma("sp", kh[hl][64:67, :], scr_d[3:6, h, :], owner=qkb[hl], reads=[scrb], writes=[qkb[hl]])
                        items = [(hl, tg, kb) for hl in range(2) for tg in range(NG) for kb in range(4 * tg + 4)]
                        LOOK = 2

                        def emit_s(n):
                            hl, tg, kb = items[n]
                            sb_ = n % 4
                            t0 = max(kb * 128, tg * 512)
                            t1 = (tg + 1) * 512
                            N = t1 - t0
                            diag = (kb >= 4 * tg)
                            P.op("pe", lambda e, hl=hl, kb=kb, t0=t0, t1=t1, N=N, sb_=sb_, diag=diag: e.matmul(
                                banks[sb_][:, 0:N], lhsT=kh[hl][0:70, kb * 128:(kb + 1) * 128], rhs=qh[hl][0:70, t0:t1],
                                start=True, stop=(not diag)), reads=[qkb[hl]], writes=[bankb[sb_]], signal=(not diag))
                            if diag:
                                P.op("pe", lambda e, sb_=sb_: e.matmul(banks[sb_][:, 0:128], lhsT=identb, rhs=maskneg,
                                                                      start=False, stop=True), writes=[bankb[sb_]], signal=True)
                            P.op("act", lambda e, sb_=sb_, N=N: e.activation(out=pTf[sb_][:, 0:N], in_=banks[sb_][:, 0:N], func=AF.Exp, scale=0.125),
                                 reads=[bankb[sb_]], writes=[pTb[sb_]])

                        def emit_pv(n):
                            hl, tg, kb = items[n]
                            h = 2 * p + hl
                            sb_ = n % 4
                            ab = 4 + (tg % 2)
                            tb0 = max(kb, 4 * tg)
                            for tb in range(tb0, 4 * tg + 4):
                                j = tb - 4 * tg
                                c0 = (tb - tb0) * 128
                                first = (kb == 0 and j == 0)
                                last = (kb == 4 * tg + 3)
                                P.op("pe", lambda e, sb_=sb_, kb=kb, hl=hl, ab=ab, j=j, c0=c0, first=first: e.matmul(
                                    banks[ab][:, j * 65:(j + 1) * 65], lhsT=pTf[sb_][:, c0:c0 + 128], rhs=VAf[:, kb, hl, :],
                                    start=first, stop=True, skip_group_check=True),
                                    reads=[pTb[sb_], VAb], writes=[bankb[ab]], signal=(tb == 4 * tg + 3))
                            if kb == 4 * tg + 3:
                                rj = tg % 2
                                P.op("dve", lambda e, rj=rj, ab=ab: e.reciprocal(
                                    out=rect[:, rj, :], in_=banks[ab][:, 0:260].rearrange("p (j c) -> p j c", c=65)[:, :, 64]),
                                    reads=[bankb[ab]], writes=[recb[rj]])
                                for j in range(4):
                                    tb = 4 * tg + j
                                    P.op("dve", lambda e, rj=rj, ab=ab, tb=tb, j=j, hl=hl, h=h: e.scalar_tensor_tensor(
                                        out=Y[:, tb, h * 64:(h + 1) * 64], in0=banks[ab][:, j * 65:j * 65 + 64], scalar=rect[:, rj, j:j + 1],
                                        in1=sgbt[:, tb, hl * 64:(hl + 1) * 64], op0=ALU.mult, op1=ALU.mult),
                                        reads=[bankb[ab], recb[rj], sgbb], writes=[Yb[tb]])

                        for n in range(min(LOOK, len(items))):
                            emit_s(n)
                        for n in range(len(items)):
                            if n + LOOK < len(items):
                                emit_s(n + LOOK)
                            emit_pv(n)
                    P.barrier()


                with ExitStack() as pm:
                    qT = sbt(pm, "qT", [128, 2, T], BF16)
                    kT = sbt(pm, "kT", [128, 2, T], BF16)
                    qkb = P.buf("qk")
                    VA = sbt(pm, "VAm", [128, NTI, 258], BF16)
                    VAb = P.buf("VAm")
                    GAt = sbt(pm, "GAt", [128, 3, 256], BF16)
                    GA = [GAt[:, j, :] for j in range(3)]
                    GAb = [P.buf("GA") for _ in range(3)]
                    gn = sbt(pm, "gn", [128, 256], F32)
                    gnb = P.buf("gn")
                    BR = sbt(pm, "BRm", [1, 768], BF16)
                    brb = P.buf("browm")
                    stgt = sbt(pm, "stgt", [128, 2, 516], F32)
                    stg = [stgt[:, j, :] for j in range(2)]
                    stgb = [P.buf("stg") for _ in range(2)]
                    cacct = sbt(pm, "cacc", [128, 2, 512], F32)
                    cacc = [cacct[:, j, :] for j in range(2)]
                    caccb = [P.buf("cacc") for _ in range(2)]
                    sig = sbt(pm, "sig", [128, 512], F32)
                    sigb = P.buf("sig")
                    CT = sbt(pm, "CT", [128, 2, 257], F32)
                    CTb = P.buf("CT")
                    CTht = sbt(pm, "CTht", [128, 2, 2, 258], BF16)
                    CTh = [CTht[:, j, :, 0:257] for j in range(2)]
                    CThb = [P.buf("CTh") for _ in range(2)]
                    ktokt = sbt(pm, "ktokt", [128, 2, 256], BF16)
                    ktok = [ktokt[:, j, :] for j in range(2)]
                    ktokb = [P.buf("ktok") for _ in range(2)]
                    pTt = sbt(pm, "pTmt", [128, 2, 64], BF16)
                    pT = [pTt[:, j, :] for j in range(2)]
                    pTb = [P.buf("pTm") for _ in range(2)]
                    hnnt = sbt(pm, "hnnt", [128, 2, 256], BF16)
                    hnn = [hnnt[:, j, :] for j in range(2)]
                    hnnb = [P.buf("hnn") for _ in range(2)]
                    smt_ = sbt(pm, "smt", [128, 2, 16], F32)
                    sm = [smt_[:, j, :] for j in range(2)]
                    smb = [P.buf("sm") for _ in range(2)]
                    scnt = 0
                    for hd in (range(4) if do_mlstm else ()):
                        tq, tqb = ring_next()
                        wq = load_slab(win_d, 0, NDC, O_MLQ + hd * 256, 256, tq, tqb)
                        tk, tkb = ring_next()
                        wkk = load_slab(win_d, 0, NDC, O_MLK + hd * 256, 256, tk, tkb)
                        tv, tvb = ring_next()
                        wvv = load_slab(win_d, 0, NDC, O_MLV + hd * 256, 256, tv, tvb)
                        for k, o in enumerate((O_MLV, O_MLO, O_GA)):
                            P.dma("pool", BR[:, k * 256:(k + 1) * 256], bin_d[o + hd * 256:o + (hd + 1) * 256].rearrange("(o n) -> o n", o=1),
                                  owner=brb, writes=[brb])
                        P.dma("sp", gn[:], hn_d[hd * 256:(hd + 1) * 256].partition_broadcast(128), owner=gnb, writes=[gnb])
                        pend = [None]

                        def flush_silu():
                            if pend[0] is not None:
                                dstv, cj = pend[0]
                                P.op("act", lambda e, dstv=dstv, cj=cj: e.activation(out=dstv, in_=cacc[cj], func=AF.Silu),
                                     reads=[caccb[cj]], writes=[qkb])
                                pend[0] = None

                        for (ws, wb, dstT, bcol, cofs) in ((wq, tqb, qT, 0, 0), (wkk, tkb, kT, 1, 8)):
                            for cc in range(2):
                                ch = hd * 2 + cc
                                cch = cofs + ch
                                for g in range(NG):
                                    bk = (scnt % 2)
                                    j = scnt % 2
                                    scnt += 1
                                    for dc in range(NDC):
                                        P.op("pe", lambda e, ws=ws, dc=dc, cc=cc, g=g, bk=bk: e.matmul(
                                            banks[bk][:, :], lhsT=ws[:, dc, cc * 128:(cc + 1) * 128], rhs=HT[:, dc, g * 512:(g + 1) * 512],
                                            start=(dc == 0), stop=(dc == NDC - 1)),
                                            reads=[wb, hTb[g]], writes=[bankb[bk]], signal=(dc == NDC - 1))
                                    if g == 0:
                                        P.op("dve", lambda e, j=j: e.memset(stg[j][:, 0:3], 0.0), writes=[stgb[j]])
                                    else:
                                        P.op("dve", lambda e, j=j: e.tensor_copy(out=stg[j][:, 0:3], in_=stg[1 - j][:, 512:515]),
                                             reads=[stgb[1 - j]], writes=[stgb[j]])
                                    P.op("act", lambda e, j=j, bk=bk, bcol=bcol, ch=ch: e.activation(
                                        out=stg[j][:, 3:515], in_=banks[bk][:, :], func=AF.Identity, bias=bfm[:, bcol, ch:ch + 1]),
                                        reads=[bankb[bk]], writes=[stgb[j]])
                                    flush_silu()
                                    P.op("dve", lambda e, j=j, cch=cch: e.tensor_scalar(
                                        out=cacc[j], in0=stg[j][:, 3:515], scalar1=cwc[:, 3, cch:cch + 1], scalar2=cbc[:, cch:cch + 1],
                                        op0=ALU.mult, op1=ALU.add), reads=[stgb[j]], writes=[caccb[j]])
                                    for jj in range(3):
                                        P.op("dve", lambda e, j=j, cch=cch, jj=jj: e.scalar_tensor_tensor(
                                            out=cacc[j], in0=stg[j][:, jj:jj + 512], scalar=cwc[:, jj, cch:cch + 1], in1=cacc[j],
                                            op0=ALU.mult, op1=ALU.add), reads=[stgb[j], caccb[j]], writes=[caccb[j]])
                                    pend[0] = (dstT[:, cc, g * 512:(g + 1) * 512], j)
                        flush_silu()
                        to, tob = ring_next()
                        woo = load_slab(win_d, 0, NDC, O_MLO + hd * 256, 256, to, tob)
                        ta, tab = ring_next()
                        wga = load_slab(win_d, 0, NDC, O_GA + hd * 256, 256, ta, tab)
                        P.op("dve", lambda e: e.memset(VA[:, :, 256:257], 1.0), writes=[VAb])
                        for i in range(NTI):
                            bk = (scnt % 2)
                            scnt += 1
                            for dc in range(NDC):
                                P.op("pe", lambda e, dc=dc, i=i, bk=bk, wvv=wvv: e.matmul(
                                    banks[bk][:, 0:256], lhsT=HT[:, dc, i * 128:(i + 1) * 128], rhs=wvv[:, dc, :],
                                    start=(dc == 0), stop=False, skip_group_check=True),
                                    reads=[tvb, hTb[i // 4]], writes=[bankb[bk]], signal=False)
                            P.op("pe", lambda e, bk=bk: e.matmul(banks[bk][:, 0:256], lhsT=onesb[0:1, :], rhs=BR[:, 0:256], start=False, stop=True,
                                                                 skip_group_check=True), reads=[brb], writes=[bankb[bk]], signal=True)
                            P.op("act", lambda e, i=i, bk=bk: e.copy(out=VA[:, i, 0:256], in_=banks[bk][:, 0:256]),
                                 reads=[bankb[bk]], writes=[VAb])
                        P.op("dve", lambda e: e.memset(CT[:], 0.0), writes=[CTb])
                        P.op("dve", lambda e: e.memset(CTh[0], 0.0), writes=[CThb[0]])
                        b3bf = banks[3].bitcast(BF16)
                        UB = ((6, 7), (6, 7))

                        def gates(i):
                            s3 = i % 3
                            for dc in range(NDC):
                                for (ws, wb2, c0) in ((woo, tob, 0), (wga, tab, 256)):
                                    P.op("pe", lambda e, ws=ws, dc=dc, i=i, c0=c0: e.matmul(
                                        banks[0][:, c0:c0 + 256], lhsT=HT[:, dc, i * 128:(i + 1) * 128], rhs=ws[:, dc, :],
                                        start=(dc == 0 and c0 == 0), stop=False, skip_group_check=True),
                                        reads=[wb2, hTb[i // 4]], writes=[bankb[0]], signal=False)
                            P.op("pe", lambda e: e.matmul(banks[0][:, :], lhsT=onesb[0:1, :], rhs=BR[:, 256:768], start=False, stop=True,
                                                          skip_group_check=True), reads=[brb], writes=[bankb[0]], signal=True)
                            P.op("act", lambda e: e.activation(out=sig[:], in_=banks[0][:, :], func=AF.Exp, scale=-1.0),
                                 reads=[bankb[0]], writes=[sigb])
                            P.op("act", lambda e: e.activation(out=sig[:], in_=sig[:], func=AF.Ln, bias=1.0), reads=[sigb], writes=[sigb])
                            P.op("dve", lambda e: e.tensor_tensor(out=sig[:, 0:256], in0=sig[:, 0:256], in1=sig[:, 256:512], op=ALU.add),
                                 reads=[sigb], writes=[sigb])

                        def gatesB(i):
                            s3 = i % 3
                            P.op("act", lambda e: e.activation(out=sig[:, 256:512], in_=sig[:, 0:256], func=AF.Exp, scale=-1.0),
                                 reads=[sigb], writes=[sigb])
                            P.op("dve", lambda e, s3=s3: e.tensor_tensor(out=GA[s3], in0=sig[:, 256:512], in1=gn[:], op=ALU.mult),
                                 reads=[sigb, gnb], writes=[GAb[s3]])

                        def pre(c):
                            i, hf = c // 2, c % 2
                            rows = slice(hf * 64, hf * 64 + 64)
                            tok = slice(c * 64, (c + 1) * 64)
                            s2 = c % 2
                            gcol = hd * NTI + i
                            for cc in range(2):
                                P.op("pe", lambda e, rows=rows, cc=cc, tok=tok: e.transpose(
                                    out=b3bf[rows, cc * 128:(cc + 1) * 128], in_=kT[:, cc, tok], identity=identb),
                                    reads=[qkb], writes=[bankb[3]], signal=(cc == 1))
                            P.op("act", lambda e, rows=rows, s2=s2, gcol=gcol: e.activation(
                                out=ktok[s2][rows, :], in_=b3bf[rows, 0:256], func=AF.Copy, scale=wv[rows, gcol:gcol + 1]),
                                reads=[bankb[3], gb], writes=[ktokb[s2]])
                            for cc in range(2):
                                P.op("pe", lambda e, rows=rows, cc=cc, tok=tok: e.matmul(
                                    banks[2][rows, 0:64], lhsT=kT[:, cc, tok], rhs=qT[:, cc, tok], start=(cc == 0), stop=(cc == 1)),
                                    reads=[qkb], writes=[bankb[2]], signal=(cc == 1))
                            P.op("dve", lambda e, rows=rows, s2=s2, gcol=gcol: e.scalar_tensor_tensor(
                                out=pT[s2][rows, :], in0=banks[2][rows, 0:64], scalar=wk[rows, gcol:gcol + 1], in1=mask2[rows, :],
                                op0=ALU.mult, op1=ALU.mult), reads=[bankb[2], gb], writes=[pTb[s2]])

                        def main(c):
                            i, hf = c // 2, c % 2
                            rows = slice(hf * 64, hf * 64 + 64)
                            tok = slice(c * 64, (c + 1) * 64)
                            s2 = c % 2
                            s3 = i % 2
                            gcol = hd * NTI + i
                            ob = 4 + (i % 2)
                            cur = c % 2
                            ub = UB[c % 2]
                            for cc in range(2):
                                P.op("pe", lambda e, rows=rows, cc=cc, s2=s2, ub=ub, i=i: e.matmul(
                                    banks[ub[cc]][:, 0:257], lhsT=ktok[s2][rows, cc * 128:(cc + 1) * 128], rhs=VA[rows, i, 0:257],
                                    start=True, stop=True), reads=[ktokb[s2], VAb], writes=[bankb[ub[cc]]], signal=True)
                            P.op("pe", lambda e, rows=rows, s2=s2, i=i, ob=ob: e.matmul(
                                banks[ob][rows, 0:257], lhsT=pT[s2][rows, :], rhs=VA[rows, i, 0:257], start=True, stop=False),
                                reads=[pTb[s2], VAb], writes=[bankb[ob]], signal=False)
                            for cc in range(2):
                                P.op("pe", lambda e, rows=rows, cc=cc, tok=tok, ob=ob, cur=cur: e.matmul(
                                    banks[ob][rows, 0:257], lhsT=qT[:, cc, tok], rhs=CTh[cur][:, cc, :], start=False, stop=(cc == 1)),
                                    reads=[qkb, CThb[cur]], writes=[bankb[ob]], signal=(cc == 1))
                            ecol = hf * 64 + gcol
                            P.op("dve", lambda e, ecol=ecol: e.scalar_tensor_tensor(
                                out=CT[:], in0=CT[:], scalar=ebL[:, ecol:ecol + 1],
                                in1=bank67[:, :].rearrange("p (c n) -> p c n", n=512)[:, :, 0:257],
                                op0=ALU.mult, op1=ALU.add), reads=[CTb, bankb[6], bankb[7], gb], writes=[CTb])
                            P.op("act", lambda e, cur=cur: e.copy(out=CTh[1 - cur], in_=CT[:]), reads=[CTb], writes=[CThb[1 - cur]])

                        def postA(i):
                            s3 = i % 2
                            ob = 4 + (i % 2)
                            gcol = hd * NTI + i
                            smt = sm[s3]
                            SD = lambda fn: P.op("dve", fn, reads=[smb[s3], bankb[ob], gb], writes=[smb[s3]])
                            SD(lambda e: e.tensor_scalar(out=smt[:, 0:1], in0=banks[ob][:, 256:257], scalar1=-1.0, scalar2=None, op0=ALU.mult))
                            SD(lambda e: e.tensor_tensor(out=smt[:, 0:1], in0=smt[:, 0:1], in1=banks[ob][:, 256:257], op=ALU.max))
                            SD(lambda e: e.tensor_tensor(out=smt[:, 1:2], in0=smt[:, 0:1], in1=ieb[:, gcol:gcol + 1], op=ALU.max))
                            SD(lambda e: e.reciprocal(out=smt[:, 2:3], in_=smt[:, 1:2]))
                            SD(lambda e: e.bn_stats(smt[:, 4:10], banks[ob][:, 0:256]))
                            SD(lambda e: e.bn_aggr(smt[:, 10:12], smt[:, 4:10]))
                            SD(lambda e: e.scalar_tensor_tensor(out=smt[:, 3:4], in0=smt[:, 2:3], scalar=smt[:, 2:3], in1=smt[:, 11:12],
                                                                op0=ALU.mult, op1=ALU.mult))

                        def postB(i):
                            s3 = i % 2
                            smt = sm[s3]
                            SA = lambda fn: P.op("act", fn, reads=[smb[s3]], writes=[smb[s3]])
                            SA(lambda e: e.activation(out=smt[:, 12:13], in_=smt[:, 3:4], func=AF.Ln, bias=epsc))
                            SA(lambda e: e.activation(out=smt[:, 13:14], in_=smt[:, 12:13], func=AF.Exp, scale=-0.5))

                        def postC(i):
                            s3 = i % 2
                            ob = 4 + (i % 2)
                            smt = sm[s3]
                            SD = lambda fn: P.op("dve", fn, reads=[smb[s3]], writes=[smb[s3]])
                            SD(lambda e: e.tensor_tensor(out=smt[:, 14:15], in0=smt[:, 2:3], in1=smt[:, 13:14], op=ALU.mult))
                            SD(lambda e: e.scalar_tensor_tensor(out=smt[:, 15:16], in0=smt[:, 10:11], scalar=-1.0, in1=smt[:, 14:15],
                                                                op0=ALU.mult, op1=ALU.mult))
                            P.op("act", lambda e: e.activation(out=hnn[s3], in_=banks[ob][:, 0:256], func=AF.Identity,
                                                               scale=smt[:, 14:15], bias=smt[:, 15:16]),
                                 reads=[bankb[ob], smb[s3]], writes=[hnnb[s3]])

                        def postE(i):
                            s3 = i % 2
                            P.op("dve", lambda e: e.tensor_tensor(out=hnn[s3], in0=hnn[s3], in1=GA[i % 3], op=ALU.mult),
                                 reads=[hnnb[s3], GAb[i % 3]], writes=[hnnb[s3]])
                            ydst = Y[:, i, hd * 256:(hd + 1) * 256]
                            P.op("dve", lambda e: e.tensor_tensor(out=ydst, in0=ydst, in1=hnn[s3], op=ALU.add),
                                 reads=[hnnb[s3], Yb[i]], writes=[Yb[i]])

                        NCH = T // 64
                        gates(0)
                        gatesB(0)
                        pre(0)
                        for c in range(NCH):
                            if c + 1 < NCH:
                                pre(c + 1)
                            if c % 2 == 0 and c >= 2:
                                postB(c // 2 - 1)
                            main(c)
                            if c % 2 == 1:
                                postA(c // 2)
                                if c >= 3:
                                    postE((c - 3) // 2)
                                if c // 2 + 1 < NTI:
                                    gatesB(c // 2 + 1)
                            else:
                                if c >= 2:
                                    postC(c // 2 - 1)
                                if c // 2 + 1 < NTI:
                                    gates(c // 2 + 1)
                        postB(NTI - 1)
                        postC(NTI - 1)
                        postE(NTI - 1)
                    P.barrier()

                for i in range(NTI):
                    for h in range(2):
                        bk = bank_rr[0] % 8
                        bank_rr[0] += 1
                        bbf = banks[bk].bitcast(BF16)
                        for k in range(4):
                            dc = 4 * h + k
                            P.op("pe", lambda e, dc=dc, k=k, i=i, bbf=bbf: e.transpose(
                                out=bbf[:, k * 128:(k + 1) * 128], in_=Y[:, i, dc * 128:(dc + 1) * 128], identity=identb[:]),
                                reads=[Yb[i]], writes=[bankb[bk]], signal=(k == 3))
                        src = bbf[:, 0:512].rearrange("p (c n) -> p c n", n=128)
                        dst = HT[:, 4 * h:4 * h + 4, i * 128:(i + 1) * 128]
                        if (2 * i + h) % 2 == 0:
                            P.op("act", lambda e, src=src, dst=dst: e.copy(out=dst, in_=src), reads=[bankb[bk]], writes=[hTb[i // 4]])
                        else:
                            P.op("dve", lambda e, src=src, dst=dst: e.tensor_copy(out=dst, in_=src), reads=[bankb[bk]], writes=[hTb[i // 4]])
                for nn in range(4):
                    tw, twb = ring_next()
                    wo = load_slab(wo_d, 0, NDC, nn * 256, 256, tw, twb)
                    for hh in range(2):
                        ncx = 2 * nn + hh
                        bs = 4 * (ncx % 2)
                        for dc in range(NDC):
                            for g in range(NG):
                                P.op("pe", lambda e, dc=dc, g=g, hh=hh, bs=bs, wo=wo: e.matmul(
                                    banks[bs + g][:, :], lhsT=wo[:, dc, hh * 128:(hh + 1) * 128], rhs=HT[:, dc, g * 512:(g + 1) * 512],
                                    start=(dc == 0), stop=(dc == NDC - 1)),
                                    reads=[twb, hTb[g]], writes=[bankb[bs + g]], signal=(dc == NDC - 1))
                        for g in range(NG):
                            P.op("dve", lambda e, ncx=ncx, g=g, bs=bs: e.tensor_tensor(
                                out=XT[:, ncx, g * 512:(g + 1) * 512], in0=banks[bs + g][:, :], in1=XT[:, ncx, g * 512:(g + 1) * 512], op=ALU.add),
                                reads=[bankb[bs + g], XTb[g]], writes=[XTb[g]])
                P.barrier()
            return None

        setup()
        phase_load()
        if stop_after not in ("load",):
            phase_ffn(0, wg1_d, wu1_d, wd1_d)
        if stop_after not in ("load", "ffn1"):
            phase_mixer()
        if stop_after not in ("load", "ffn1", "mixer"):
            phase_ffn(2, wg2_d, wu2_d, wd2_d)
        phase_final()
        P.replay()
    return nc


_NC_CACHE = {}
_NAMES = ("ffn1_norm", "ffn1_w_gate", "ffn1_w_up", "ffn1_w_down", "mix_norm", "w_in", "b_in", "conv_w", "conv_b",
          "ml_head_norm", "w_out", "ffn2_norm", "ffn2_w_gate", "ffn2_w_up", "ffn2_w_down")


def kernel(**inputs):
    stop_after = inputs.pop("_stop_after", "all")
    do_fox = inputs.pop("_do_fox", True)
    do_mlstm = inputs.pop("_do_mlstm", True)
    x = np.ascontiguousarray(np.asarray(inputs["x"], dtype=np.float32))
    shared = {}
    for n in _NAMES:
        a = np.asarray(inputs[n], dtype=np.float32)
        shared[n] = np.ascontiguousarray(a.reshape(a.shape[1:]))
    shared["final_norm"] = np.ascontiguousarray(np.asarray(inputs["final_norm"], dtype=np.float32))
    key = (stop_after, do_fox, do_mlstm)
    if key not in _NC_CACHE:
        _NC_CACHE[key] = build(stop_after, do_fox, do_mlstm)
    nc = _NC_CACHE[key]
    in_maps = []
    for b in range(8):
        m = {"x": np.ascontiguousarray(x[b])}
        m.update(shared)
        in_maps.append(m)
    res = run_bass_kernel_spmd(nc, in_maps, core_ids=list(range(8)))
    return np.stack([np.asarray(r["out"], dtype=np.float32) for r in res.results], axis=0)
```

```python
import numpy as np
from contextlib import ExitStack
import concourse.bass as bass
import concourse.mybir as mybir
from concourse.bass_utils import run_bass_kernel_spmd

F32 = mybir.dt.float32
BF16 = mybir.dt.bfloat16
AF = mybir.ActivationFunctionType
ALU = mybir.AluOpType

D = 1024
T = 2048
DFF = 2816
NIN = 9240
NDC = 8
NFC = 22
NG = 4
NTI = 16
EPS = 1e-6
O_MLQ, O_MLK, O_MLV, O_MLO, O_MLI, O_MLF = 0, 1024, 2048, 3072, 4096, 4100
O_FXQ, O_FXK, O_FXV, O_FXF, O_GA, O_GB = 4104, 5128, 6152, 7176, 7192, 8216
LN16 = 2.772588722239781


class Buf:
    __slots__ = ("name", "w", "r", "dsem", "dcnt")

    def __init__(self, name):
        self.name = name
        self.w = None
        self.r = {}
        self.dsem = None
        self.dcnt = 0


class Prog:
    ENG = ("pe", "act", "dve", "pool", "sp")

    def __init__(self, nc, stack):
        self.nc = nc
        self.stack = stack
        self.ops = {e: [] for e in self.ENG}
        self.tick = {e: 0 for e in self.ENG}
        self.waited = {e: {} for e in self.ENG}
        self.sems = {}
        self.nbuf = 0

    def buf(self, name):
        self.nbuf += 1
        return Buf(f"{name}_{self.nbuf}")

    def sem(self, name):
        if name not in self.sems:
            self.sems[name] = self.stack.enter_context(self.nc.semaphore(name))
        return self.sems[name]

    def _waits(self, eng, reads, writes, extra):
        ws = list(extra)
        for b in reads:
            if b.w is not None:
                ws.append(b.w)
        for b in writes:
            if b.w is not None:
                ws.append(b.w)
            ws.extend(b.r.items())
        out = []
        for (s, v) in ws:
            if eng == "pe" and s == "pe":
                continue
            if self.waited[eng].get(s, 0) >= v:
                continue
            self.waited[eng][s] = v
            out.append((s, v))
        return out

    def _update(self, tick, reads, writes):
        for b in reads:
            b.r[tick[0]] = max(b.r.get(tick[0], 0), tick[1])
        for b in writes:
            b.w = tick
            b.r = {}

    def op(self, eng, fn, reads=(), writes=(), signal=True, waits=()):
        w = self._waits(eng, reads, writes, waits)
        if signal:
            self.tick[eng] += 1
            tick = (eng, self.tick[eng])
            self.ops[eng].append((w, fn, ("inc", eng)))
        else:
            assert eng == "pe"
            tick = (eng, self.tick[eng] + 1)
            self.ops[eng].append((w, fn, None))
        self._update(tick, reads, writes)
        return tick

    def dma(self, eng, out, in_, owner, reads=(), writes=(), **kw):
        w = self._waits(eng, reads, writes, ())
        if owner.dsem is None:
            owner.dsem = "d_" + owner.name
            self.sem(owner.dsem)
        owner.dcnt += 1
        tick = (owner.dsem, 16 * owner.dcnt)
        self.ops[eng].append((w, (lambda e: e.dma_start(out=out, in_=in_, **kw)), ("dma", owner.dsem)))
        self._update(tick, reads, writes)
        return tick

    def wait_only(self, eng, ticks):
        w = self._waits(eng, (), (), ticks)
        if w:
            self.ops[eng].append((w, None, None))

    def barrier(self, engines=("pe", "act", "dve", "sp", "pool")):
        ticks = [(e, self.tick[e]) for e in ("pe", "act", "dve", "pool") if self.tick[e] > 0]
        for e in engines:
            self.wait_only(e, [t for t in ticks if t[0] != e])

    def replay(self):
        nc = self.nc
        for e in self.ENG:
            self.sem(e)
        sems = self.sems
        ops = self.ops

        def run(engname, engobj):
            for (w, fn, sig) in ops[engname]:
                for (s, v) in w:
                    engobj.wait_ge(sems[s], v)
                if fn is None:
                    continue
                inst = fn(engobj)
                if sig is not None:
                    inst.then_inc(sems[sig[1]], 16 if sig[0] == "dma" else 1)

        with nc.Block() as block:
            @block.tensor
            def _(e):
                run("pe", e)

            @block.scalar
            def _(e):
                run("act", e)

            @block.vector
            def _(e):
                run("dve", e)

            @block.gpsimd
            def _(e):
                run("pool", e)

            @block.sync
            def _(e):
                run("sp", e)


def build(stop_after="all", do_fox=True, do_mlstm=True):
    nc = bass.Bass("TRN2", target_bir_lowering=False)
    dt_in = lambda name, shape: nc.dram_tensor(name, shape, F32, kind="ExternalInput").ap()
    x_d = dt_in("x", [T, D])
    g1_d = dt_in("ffn1_norm", [D])
    wg1_d = dt_in("ffn1_w_gate", [D, DFF])
    wu1_d = dt_in("ffn1_w_up", [D, DFF])
    wd1_d = dt_in("ffn1_w_down", [DFF, D])
    gm_d = dt_in("mix_norm", [D])
    win_d = dt_in("w_in", [D, NIN])
    bin_d = dt_in("b_in", [NIN])
    cw_d = dt_in("conv_w", [4, 2 * D])
    cb_d = dt_in("conv_b", [2 * D])
    hn_d = dt_in("ml_head_norm", [D])
    wo_d = dt_in("w_out", [D, D])
    g2_d = dt_in("ffn2_norm", [D])
    wg2_d = dt_in("ffn2_w_gate", [D, DFF])
    wu2_d = dt_in("ffn2_w_up", [D, DFF])
    wd2_d = dt_in("ffn2_w_down", [DFF, D])
    gf_d = dt_in("final_norm", [D])
    out_d = nc.dram_tensor("out", [T, D], F32, kind="ExternalOutput").ap()
    scr_d = nc.dram_tensor("fox_scr", [6, 16, T], BF16, kind="Internal").ap()

    with ExitStack() as st:
        P = Prog(nc, st)
        _uid = [0]

        def sbt(stack, name, shape, dt):
            _uid[0] += 1
            return stack.enter_context(nc.sbuf_tensor(f"{name}_{_uid[0]}", shape, dt))

        XT = sbt(st, "XT", [128, NDC, T], F32)
        XTb = [P.buf(f"xt{g}") for g in range(NG)]
        CF = sbt(st, "CF", [128, 1232], F32)
        CB = sbt(st, "CB", [128, 512], BF16)
        identf = CF[:, 0:128]
        tri128 = CF[:, 128:256]
        tri64 = CF[:, 256:384]
        ones64 = CF[:, 384:512]
        oneslo = CF[:, 512:640]
        oneshi = CF[:, 640:768]
        allones = CF[:, 768:896]
        mask2 = CF[:, 896:960]
        gcols = CF[:, 960:992].rearrange("p (a c) -> p a c", c=NDC)
        bfm = CF[:, 992:1024].rearrange("p (a c) -> p a c", c=NDC)
        cwc = CF[:, 1024:1088].rearrange("p (a c) -> p a c", c=16)
        cbc = CF[:, 1088:1104]
        epsc = CF[:, 1104:1105]
        identb = CB[:, 0:128]
        onesms = CB[:, 128:256]
        onesb = CB[:, 256:384]
        maskneg = CB[:, 384:512]
        RSL = 2048
        NRING = 4
        ring_t = [sbt(st, f"ring{i}", [128, RSL], BF16) for i in range(NRING)]
        ring_b = [P.buf(f"ring{i}") for i in range(NRING)]
        ring_i = [0]
        banks = [st.enter_context(nc.psum_tensor(f"bank{i}", [128, 512], F32)) for i in range(6)]
        bank67 = st.enter_context(nc.psum_tensor("bank67", [128, 1024], F32))
        banks.append(bank67[:, 0:512])
        banks.append(bank67[:, 512:1024])
        bankb = [P.buf(f"bank{i}") for i in range(8)]
        cbuf = P.buf("consts")
        scrb = P.buf("scr")

        def ring_next():
            i = ring_i[0] % NRING
            ring_i[0] += 1
            return ring_t[i], ring_b[i]

        def load_slab(w_d, r0, nrow_chunks, c0, ncols, tens, tb, dst_off=0):
            src = w_d[r0:r0 + nrow_chunks * 128, c0:c0 + ncols].rearrange("(c p) n -> p c n", p=128)
            dst = tens[:, dst_off:dst_off + nrow_chunks * ncols].rearrange("p (c n) -> p c n", n=ncols)
            P.dma("pool", dst, src, owner=tb, writes=[tb])
            return dst

        def setup():
            P.dma("sp", gcols[:, 0, :], g1_d.rearrange("(c p) -> p c", p=128), owner=cbuf, allow_slow_non_contiguous=True)
            P.dma("sp", gcols[:, 1, :], gm_d.rearrange("(c p) -> p c", p=128), owner=cbuf, allow_slow_non_contiguous=True)
            P.dma("sp", gcols[:, 2, :], g2_d.rearrange("(c p) -> p c", p=128), owner=cbuf, allow_slow_non_contiguous=True)
            P.dma("sp", gcols[:, 3, :], gf_d.rearrange("(c p) -> p c", p=128), owner=cbuf, allow_slow_non_contiguous=True)
            for k, o in enumerate((O_MLQ, O_MLK, O_FXQ, O_FXK)):
                P.dma("sp", bfm[:, k, :], bin_d[o:o + 1024].rearrange("(c p) -> p c", p=128), owner=cbuf, allow_slow_non_contiguous=True)
            for j in range(4):
                P.dma("sp", cwc[:, j, :], cw_d[j, :].rearrange("(c p) -> p c", p=128), owner=cbuf, allow_slow_non_contiguous=True)
            P.dma("sp", cbc[:, :], cb_d.rearrange("(c p) -> p c", p=128), owner=cbuf, allow_slow_non_contiguous=True)
            V = lambda fn: P.op("dve", fn)
            t = V(lambda e: e.memset(identb[:], 0.0))
            t2 = V(lambda e: e.memset(identf[:], 0.0))
            t3 = V(lambda e: e.memset(tri128[:], 1.0))
            V(lambda e: e.memset(onesms[:], 1.0 / D))
            V(lambda e: e.memset(onesb[:], 1.0))
            V(lambda e: e.memset(allones[:], 1.0))
            V(lambda e: e.memset(epsc[:], EPS))
            V(lambda e: e.memset(ones64[:], 0.0))
            V(lambda e: e.memset(oneslo[:], 0.0))
            V(lambda e: e.memset(oneshi[:], 0.0))
            t4 = V(lambda e: e.memset(ones64[0:64, 0:64], 1.0))
            t4 = V(lambda e: e.memset(ones64[64:128, 64:128], 1.0))
            V(lambda e: e.memset(oneslo[0:64, :], 1.0))
            t5 = V(lambda e: e.memset(oneshi[64:128, :], 1.0))
            aff = lambda tens, cm, pat, cmp: (lambda e: e.affine_select(
                out=tens[:], in_=tens[:], pattern=[[pat, 128]], compare_op=cmp, fill=(1.0 if cmp == ALU.not_equal else 0.0),
                base=0, channel_multiplier=cm))
            p1 = P.op("pool", aff(identb, 1, -1, ALU.not_equal), waits=[t])
            p2 = P.op("pool", aff(identf, 1, -1, ALU.not_equal), waits=[t2])
            p3 = P.op("pool", aff(tri128, -1, 1, ALU.is_ge), waits=[t3])
            v = P.op("dve", lambda e: e.tensor_copy(out=tri64[:], in_=tri128[:]), waits=[p3])
            v = P.op("dve", lambda e: e.memset(tri64[0:64, 64:128], 0.0), waits=[v])
            v = P.op("dve", lambda e: e.tensor_copy(out=mask2[0:64, :], in_=tri128[0:64, 0:64]), waits=[v])
            v = P.op("dve", lambda e: e.tensor_copy(out=mask2[64:128, :], in_=tri128[64:128, 64:128]), waits=[v])
            v = P.op("dve", lambda e: e.tensor_scalar(out=maskneg[:], in0=tri128[:], scalar1=-1.0, scalar2=30000.0,
                                                      op0=ALU.add, op1=ALU.mult), waits=[v])
            for e_ in ("pe", "act", "dve", "sp"):
                P.wait_only(e_, [(cbuf.dsem, 16 * cbuf.dcnt)])
            P.barrier()

        bank_rr = [0]

        def rms_to_hT(gi, HT, hTb, ph, groups=range(NG), out_fn=None):
            if out_fn is None:
                with ExitStack() as tmp:
                    _rms(gi, HT, hTb, tmp, groups, None)
                    P.barrier()
            else:
                _rms(gi, HT, hTb, ph, groups, out_fn)

        def _rms(gi, HT, hTb, ph, groups, out_fn):
            sqt = sbt(ph, "sqt", [128, 2, NDC, 512], BF16)
            sq = [sqt[:, j, :, :] for j in range(2)]
            sqb = [P.buf("sq") for _ in range(2)]
            rst = sbt(ph, "rst", [128, 2, 512], F32)
            rs = [rst[:, j, :] for j in range(2)]
            rsb = [P.buf("rs") for _ in range(2)]
            for g in groups:
                j = g % 2
                tok = slice(g * 512, (g + 1) * 512)
                P.op("act", lambda e, j=j, tok=tok: e.activation(out=sq[j][:], in_=XT[:, :, tok], func=AF.Square),
                     reads=[XTb[g]], writes=[sqb[j]])
                bk = bank_rr[0] % 8
                bank_rr[0] += 1
                for dc in range(NDC):
                    P.op("pe", lambda e, j=j, dc=dc, bk=bk: e.matmul(banks[bk][:, :], lhsT=onesms[:], rhs=sq[j][:, dc, :],
                                                                        start=(dc == 0), stop=(dc == NDC - 1)),
                         reads=[sqb[j]], writes=[bankb[bk]], signal=(dc == NDC - 1))
                P.op("act", lambda e, j=j, bk=bk: e.activation(out=rs[j][:], in_=banks[bk][:, :], func=AF.Sqrt, bias=epsc[:]),
                     reads=[bankb[bk]], writes=[rsb[j]])
                P.op("dve", lambda e, j=j: e.reciprocal(out=rs[j][:], in_=rs[j][:]), reads=[rsb[j]], writes=[rsb[j]])
                if out_fn is not None:
                    out_fn(g, rs[j], rsb[j])
                    continue
                for dc in range(NDC):
                    P.op("dve", lambda e, j=j, dc=dc, tok=tok: e.scalar_tensor_tensor(
                        out=HT[:, dc, tok], in0=XT[:, dc, tok], scalar=gcols[:, gi, dc:dc + 1], in1=rs[j][:],
                        op0=ALU.mult, op1=ALU.mult), reads=[XTb[g], rsb[j]], writes=[hTb[g]])

        def phase_load():
            with ExitStack() as ph:
                xs = [sbt(ph, f"xs{j}", [128, D], F32) for j in range(4)]
                xsb = [P.buf("xs") for _ in range(4)]
                for i in range(NTI):
                    j = i % 4
                    P.dma("sp", xs[j][:], x_d[i * 128:(i + 1) * 128, :], owner=xsb[j], writes=[xsb[j]])
                    for h in range(2):
                        bk = bank_rr[0] % 8
                        bank_rr[0] += 1
                        for k in range(4):
                            dc = 4 * h + k
                            P.op("pe", lambda e, j=j, dc=dc, k=k, bk=bk: e.transpose(
                                out=banks[bk][:, k * 128:(k + 1) * 128], in_=xs[j][:, dc * 128:(dc + 1) * 128], identity=identf[:]),
                                reads=[xsb[j]], writes=[bankb[bk]], signal=(k == 3))
                        eng = "act" if (2 * i + h) % 2 == 0 else "dve"
                        src = lambda bk=bk: banks[bk][:, :].rearrange("p (c n) -> p c n", n=128)
                        dst = lambda i=i, h=h: XT[:, 4 * h:4 * h + 4, i * 128:(i + 1) * 128]
                        if eng == "act":
                            P.op("act", lambda e, src=src, dst=dst: e.copy(out=dst(), in_=src()), reads=[bankb[bk]], writes=[XTb[i // 4]])
                        else:
                            P.op("dve", lambda e, src=src, dst=dst: e.tensor_copy(out=dst(), in_=src()), reads=[bankb[bk]], writes=[XTb[i // 4]])
                P.barrier()

        def phase_ffn(gi, wg_d, wu_d, wd_d):
            with ExitStack() as ph:
                HT = sbt(ph, "ffn_hT", [128, NDC, T], BF16)
                hTb = [P.buf("hT") for _ in range(NG)]
                rms_to_hT(gi, HT, hTb, ph)
                TH = T // 2
                AT = sbt(ph, "ffn_aT", [128, NFC, TH], BF16)
                aTb = [P.buf("aT") for _ in range(2)]
                sgt = sbt(ph, "sgt", [128, 2, 512], BF16)
                sg = [sgt[:, j, :] for j in range(2)]
                sgb = [P.buf("sg") for _ in range(2)]
                cnt = 0
                for th in range(2):
                    for fp in range(NFC // 2):
                        tg, tgb = ring_next()
                        wgs = load_slab(wg_d, 0, NDC, fp * 256, 256, tg, tgb)
                        tu, tub = ring_next()
                        wus = load_slab(wu_d, 0, NDC, fp * 256, 256, tu, tub)
                        for fi in range(2):
                            fc = 2 * fp + fi
                            fcol = slice(fi * 128, (fi + 1) * 128)
                            bs = 4 * (cnt % 2)
                            cnt += 1
                            for (ws, wb, boff) in ((wgs, tgb, 0), (wus, tub, 2)):
                                for dc in range(NDC):
                                    for q in range(2):
                                        g = 2 * th + q
                                        bk = bs + boff + q
                                        P.op("pe", lambda e, ws=ws, dc=dc, fcol=fcol, g=g, bk=bk: e.matmul(
                                            banks[bk][:, :], lhsT=ws[:, dc, fcol], rhs=HT[:, dc, g * 512:(g + 1) * 512],
                                            start=(dc == 0), stop=(dc == NDC - 1)),
                                            reads=[wb, hTb[g]], writes=[bankb[bk]], signal=(dc == NDC - 1))
                            for q in range(2):
                                P.op("act", lambda e, q=q, bk=bs + q: e.activation(out=sg[q][:], in_=banks[bk][:, :], func=AF.Silu),
                                     reads=[bankb[bs + q]], writes=[sgb[q]])
                                P.op("dve", lambda e, q=q, bk=bs + 2 + q, fc=fc: e.tensor_tensor(
                                    out=AT[:, fc, q * 512:(q + 1) * 512], in0=sg[q][:], in1=banks[bk][:, :], op=ALU.mult),
                                    reads=[sgb[q], bankb[bs + 2 + q]], writes=[aTb[q]])
                    for dc in range(NDC):
                        t0, t0b = ring_next()
                        s0 = load_slab(wd_d, 0, 11, dc * 128, 128, t0, t0b)
                        t1, t1b = ring_next()
                        s1 = load_slab(wd_d, 11 * 128, 11, dc * 128, 128, t1, t1b)
                        bs = 2 * (dc % 4)
                        for fc in range(NFC):
                            ws, wb = (s0, t0b) if fc < 11 else (s1, t1b)
                            for q in range(2):
                                bk = bs + q
                                P.op("pe", lambda e, ws=ws, fc=fc, q=q, bk=bk: e.matmul(
                                    banks[bk][:, :], lhsT=ws[:, fc % 11, :], rhs=AT[:, fc, q * 512:(q + 1) * 512],
                                    start=(fc == 0), stop=(fc == NFC - 1)),
                                    reads=[wb, aTb[q]], writes=[bankb[bk]], signal=(fc == NFC - 1))
                        for q in range(2):
                            bk = bs + q
                            g = 2 * th + q
                            P.op("dve", lambda e, dc=dc, g=g, bk=bk: e.scalar_tensor_tensor(
                                out=XT[:, dc, g * 512:(g + 1) * 512], in0=banks[bk][:, :], scalar=0.5,
                                in1=XT[:, dc, g * 512:(g + 1) * 512], op0=ALU.mult, op1=ALU.add),
                                reads=[bankb[bk], XTb[g]], writes=[XTb[g]])
                P.barrier()

        def phase_final():
            with ExitStack() as ph:
                FT = sbt(ph, "fT", [128, NDC, 512], F32)
                FTb = P.buf("fT")
                ost = [sbt(ph, f"ost{j}", [128, D], F32) for j in range(2)]
                ostb = [P.buf("ost") for _ in range(2)]
                stores = []

                def out_fn(g, rs, rsb):
                    tok = slice(g * 512, (g + 1) * 512)
                    for dc in range(NDC):
                        P.op("dve", lambda e, dc=dc, tok=tok, rs=rs: e.scalar_tensor_tensor(
                            out=FT[:, dc, :], in0=XT[:, dc, tok], scalar=gcols[:, 3, dc:dc + 1], in1=rs[:],
                            op0=ALU.mult, op1=ALU.mult), reads=[XTb[g], rsb], writes=[FTb])
                    for it in range(4):
                        i = 4 * g + it
                        j = i % 2
                        for h in range(2):
                            bk = bank_rr[0] % 8
                            bank_rr[0] += 1
                            for k in range(4):
                                dc = 4 * h + k
                                P.op("pe", lambda e, dc=dc, k=k, bk=bk, it=it: e.transpose(
                                    out=banks[bk][:, k * 128:(k + 1) * 128], in_=FT[:, dc, it * 128:(it + 1) * 128], identity=identf[:]),
                                    reads=[FTb], writes=[bankb[bk]], signal=(k == 3))
                            if h == 0:
                                P.op("act", lambda e, j=j, bk=bk: e.copy(out=ost[j][:, 0:512], in_=banks[bk][:, :]),
                                     reads=[bankb[bk]], writes=[ostb[j]])
                            else:
                                P.op("dve", lambda e, j=j, bk=bk: e.tensor_copy(out=ost[j][:, 512:1024], in_=banks[bk][:, :]),
                                     reads=[bankb[bk]], writes=[ostb[j]])
                        stores.append(P.dma("sp", out_d[i * 128:(i + 1) * 128, :], ost[j][:], owner=ostb[j], reads=[ostb[j]]))

                rms_to_hT(3, None, None, ph, out_fn=out_fn)
                P.wait_only("sp", stores)

        def phase_mixer():
            with ExitStack() as ph:
                HT = sbt(ph, "mx_hT", [128, NDC, T], BF16)
                hTb = [P.buf("hT") for _ in range(NG)]
                Y = sbt(ph, "mx_Y", [128, NTI, D], BF16)
                Yb = [P.buf("Y") for _ in range(NTI)]
                rms_to_hT(1, HT, hTb, ph)
                GF = sbt(ph, "GF", [128, 2592], F32)
                GFW = 2592
                gpre = GF[:, 0:384].rearrange("p (j i) -> p j i", i=NTI)
                lf = GF[:, 384:704].rearrange("p (j i) -> p j i", i=NTI)
                tmpa = GF[:, 704:1024].rearrange("p (j i) -> p j i", i=NTI)
                tmpb = GF[:, 1024:1344].rearrange("p (j i) -> p j i", i=NTI)
                bsb = GF[:, 1344:1408]
                wk = GF[:, 1408:1472]
                wv = GF[:, 1472:1536]
                eb = GF[:, 1536:1600]
                ebL = GF[:, 1600:1728]
                tot = GF[:, 1728:1984].rearrange("p (h i) -> p h i", i=NTI)
                OFF_O = 1984
                off = GF[:, 1984:2240].rearrange("p (h i) -> p h i", i=NTI)
                cc_ = GF[:, 2240:2496].rearrange("p (h i) -> p h i", i=NTI)
                GB_O = 2496
                gbias = GF[:, 2496:2520]
                ieb = GF[:, 2528:2592]
                gb = P.buf("gates")

                def gates_block():
                    tg, tgb = ring_next()
                    gsl = tg[:, 0:NDC * 24].rearrange("p (c n) -> p c n", n=24)
                    P.dma("pool", gsl[:, :, 0:8], win_d[:, O_MLI:O_MLI + 8].rearrange("(c p) n -> p c n", p=128), owner=tgb, writes=[tgb])
                    P.dma("pool", gsl[:, :, 8:24], win_d[:, O_FXF:O_FXF + 16].rearrange("(c p) n -> p c n", p=128), owner=tgb, writes=[tgb])
                    P.dma("sp", gbias[:, 0:8], bin_d[O_MLI:O_MLI + 8].partition_broadcast(128), owner=gb, writes=[gb])
                    P.dma("sp", gbias[:, 8:24], bin_d[O_FXF:O_FXF + 16].partition_broadcast(128), owner=gb, writes=[gb])
                    bk = 0
                    for i in range(NTI):
                        for dc in range(NDC):
                            last = (i == NTI - 1 and dc == NDC - 1)
                            P.op("pe", lambda e, i=i, dc=dc: e.matmul(
                                banks[0][:, i * 24:(i + 1) * 24], lhsT=HT[:, dc, i * 128:(i + 1) * 128], rhs=gsl[:, dc, :],
                                start=(i == 0 and dc == 0), stop=(dc == NDC - 1), skip_group_check=True),
                                reads=[tgb, hTb[i // 4]], writes=[bankb[0]], signal=last)
                    P.op("dve", lambda e: e.tensor_tensor(
                        out=gpre.rearrange("p j i -> p i j"), in0=banks[0][:, 0:NTI * 24].rearrange("p (i j) -> p i j", j=24),
                        in1=bass.AP(GF, GB_O, [[GFW, 128], [0, NTI], [1, 24]]), op=ALU.add), reads=[bankb[0], gb], writes=[gb])
                    xg = gpre[:, 4:24, :]
                    DV = lambda fn: P.op("dve", fn, reads=[gb], writes=[gb])
                    AC = lambda fn: P.op("act", fn, reads=[gb], writes=[gb])
                    DV(lambda e: e.tensor_scalar(out=tmpa[:], in0=xg, scalar1=-1.0, scalar2=None, op0=ALU.mult))
                    DV(lambda e: e.tensor_tensor(out=tmpa[:], in0=tmpa[:], in1=xg, op=ALU.max))
                    AC(lambda e: e.activation(out=tmpa[:], in_=tmpa[:], func=AF.Exp, scale=-1.0))
                    AC(lambda e: e.activation(out=tmpa[:], in_=tmpa[:], func=AF.Ln, bias=1.0))
                    DV(lambda e: e.tensor_scalar(out=tmpb[:], in0=xg, scalar1=0.0, scalar2=None, op0=ALU.min))
                    DV(lambda e: e.tensor_tensor(out=lf[:], in0=tmpb[:], in1=tmpa[:], op=ALU.subtract))
                    lfa = GF[:, 384:448]
                    lfb = GF[:, 448:704]
                    li = GF[:, 0:64]
                    for k, m in enumerate((tri64, ones64, oneslo, oneshi)):
                        P.op("pe", lambda e, k=k, m=m: e.matmul(banks[1][:, k * 64:(k + 1) * 64], lhsT=m[:], rhs=lfa,
                                                                 start=(k == 0), stop=True, skip_group_check=True),
                             reads=[gb], writes=[bankb[1]], signal=(k == 3))
                    P.op("pe", lambda e: e.matmul(banks[2][:, 0:256], lhsT=tri128[:], rhs=lfb, start=True, stop=True, skip_group_check=True),
                         reads=[gb], writes=[bankb[2]], signal=False)
                    P.op("pe", lambda e: e.matmul(banks[2][:, 256:512], lhsT=allones[:], rhs=lfb, start=False, stop=True, skip_group_check=True),
                         reads=[gb], writes=[bankb[2]], signal=True)
                    A1 = lambda fn: P.op("act", fn, reads=[gb, bankb[1]], writes=[gb])
                    D1 = lambda fn: P.op("dve", fn, reads=[gb, bankb[1]], writes=[gb])
                    A1(lambda e: e.copy(out=bsb[:], in_=banks[1][:, 0:64]))
                    D1(lambda e: e.tensor_tensor(out=wk[:], in0=li, in1=bsb[:], op=ALU.subtract))
                    D1(lambda e: e.tensor_tensor(out=wv[:], in0=banks[1][:, 64:128], in1=bsb[:], op=ALU.subtract))
                    D1(lambda e: e.tensor_tensor(out=wv[:], in0=wv[:], in1=li, op=ALU.add))
                    A1(lambda e: e.activation(out=wk[:], in_=wk[:], func=AF.Exp, bias=-LN16))
                    A1(lambda e: e.activation(out=wv[:], in_=wv[:], func=AF.Exp, bias=-LN16))
                    A1(lambda e: e.activation(out=eb[:], in_=bsb[:], func=AF.Exp))
                    A1(lambda e: e.activation(out=ieb[:], in_=bsb[:], func=AF.Exp, scale=-1.0))
                    A1(lambda e: e.activation(out=ebL[:], in_=banks[1][:, 128:256], func=AF.Exp))
                    A2 = lambda fn: P.op("act", fn, reads=[gb, bankb[2]], writes=[gb])
                    D2 = lambda fn: P.op("dve", fn, reads=[gb, bankb[2]], writes=[gb])
                    A2(lambda e: e.copy(out=GF[:, 1728:1984], in_=banks[2][:, 256:512]))
                    D2(lambda e: e.memset(off[:, :, 0:1], 0.0))
                    for i in range(1, NTI):
                        D2(lambda e, i=i: e.tensor_tensor(out=off[:, :, i:i + 1], in0=off[:, :, i - 1:i], in1=tot[:, :, i - 1:i], op=ALU.add))
                    D2(lambda e: e.tensor_tensor(out=GF[:, 2240:2496], in0=banks[2][:, 0:256], in1=GF[:, 1984:2240], op=ALU.add))


                if not do_fox:
                    gates_block()

                if not do_fox:
                    for i in range(NTI):
                        P.op("dve", lambda e, i=i: e.memset(Y[:, i, :], 0.0), writes=[Yb[i]])
                with ExitStack() as pf:
                    qh = [sbt(pf, f"qh{j}", [128, T], BF16) for j in range(2)]
                    kh = [sbt(pf, f"kh{j}", [128, T], BF16) for j in range(2)]
                    qkb = [P.buf("qk2a"), P.buf("qk2b")]
                    VAf = sbt(pf, "VAf", [128, NTI, 2, 65], BF16)
                    VAb = P.buf("VA")
                    sgbt = sbt(pf, "sgb", [128, NTI, 128], BF16)
                    sgbb = P.buf("sgb")
                    BRf = sbt(pf, "BRf", [1, 512], BF16)
                    brb = P.buf("brow")
                    ones512 = sbt(pf, "ones512", [1, 512], BF16)
                    pTt = sbt(pf, "pTft", [128, 4, 512], BF16)
                    pTf = [pTt[:, j, :] for j in range(4)]
                    pTb = [P.buf("pT") for _ in range(4)]
                    rect = sbt(pf, "rect", [128, 2, 4], F32)
                    recb = [P.buf("rec") for _ in range(2)]
                    RW = sbt(pf, "RW", [128, 6, 256], F32)
                    PCS = sbt(pf, "PCS", [128, 6, 256], BF16)
                    rwb = P.buf("rw")
                    P.op("dve", lambda e: e.memset(VAf[:, :, :, 64:65], 1.0), writes=[VAb])
                    P.op("dve", lambda e: e.memset(ones512[:], 1.0), writes=[brb])
                    if do_fox:
                        for j in range(2):
                            P.op("dve", lambda e, j=j: e.memset(qh[j][64:70, :], 1.0), writes=[qkb[j]])
                            P.op("dve", lambda e, j=j: e.memset(kh[j][64:70, :], 1.0), writes=[qkb[j]])
                    def fox_rows():
                        for (k, src) in ((0, cc_), (1, off)):
                            for par in range(2):
                                P.op("dve", lambda e, k=k, src=src, par=par: e.tensor_copy(
                                    out=RW[:, 2 + k, par * 128:(par + 1) * 128].rearrange("p (h g) -> p h g", g=8),
                                    in_=src[:, :, par::2]), reads=[gb], writes=[rwb])
                            for par in range(2):
                                P.op("pe", lambda e, k=k, par=par: e.transpose(
                                    out=banks[6][:, k * 256 + par * 128:k * 256 + (par + 1) * 128],
                                    in_=RW[:, 2 + k, par * 128:(par + 1) * 128], identity=identf),
                                    reads=[rwb], writes=[bankb[6]], signal=(par == 1))
                            P.op("act", lambda e, k=k: e.copy(out=RW[:, k, :], in_=banks[6][:, k * 256:(k + 1) * 256]),
                                 reads=[bankb[6]], writes=[rwb])
                        DR = lambda fn: P.op("dve", fn, reads=[rwb], writes=[rwb])
                        for (k, pc0, mul) in ((1, 0, 8.0), (0, 3, -8.0)):
                            x8 = RW[:, 2, :]
                            r1 = RW[:, 3, :]
                            DR(lambda e, k=k, mul=mul, x8=x8: e.tensor_scalar(out=x8, in0=RW[:, k, :], scalar1=mul, scalar2=None, op0=ALU.mult))
                            DR(lambda e, pc0=pc0, x8=x8: e.tensor_copy(out=PCS[:, pc0, :], in_=x8))
                            DR(lambda e, pc0=pc0, x8=x8, r1=r1: e.tensor_tensor(out=r1, in0=x8, in1=PCS[:, pc0, :], op=ALU.subtract))
                            DR(lambda e, pc0=pc0, r1=r1: e.tensor_copy(out=PCS[:, pc0 + 1, :], in_=r1))
                            DR(lambda e, pc0=pc0, r1=r1: e.tensor_tensor(out=r1, in0=r1, in1=PCS[:, pc0 + 1, :], op=ALU.subtract))
                            DR(lambda e, pc0=pc0, r1=r1: e.tensor_copy(out=PCS[:, pc0 + 2, :], in_=r1))
                        for j in range(6):
                            P.dma("sp", scr_d[j].rearrange("h (g n) -> (h g) n", n=256), PCS[:, j, :], owner=rwb, reads=[rwb], writes=[scrb])
                    scnt = 0
                    for p in (range(8) if do_fox else ()):
                        tq, tqb = ring_next()
                        wq = load_slab(win_d, 0, NDC, O_FXQ + p * 128, 128, tq, tqb, 0)
                        wkk = load_slab(win_d, 0, NDC, O_FXK + p * 128, 128, tq, tqb, NDC * 128)
                        tv, tvb = ring_next()
                        wvg = tv[:, 0:NDC * 256].rearrange("p (c n) -> p c n", n=256)
                        P.dma("pool", wvg[:, :, 0:128], win_d[:, O_FXV + p * 128:O_FXV + (p + 1) * 128].rearrange("(c p) n -> p c n", p=128),
                              owner=tvb, writes=[tvb])
                        P.dma("pool", wvg[:, :, 128:256], win_d[:, O_GB + p * 128:O_GB + (p + 1) * 128].rearrange("(c p) n -> p c n", p=128),
                              owner=tvb, writes=[tvb])
                        for k, o in enumerate((O_FXV, O_GB, O_FXQ, O_FXK)):
                            P.dma("pool", BRf[:, k * 128:(k + 1) * 128], bin_d[o + p * 128:o + (p + 1) * 128].rearrange("(o n) -> o n", o=1),
                                  owner=brb, writes=[brb])
                        for (ws, dst, bseg) in ((wq, qh, 2), (wkk, kh, 3)):
                            for g in range(NG):
                                bk = 6 + (scnt % 2)
                                scnt += 1
                                for dc in range(NDC):
                                    P.op("pe", lambda e, ws=ws, dc=dc, g=g, bk=bk: e.matmul(
                                        banks[bk][:, :], lhsT=ws[:, dc, :], rhs=HT[:, dc, g * 512:(g + 1) * 512],
                                        start=(dc == 0), stop=False),
                                        reads=[tqb, hTb[g]], writes=[bankb[bk]], signal=False)
                                P.op("pe", lambda e, bk=bk, bseg=bseg: e.matmul(
                                    banks[bk][:, :], lhsT=BRf[:, bseg * 128:(bseg + 1) * 128], rhs=ones512[:], start=False, stop=True),
                                    reads=[brb], writes=[bankb[bk]], signal=True)
                                P.op("act", lambda e, dst=dst, g=g, bk=bk: e.copy(out=dst[0][0:64, g * 512:(g + 1) * 512], in_=banks[bk][0:64, :]),
                                     reads=[bankb[bk]], writes=[qkb[0]])
                                P.op("dve", lambda e, dst=dst, g=g, bk=bk: e.tensor_copy(out=dst[1][0:64, g * 512:(g + 1) * 512], in_=banks[bk][64:128, :]),
                                     reads=[bankb[bk]], writes=[qkb[1]])
                        for i in range(NTI):
                            bk = 6 + (scnt % 2)
                            scnt += 1
                            for dc in range(NDC):
                                P.op("pe", lambda e, wvg=wvg, dc=dc, i=i, bk=bk: e.matmul(
                                    banks[bk][:, 0:256], lhsT=HT[:, dc, i * 128:(i + 1) * 128], rhs=wvg[:, dc, :],
                                    start=(dc == 0), stop=False, skip_group_check=True),
                                    reads=[tvb, hTb[i // 4]], writes=[bankb[bk]], signal=False)
                            P.op("pe", lambda e, bk=bk: e.matmul(banks[bk][:, 0:256], lhsT=onesb[0:1, :], rhs=BRf[:, 0:256], start=False, stop=True,
                                                                 skip_group_check=True), reads=[brb], writes=[bankb[bk]], signal=True)
                            P.op("act", lambda e, i=i, bk=bk: e.copy(out=VAf[:, i, :, 0:64], in_=banks[bk][:, 0:128].rearrange("p (h d) -> p h d", d=64)),
                                 reads=[bankb[bk]], writes=[VAb])
                            P.op("act", lambda e, i=i, bk=bk: e.activation(out=sgbt[:, i, :], in_=banks[bk][:, 128:256], func=AF.Sigmoid),
                                 reads=[bankb[bk]], writes=[sgbb])
                        if p == 0:
                            gates_block()
                            fox_rows()
                        for hl in range(2):
                            h = 2 * p + hl
                            P.dma("sp", qh[hl][67:70, :], scr_d[0:3, h, :], owner=qkb[hl], reads=[scrb], writes=[qkb[hl]])
                            P.dma("sp", kh[hl][64:67, :], scr_d[3:6, h, :], owner=qkb[hl], reads=[scrb], writes=[qkb[hl]])
                        items = [(hl, tg, kb) for hl in range(2) for tg in range(NG) for kb in range(4 * tg + 4)]
                        LOOK = 3

                        def emit_s(n):
                            hl, tg, kb = items[n]
                            sb_ = n % 4
                            t0 = max(kb * 128, tg * 512)
                            t1 = (tg + 1) * 512
                            N = t1 - t0
                            diag = (kb >= 4 * tg)
                            P.op("pe", lambda e, hl=hl, kb=kb, t0=t0, t1=t1, N=N, sb_=sb_, diag=diag: e.matmul(
                                banks[sb_][:, 0:N], lhsT=kh[hl][0:70, kb * 128:(kb + 1) * 128], rhs=qh[hl][0:70, t0:t1],
                                start=True, stop=(not diag)), reads=[qkb[hl]], writes=[bankb[sb_]], signal=(not diag))
                            if diag:
                                P.op("pe", lambda e, sb_=sb_: e.matmul(banks[sb_][:, 0:128], lhsT=identb, rhs=maskneg,
                                                                      start=False, stop=True), writes=[bankb[sb_]], signal=True)
                            P.op("act", lambda e, sb_=sb_, N=N: e.activation(out=pTf[sb_][:, 0:N], in_=banks[sb_][:, 0:N], func=AF.Exp, scale=0.125),
                                 reads=[bankb[sb_]], writes=[pTb[sb_]])

                        def emit_pv(n):
                            hl, tg, kb = items[n]
                            h = 2 * p + hl
                            sb_ = n % 4
                            ab = 4 + (tg % 2)
                            tb0 = max(kb, 4 * tg)
                            for tb in range(tb0, 4 * tg + 4):
                                j = tb - 4 * tg
                                c0 = (tb - tb0) * 128
                                first = (kb == 0 and j == 0)
                                last = (kb == 4 * tg + 3)
                                P.op("pe", lambda e, sb_=sb_, kb=kb, hl=hl, ab=ab, j=j, c0=c0, first=first: e.matmul(
                                    banks[ab][:, j * 65:(j + 1) * 65], lhsT=pTf[sb_][:, c0:c0 + 128], rhs=VAf[:, kb, hl, :],
                                    start=first, stop=True, skip_group_check=True),
                                    reads=[pTb[sb_], VAb], writes=[bankb[ab]], signal=(tb == 4 * tg + 3))
                            if kb == 4 * tg + 3:
                                rj = tg % 2
                                P.op("dve", lambda e, rj=rj, ab=ab: e.reciprocal(
                                    out=rect[:, rj, :], in_=banks[ab][:, 0:260].rearrange("p (j c) -> p j c", c=65)[:, :, 64]),
                                    reads=[bankb[ab]], writes=[recb[rj]])
                                for j in range(4):
                                    tb = 4 * tg + j
                                    P.op("dve", lambda e, rj=rj, ab=ab, tb=tb, j=j, hl=hl, h=h: e.scalar_tensor_tensor(
                                        out=Y[:, tb, h * 64:(h + 1) * 64], in0=banks[ab][:, j * 65:j * 65 + 64], scalar=rect[:, rj, j:j + 1],
                                        in1=sgbt[:, tb, hl * 64:(hl + 1) * 64], op0=ALU.mult, op1=ALU.mult),
                                        reads=[bankb[ab], recb[rj], sgbb], writes=[Yb[tb]])

                        for n in range(min(LOOK, len(items))):
                            emit_s(n)
                        for n in range(len(items)):
                            if n + LOOK < len(items):
                                emit_s(n + LOOK)
                            emit_pv(n)
                    P.barrier()


                with ExitStack() as pm:
                    qT = sbt(pm, "qT", [128, 2, T], BF16)
                    kT = sbt(pm, "kT", [128, 2, T], BF16)
                    qkb = P.buf("qk")
                    VA = sbt(pm, "VAm", [128, NTI, 258], BF16)
                    VAb = P.buf("VAm")
                    GAt = sbt(pm, "GAt", [128, 3, 256], BF16)
                    GA = [GAt[:, j, :] for j in range(3)]
                    GAb = [P.buf("GA") for _ in range(3)]
                    gn = sbt(pm, "gn", [128, 256], F32)
                    gnb = P.buf("gn")
                    BR = sbt(pm, "BRm", [1, 768], BF16)
                    brb = P.buf("browm")
                    stgt = sbt(pm, "stgt", [128, 2, 516], F32)
                    stg = [stgt[:, j, :] for j in range(2)]
                    stgb = [P.buf("stg") for _ in range(2)]
                    cacct = sbt(pm, "cacc", [128, 2, 512], F32)
                    cacc = [cacct[:, j, :] for j in range(2)]
                    caccb = [P.buf("cacc") for _ in range(2)]
                    sig = sbt(pm, "sig", [128, 512], F32)
                    sigb = P.buf("sig")
                    CT = sbt(pm, "CT", [128, 2, 257], F32)
                    CTb = P.buf("CT")
                    CTht = sbt(pm, "CTht", [128, 2, 2, 258], BF16)
                    CTh = [CTht[:, j, :, 0:257] for j in range(2)]
                    CThb = [P.buf("CTh") for _ in range(2)]
                    ktokt = sbt(pm, "ktokt", [128, 2, 256], BF16)
                    ktok = [ktokt[:, j, :] for j in range(2)]
                    ktokb = [P.buf("ktok") for _ in range(2)]
                    pTt = sbt(pm, "pTmt", [128, 2, 64], BF16)
                    pT = [pTt[:, j, :] for j in range(2)]
                    pTb = [P.buf("pTm") for _ in range(2)]
                    hnnt = sbt(pm, "hnnt", [128, 2, 256], BF16)
                    hnn = [hnnt[:, j, :] for j in range(2)]
                    hnnb = [P.buf("hnn") for _ in range(2)]
                    smt_ = sbt(pm, "smt", [128, 2, 16], F32)
                    sm = [smt_[:, j, :] for j in range(2)]
                    smb = [P.buf("sm") for _ in range(2)]
                    scnt = 0
                    for hd in (range(4) if do_mlstm else ()):
                        tq, tqb = ring_next()
                        wq = load_slab(win_d, 0, NDC, O_MLQ + hd * 256, 256, tq, tqb)
                        tk, tkb = ring_next()
                        wkk = load_slab(win_d, 0, NDC, O_MLK + hd * 256, 256, tk, tkb)
                        tv, tvb = ring_next()
                        wvv = load_slab(win_d, 0, NDC, O_MLV + hd * 256, 256, tv, tvb)
                        for k, o in enumerate((O_MLV, O_MLO, O_GA)):
                            P.dma("pool", BR[:, k * 256:(k + 1) * 256], bin_d[o + hd * 256:o + (hd + 1) * 256].rearrange("(o n) -> o n", o=1),
                                  owner=brb, writes=[brb])
                        P.dma("sp", gn[:], hn_d[hd * 256:(hd + 1) * 256].partition_broadcast(128), owner=gnb, writes=[gnb])
                        pend = [None]

                        def flush_silu():
                            if pend[0] is not None:
                                dstv, cj = pend[0]
                                P.op("act", lambda e, dstv=dstv, cj=cj: e.activation(out=dstv, in_=cacc[cj], func=AF.Silu),
                                     reads=[caccb[cj]], writes=[qkb])
                                pend[0] = None

                        for (ws, wb, dstT, bcol, cofs) in ((wq, tqb, qT, 0, 0), (wkk, tkb, kT, 1, 8)):
                            for cc in range(2):
                                ch = hd * 2 + cc
                                cch = cofs + ch
                                for g in range(NG):
                                    bk = (scnt % 2)
                                    j = scnt % 2
                                    scnt += 1
                                    for dc in range(NDC):
                                        P.op("pe", lambda e, ws=ws, dc=dc, cc=cc, g=g, bk=bk: e.matmul(
                                            banks[bk][:, :], lhsT=ws[:, dc, cc * 128:(cc + 1) * 128], rhs=HT[:, dc, g * 512:(g + 1) * 512],
                                            start=(dc == 0), stop=(dc == NDC - 1)),
                                            reads=[wb, hTb[g]], writes=[bankb[bk]], signal=(dc == NDC - 1))
                                    if g == 0:
                                        P.op("dve", lambda e, j=j: e.memset(stg[j][:, 0:3], 0.0), writes=[stgb[j]])
                                    else:
                                        P.op("dve", lambda e, j=j: e.tensor_copy(out=stg[j][:, 0:3], in_=stg[1 - j][:, 512:515]),
                                             reads=[stgb[1 - j]], writes=[stgb[j]])
                                    P.op("act", lambda e, j=j, bk=bk, bcol=bcol, ch=ch: e.activation(
                                        out=stg[j][:, 3:515], in_=banks[bk][:, :], func=AF.Identity, bias=bfm[:, bcol, ch:ch + 1]),
                                        reads=[bankb[bk]], writes=[stgb[j]])
                                    flush_silu()
                                    P.op("dve", lambda e, j=j, cch=cch: e.tensor_scalar(
                                        out=cacc[j], in0=stg[j][:, 3:515], scalar1=cwc[:, 3, cch:cch + 1], scalar2=cbc[:, cch:cch + 1],
                                        op0=ALU.mult, op1=ALU.add), reads=[stgb[j]], writes=[caccb[j]])
                                    for jj in range(3):
                                        P.op("dve", lambda e, j=j, cch=cch, jj=jj: e.scalar_tensor_tensor(
                                            out=cacc[j], in0=stg[j][:, jj:jj + 512], scalar=cwc[:, jj, cch:cch + 1], in1=cacc[j],
                                            op0=ALU.mult, op1=ALU.add), reads=[stgb[j], caccb[j]], writes=[caccb[j]])
                                    pend[0] = (dstT[:, cc, g * 512:(g + 1) * 512], j)
                        flush_silu()
                        to, tob = ring_next()
                        woo = load_slab(win_d, 0, NDC, O_MLO + hd * 256, 256, to, tob)
                        ta, tab = ring_next()
                        wga = load_slab(win_d, 0, NDC, O_GA + hd * 256, 256, ta, tab)
                        P.op("dve", lambda e: e.memset(VA[:, :, 256:257], 1.0), writes=[VAb])
                        for i in range(NTI):
                            bk = (scnt % 2)
                            scnt += 1
                            for dc in range(NDC):
                                P.op("pe", lambda e, dc=dc, i=i, bk=bk, wvv=wvv: e.matmul(
                                    banks[bk][:, 0:256], lhsT=HT[:, dc, i * 128:(i + 1) * 128], rhs=wvv[:, dc, :],
                                    start=(dc == 0), stop=False, skip_group_check=True),
                                    reads=[tvb, hTb[i // 4]], writes=[bankb[bk]], signal=False)
                            P.op("pe", lambda e, bk=bk: e.matmul(banks[bk][:, 0:256], lhsT=onesb[0:1, :], rhs=BR[:, 0:256], start=False, stop=True,
                                                                 skip_group_check=True), reads=[brb], writes=[bankb[bk]], signal=True)
                            P.op("act", lambda e, i=i, bk=bk: e.copy(out=VA[:, i, 0:256], in_=banks[bk][:, 0:256]),
                                 reads=[bankb[bk]], writes=[VAb])
                        P.op("dve", lambda e: e.memset(CT[:], 0.0), writes=[CTb])
                        P.op("dve", lambda e: e.memset(CTh[0], 0.0), writes=[CThb[0]])
                        b3bf = banks[3].bitcast(BF16)
                        UB = ((6, 7), (6, 7))

                        def gates(i):
                            s3 = i % 3
                            for dc in range(NDC):
                                for (ws, wb2, c0) in ((woo, tob, 0), (wga, tab, 256)):
                                    P.op("pe", lambda e, ws=ws, dc=dc, i=i, c0=c0: e.matmul(
                                        banks[0][:, c0:c0 + 256], lhsT=HT[:, dc, i * 128:(i + 1) * 128], rhs=ws[:, dc, :],
                                        start=(dc == 0 and c0 == 0), stop=False, skip_group_check=True),
                                        reads=[wb2, hTb[i // 4]], writes=[bankb[0]], signal=False)
                            P.op("pe", lambda e: e.matmul(banks[0][:, :], lhsT=onesb[0:1, :], rhs=BR[:, 256:768], start=False, stop=True,
                                                          skip_group_check=True), reads=[brb], writes=[bankb[0]], signal=True)
                            P.op("act", lambda e: e.activation(out=sig[:], in_=banks[0][:, :], func=AF.Exp, scale=-1.0),
                                 reads=[bankb[0]], writes=[sigb])
                            P.op("act", lambda e: e.activation(out=sig[:], in_=sig[:], func=AF.Ln, bias=1.0), reads=[sigb], writes=[sigb])
                            P.op("dve", lambda e: e.tensor_tensor(out=sig[:, 0:256], in0=sig[:, 0:256], in1=sig[:, 256:512], op=ALU.add),
                                 reads=[sigb], writes=[sigb])

                        def gatesB(i):
                            s3 = i % 3
                            P.op("act", lambda e: e.activation(out=sig[:, 256:512], in_=sig[:, 0:256], func=AF.Exp, scale=-1.0),
                                 reads=[sigb], writes=[sigb])
                            P.op("dve", lambda e, s3=s3: e.tensor_tensor(out=GA[s3], in0=sig[:, 256:512], in1=gn[:], op=ALU.mult),
                                 reads=[sigb, gnb], writes=[GAb[s3]])

                        def pre(c):
                            i, hf = c // 2, c % 2
                            rows = slice(hf * 64, hf * 64 + 64)
                            tok = slice(c * 64, (c + 1) * 64)
                            s2 = c % 2
                            gcol = hd * NTI + i
                            for cc in range(2):
                                P.op("pe", lambda e, rows=rows, cc=cc, tok=tok: e.transpose(
                                    out=b3bf[rows, cc * 128:(cc + 1) * 128], in_=kT[:, cc, tok], identity=identb),
                                    reads=[qkb], writes=[bankb[3]], signal=(cc == 1))
                            P.op("act", lambda e, rows=rows, s2=s2, gcol=gcol: e.activation(
                                out=ktok[s2][rows, :], in_=b3bf[rows, 0:256], func=AF.Copy, scale=wv[rows, gcol:gcol + 1]),
                                reads=[bankb[3], gb], writes=[ktokb[s2]])
                            for cc in range(2):
                                P.op("pe", lambda e, rows=rows, cc=cc, tok=tok: e.matmul(
                                    banks[2][rows, 0:64], lhsT=kT[:, cc, tok], rhs=qT[:, cc, tok], start=(cc == 0), stop=(cc == 1)),
                                    reads=[qkb], writes=[bankb[2]], signal=(cc == 1))
                            P.op("dve", lambda e, rows=rows, s2=s2, gcol=gcol: e.scalar_tensor_tensor(
                                out=pT[s2][rows, :], in0=banks[2][rows, 0:64], scalar=wk[rows, gcol:gcol + 1], in1=mask2[rows, :],
                                op0=ALU.mult, op1=ALU.mult), reads=[bankb[2], gb], writes=[pTb[s2]])

                        def main(c):
                            i, hf = c // 2, c % 2
                            rows = slice(hf * 64, hf * 64 + 64)
                            tok = slice(c * 64, (c + 1) * 64)
                            s2 = c % 2
                            s3 = i % 2
                            gcol = hd * NTI + i
                            ob = 4 + (i % 2)
                            cur = c % 2
                            ub = UB[c % 2]
                            for cc in range(2):
                                P.op("pe", lambda e, rows=rows, cc=cc, s2=s2, ub=ub, i=i: e.matmul(
                                    banks[ub[cc]][:, 0:257], lhsT=ktok[s2][rows, cc * 128:(cc + 1) * 128], rhs=VA[rows, i, 0:257],
                                    start=True, stop=True), reads=[ktokb[s2], VAb], writes=[bankb[ub[cc]]], signal=True)
                            P.op("pe", lambda e, rows=rows, s2=s2, i=i, ob=ob: e.matmul(
                                banks[ob][rows, 0:257], lhsT=pT[s2][rows, :], rhs=VA[rows, i, 0:257], start=True, stop=False),
                                reads=[pTb[s2], VAb], writes=[bankb[ob]], signal=False)
                            for cc in range(2):
                                P.op("pe", lambda e, rows=rows, cc=cc, tok=tok, ob=ob, cur=cur: e.matmul(
                                    banks[ob][rows, 0:257], lhsT=qT[:, cc, tok], rhs=CTh[cur][:, cc, :], start=False, stop=(cc == 1)),
                                    reads=[qkb, CThb[cur]], writes=[bankb[ob]], signal=(cc == 1))
                            ecol = hf * 64 + gcol
                            P.op("dve", lambda e, ecol=ecol: e.scalar_tensor_tensor(
                                out=CT[:], in0=CT[:], scalar=ebL[:, ecol:ecol + 1],
                                in1=bank67[:, :].rearrange("p (c n) -> p c n", n=512)[:, :, 0:257],
                                op0=ALU.mult, op1=ALU.add), reads=[CTb, bankb[6], bankb[7], gb], writes=[CTb])
                            P.op("act", lambda e, cur=cur: e.copy(out=CTh[1 - cur], in_=CT[:]), reads=[CTb], writes=[CThb[1 - cur]])

                        def postA(i):
                            s3 = i % 2
                            ob = 4 + (i % 2)
                            gcol = hd * NTI + i
                            smt = sm[s3]
                            SD = lambda fn: P.op("dve", fn, reads=[smb[s3], bankb[ob], gb], writes=[smb[s3]])
                            SD(lambda e: e.tensor_scalar(out=smt[:, 0:1], in0=banks[ob][:, 256:257], scalar1=-1.0, scalar2=None, op0=ALU.mult))
                            SD(lambda e: e.tensor_tensor(out=smt[:, 0:1], in0=smt[:, 0:1], in1=banks[ob][:, 256:257], op=ALU.max))
                            SD(lambda e: e.tensor_tensor(out=smt[:, 1:2], in0=smt[:, 0:1], in1=ieb[:, gcol:gcol + 1], op=ALU.max))
                            SD(lambda e: e.reciprocal(out=smt[:, 2:3], in_=smt[:, 1:2]))
                            SD(lambda e: e.bn_stats(smt[:, 4:10], banks[ob][:, 0:256]))
                            SD(lambda e: e.bn_aggr(smt[:, 10:12], smt[:, 4:10]))
                            SD(lambda e: e.scalar_tensor_tensor(out=smt[:, 3:4], in0=smt[:, 2:3], scalar=smt[:, 2:3], in1=smt[:, 11:12],
                                                                op0=ALU.mult, op1=ALU.mult))

                        def postB(i):
                            s3 = i % 2
                            smt = sm[s3]
                            SA = lambda fn: P.op("act", fn, reads=[smb[s3]], writes=[smb[s3]])
                            SA(lambda e: e.activation(out=smt[:, 12:13], in_=smt[:, 3:4], func=AF.Ln, bias=epsc))
                            SA(lambda e: e.activation(out=smt[:, 13:14], in_=smt[:, 12:13], func=AF.Exp, scale=-0.5))

                        def postC(i):
                            s3 = i % 2
                            ob = 4 + (i % 2)
                            smt = sm[s3]
                            SD = lambda fn: P.op("dve", fn, reads=[smb[s3]], writes=[smb[s3]])
                            SD(lambda e: e.tensor_tensor(out=smt[:, 14:15], in0=smt[:, 2:3], in1=smt[:, 13:14], op=ALU.mult))
                            SD(lambda e: e.scalar_tensor_tensor(out=smt[:, 15:16], in0=smt[:, 10:11], scalar=-1.0, in1=smt[:, 14:15],
                                                                op0=ALU.mult, op1=ALU.mult))
                            P.op("act", lambda e: e.activation(out=hnn[s3], in_=banks[ob][:, 0:256], func=AF.Identity,
                                                               scale=smt[:, 14:15], bias=smt[:, 15:16]),
                                 reads=[bankb[ob], smb[s3]], writes=[hnnb[s3]])

                        def postE(i):
                            s3 = i % 2
                            P.op("dve", lambda e: e.tensor_tensor(out=hnn[s3], in0=hnn[s3], in1=GA[i % 3], op=ALU.mult),
                                 reads=[hnnb[s3], GAb[i % 3]], writes=[hnnb[s3]])
                            ydst = Y[:, i, hd * 256:(hd + 1) * 256]
                            P.op("dve", lambda e: e.tensor_tensor(out=ydst, in0=ydst, in1=hnn[s3], op=ALU.add),
                                 reads=[hnnb[s3], Yb[i]], writes=[Yb[i]])

                        NCH = T // 64
                        gates(0)
                        gatesB(0)
                        pre(0)
                        for c in range(NCH):
                            if c + 1 < NCH:
                                pre(c + 1)
                            if c % 2 == 0 and c >= 2:
                                postB(c // 2 - 1)
                            main(c)
                            if c % 2 == 1:
                                postA(c // 2)
                                if c >= 3:
                                    postE((c - 3) // 2)
                                if c // 2 + 1 < NTI:
                                    gatesB(c // 2 + 1)
                            else:
                                if c >= 2:
                                    postC(c // 2 - 1)
                                if c // 2 + 1 < NTI:
                                    gates(c // 2 + 1)
                        postB(NTI - 1)
                        postC(NTI - 1)
                        postE(NTI - 1)
                    P.barrier()

                for i in range(NTI):
                    for h in range(2):
                        bk = bank_rr[0] % 8
                        bank_rr[0] += 1
                        bbf = banks[bk].bitcast(BF16)
                        for k in range(4):
                            dc = 4 * h + k
                            P.op("pe", lambda e, dc=dc, k=k, i=i, bbf=bbf: e.transpose(
                                out=bbf[:, k * 128:(k + 1) * 128], in_=Y[:, i, dc * 128:(dc + 1) * 128], identity=identb[:]),
                                reads=[Yb[i]], writes=[bankb[bk]], signal=(k == 3))
                        src = bbf[:, 0:512].rearrange("p (c n) -> p c n", n=128)
                        dst = HT[:, 4 * h:4 * h + 4, i * 128:(i + 1) * 128]
                        if (2 * i + h) % 2 == 0:
                            P.op("act", lambda e, src=src, dst=dst: e.copy(out=dst, in_=src), reads=[bankb[bk]], writes=[hTb[i // 4]])
                        else:
                            P.op("dve", lambda e, src=src, dst=dst: e.tensor_copy(out=dst, in_=src), reads=[bankb[bk]], writes=[hTb[i // 4]])
                for nn in range(4):
                    tw, twb = ring_next()
                    wo = load_slab(wo_d, 0, NDC, nn * 256, 256, tw, twb)
                    for hh in range(2):
                        ncx = 2 * nn + hh
                        bs = 4 * (ncx % 2)
                        for dc in range(NDC):
                            for g in range(NG):
                                P.op("pe", lambda e, dc=dc, g=g, hh=hh, bs=bs, wo=wo: e.matmul(
                                    banks[bs + g][:, :], lhsT=wo[:, dc, hh * 128:(hh + 1) * 128], rhs=HT[:, dc, g * 512:(g + 1) * 512],
                                    start=(dc == 0), stop=(dc == NDC - 1)),
                                    reads=[twb, hTb[g]], writes=[bankb[bs + g]], signal=(dc == NDC - 1))
                        for g in range(NG):
                            P.op("dve", lambda e, ncx=ncx, g=g, bs=bs: e.tensor_tensor(
                                out=XT[:, ncx, g * 512:(g + 1) * 512], in0=banks[bs + g][:, :], in1=XT[:, ncx, g * 512:(g + 1) * 512], op=ALU.add),
                                reads=[bankb[bs + g], XTb[g]], writes=[XTb[g]])
                P.barrier()
            return None

        setup()
        phase_load()
        if stop_after not in ("load",):
            phase_ffn(0, wg1_d, wu1_d, wd1_d)
        if stop_after not in ("load", "ffn1"):
            phase_mixer()
        if stop_after not in ("load", "ffn1", "mixer"):
            phase_ffn(2, wg2_d, wu2_d, wd2_d)
        phase_final()
        P.replay()
    return nc


_NC_CACHE = {}
_NAMES = ("ffn1_norm", "ffn1_w_gate", "ffn1_w_up", "ffn1_w_down", "mix_norm", "w_in", "b_in", "conv_w", "conv_b",
          "ml_head_norm", "w_out", "ffn2_norm", "ffn2_w_gate", "ffn2_w_up", "ffn2_w_down")


def kernel(**inputs):
    stop_after = inputs.pop("_stop_after", "all")
    do_fox = inputs.pop("_do_fox", True)
    do_mlstm = inputs.pop("_do_mlstm", True)
    x = np.ascontiguousarray(np.asarray(inputs["x"], dtype=np.float32))
    shared = {}
    for n in _NAMES:
        a = np.asarray(inputs[n], dtype=np.float32)
        shared[n] = np.ascontiguousarray(a.reshape(a.shape[1:]))
    shared["final_norm"] = np.ascontiguousarray(np.asarray(inputs["final_norm"], dtype=np.float32))
    key = (stop_after, do_fox, do_mlstm)
    if key not in _NC_CACHE:
        _NC_CACHE[key] = build(stop_after, do_fox, do_mlstm)
    nc = _NC_CACHE[key]
    in_maps = []
    for b in range(8):
        m = {"x": np.ascontiguousarray(x[b])}
        m.update(shared)
        in_maps.append(m)
    res = run_bass_kernel_spmd(nc, in_maps, core_ids=list(range(8)))
    return np.stack([np.asarray(r["out"], dtype=np.float32) for r in res.results], axis=0)
```

```python
import numpy as np
from contextlib import ExitStack
import concourse.bass as bass
import concourse.mybir as mybir
from concourse.bass_utils import run_bass_kernel_spmd

F32 = mybir.dt.float32
BF16 = mybir.dt.bfloat16
AF = mybir.ActivationFunctionType
ALU = mybir.AluOpType

D = 1024
T = 2048
DFF = 2816
NIN = 9240
NDC = 8
NFC = 22
NG = 4
NTI = 16
EPS = 1e-6
O_MLQ, O_MLK, O_MLV, O_MLO, O_MLI, O_MLF = 0, 1024, 2048, 3072, 4096, 4100
O_FXQ, O_FXK, O_FXV, O_FXF, O_GA, O_GB = 4104, 5128, 6152, 7176, 7192, 8216
LN16 = 2.772588722239781


class Buf:
    __slots__ = ("name", "w", "r", "dsem", "dcnt")

    def __init__(self, name):
        self.name = name
        self.w = None
        self.r = {}
        self.dsem = None
        self.dcnt = 0


class Prog:
    ENG = ("pe", "act", "dve", "pool", "sp")

    def __init__(self, nc, stack):
        self.nc = nc
        self.stack = stack
        self.ops = {e: [] for e in self.ENG}
        self.tick = {e: 0 for e in self.ENG}
        self.waited = {e: {} for e in self.ENG}
        self.sems = {}
        self.nbuf = 0

    def buf(self, name):
        self.nbuf += 1
        return Buf(f"{name}_{self.nbuf}")

    def sem(self, name):
        if name not in self.sems:
            self.sems[name] = self.stack.enter_context(self.nc.semaphore(name))
        return self.sems[name]

    def _waits(self, eng, reads, writes, extra):
        ws = list(extra)
        for b in reads:
            if b.w is not None:
                ws.append(b.w)
        for b in writes:
            if b.w is not None:
                ws.append(b.w)
            ws.extend(b.r.items())
        out = []
        for (s, v) in ws:
            if eng == "pe" and s == "pe":
                continue
            if self.waited[eng].get(s, 0) >= v:
                continue
            self.waited[eng][s] = v
            out.append((s, v))
        return out

    def _update(self, tick, reads, writes):
        for b in reads:
            b.r[tick[0]] = max(b.r.get(tick[0], 0), tick[1])
        for b in writes:
            b.w = tick
            b.r = {}

    def op(self, eng, fn, reads=(), writes=(), signal=True, waits=()):
        w = self._waits(eng, reads, writes, waits)
        if signal:
            self.tick[eng] += 1
            tick = (eng, self.tick[eng])
            self.ops[eng].append((w, fn, ("inc", eng)))
        else:
            assert eng == "pe"
            tick = (eng, self.tick[eng] + 1)
            self.ops[eng].append((w, fn, None))
        self._update(tick, reads, writes)
        return tick

    def dma(self, eng, out, in_, owner, reads=(), writes=(), **kw):
        w = self._waits(eng, reads, writes, ())
        if owner.dsem is None:
            owner.dsem = "d_" + owner.name
            self.sem(owner.dsem)
        owner.dcnt += 1
        tick = (owner.dsem, 16 * owner.dcnt)
        self.ops[eng].append((w, (lambda e: e.dma_start(out=out, in_=in_, **kw)), ("dma", owner.dsem)))
        self._update(tick, reads, writes)
        return tick

    def wait_only(self, eng, ticks):
        w = self._waits(eng, (), (), ticks)
        if w:
            self.ops[eng].append((w, None, None))

    def barrier(self, engines=("pe", "act", "dve", "sp", "pool")):
        ticks = [(e, self.tick[e]) for e in ("pe", "act", "dve", "pool") if self.tick[e] > 0]
        for e in engines:
            self.wait_only(e, [t for t in ticks if t[0] != e])

    def replay(self):
        nc = self.nc
        for e in self.ENG:
            self.sem(e)
        sems = self.sems
        ops = self.ops

        def run(engname, engobj):
            for (w, fn, sig) in ops[engname]:
                for (s, v) in w:
                    engobj.wait_ge(sems[s], v)
                if fn is None:
                    continue
                inst = fn(engobj)
                if sig is not None:
                    inst.then_inc(sems[sig[1]], 16 if sig[0] == "dma" else 1)

        with nc.Block() as block:
            @block.tensor
            def _(e):
                run("pe", e)

            @block.scalar
            def _(e):
                run("act", e)

            @block.vector
            def _(e):
                run("dve", e)

            @block.gpsimd
            def _(e):
                run("pool", e)

            @block.sync
            def _(e):
                run("sp", e)


def build(stop_after="all", do_fox=True, do_mlstm=True):
    nc = bass.Bass("TRN2", target_bir_lowering=False)
    dt_in = lambda name, shape: nc.dram_tensor(name, shape, F32, kind="ExternalInput").ap()
    x_d = dt_in("x", [T, D])
    g1_d = dt_in("ffn1_norm", [D])
    wg1_d = dt_in("ffn1_w_gate", [D, DFF])
    wu1_d = dt_in("ffn1_w_up", [D, DFF])
    wd1_d = dt_in("ffn1_w_down", [DFF, D])
    gm_d = dt_in("mix_norm", [D])
    win_d = dt_in("w_in", [D, NIN])
    bin_d = dt_in("b_in", [NIN])
    cw_d = dt_in("conv_w", [4, 2 * D])
    cb_d = dt_in("conv_b", [2 * D])
    hn_d = dt_in("ml_head_norm", [D])
    wo_d = dt_in("w_out", [D, D])
    g2_d = dt_in("ffn2_norm", [D])
    wg2_d = dt_in("ffn2_w_gate", [D, DFF])
    wu2_d = dt_in("ffn2_w_up", [D, DFF])
    wd2_d = dt_in("ffn2_w_down", [DFF, D])
    gf_d = dt_in("final_norm", [D])
    out_d = nc.dram_tensor("out", [T, D], F32, kind="ExternalOutput").ap()
    scr_d = nc.dram_tensor("fox_scr", [6, 16, T], BF16, kind="Internal").ap()

    with ExitStack() as st:
        P = Prog(nc, st)
        _uid = [0]

        def sbt(stack, name, shape, dt):
            _uid[0] += 1
            return stack.enter_context(nc.sbuf_tensor(f"{name}_{_uid[0]}", shape, dt))

        XT = sbt(st, "XT", [128, NDC, T], F32)
        XTb = [P.buf(f"xt{g}") for g in range(NG)]
        CF = sbt(st, "CF", [128, 1232], F32)
        CB = sbt(st, "CB", [128, 512], BF16)
        identf = CF[:, 0:128]
        tri128 = CF[:, 128:256]
        tri64 = CF[:, 256:384]
        ones64 = CF[:, 384:512]
        oneslo = CF[:, 512:640]
        oneshi = CF[:, 640:768]
        allones = CF[:, 768:896]
        mask2 = CF[:, 896:960]
        gcols = CF[:, 960:992].rearrange("p (a c) -> p a c", c=NDC)
        bfm = CF[:, 992:1024].rearrange("p (a c) -> p a c", c=NDC)
        cwc = CF[:, 1024:1088].rearrange("p (a c) -> p a c", c=16)
        cbc = CF[:, 1088:1104]
        epsc = CF[:, 1104:1105]
        identb = CB[:, 0:128]
        onesms = CB[:, 128:256]
        onesb = CB[:, 256:384]
        maskneg = CB[:, 384:512]
        RSL = 2048
        NRING = 4
        ring_t = [sbt(st, f"ring{i}", [128, RSL], BF16) for i in range(NRING)]
        ring_b = [P.buf(f"ring{i}") for i in range(NRING)]
        ring_i = [0]
        banks = [st.enter_context(nc.psum_tensor(f"bank{i}", [128, 512], F32)) for i in range(6)]
        bank67 = st.enter_context(nc.psum_tensor("bank67", [128, 1024], F32))
        banks.append(bank67[:, 0:512])
        banks.append(bank67[:, 512:1024])
        bankb = [P.buf(f"bank{i}") for i in range(8)]
        cbuf = P.buf("consts")
        scrb = P.buf("scr")

        def ring_next():
            i = ring_i[0] % NRING
            ring_i[0] += 1
            return ring_t[i], ring_b[i]

        def load_slab(w_d, r0, nrow_chunks, c0, ncols, tens, tb, dst_off=0):
            src = w_d[r0:r0 + nrow_chunks * 128, c0:c0 + ncols].rearrange("(c p) n -> p c n", p=128)
            dst = tens[:, dst_off:dst_off + nrow_chunks * ncols].rearrange("p (c n) -> p c n", n=ncols)
            P.dma("pool", dst, src, owner=tb, writes=[tb])
            return dst

        def setup():
            P.dma("sp", gcols[:, 0, :], g1_d.rearrange("(c p) -> p c", p=128), owner=cbuf, allow_slow_non_contiguous=True)
            P.dma("sp", gcols[:, 1, :], gm_d.rearrange("(c p) -> p c", p=128), owner=cbuf, allow_slow_non_contiguous=True)
            P.dma("sp", gcols[:, 2, :], g2_d.rearrange("(c p) -> p c", p=128), owner=cbuf, allow_slow_non_contiguous=True)
            P.dma("sp", gcols[:, 3, :], gf_d.rearrange("(c p) -> p c", p=128), owner=cbuf, allow_slow_non_contiguous=True)
            for k, o in enumerate((O_MLQ, O_MLK, O_FXQ, O_FXK)):
                P.dma("sp", bfm[:, k, :], bin_d[o:o + 1024].rearrange("(c p) -> p c", p=128), owner=cbuf, allow_slow_non_contiguous=True)
            for j in range(4):
                P.dma("sp", cwc[:, j, :], cw_d[j, :].rearrange("(c p) -> p c", p=128), owner=cbuf, allow_slow_non_contiguous=True)
            P.dma("sp", cbc[:, :], cb_d.rearrange("(c p) -> p c", p=128), owner=cbuf, allow_slow_non_contiguous=True)
            cfb = P.buf("cfill")

            def V(fn):
                return P.op("dve", fn, reads=[cfb], writes=[cfb])
            t = V(lambda e: e.memset(identb[:], 0.0))
            t2 = V(lambda e: e.memset(identf[:], 0.0))
            t3 = V(lambda e: e.memset(tri128[:], 1.0))
            V(lambda e: e.memset(onesms[:], 1.0 / D))
            V(lambda e: e.memset(onesb[:], 1.0))
            V(lambda e: e.memset(allones[:], 1.0))
            V(lambda e: e.memset(epsc[:], EPS))
            V(lambda e: e.memset(ones64[:], 0.0))
            V(lambda e: e.memset(oneslo[:], 0.0))
            V(lambda e: e.memset(oneshi[:], 0.0))
            t4 = V(lambda e: e.memset(ones64[0:64, 0:64], 1.0))
            t4 = V(lambda e: e.memset(ones64[64:128, 64:128], 1.0))
            V(lambda e: e.memset(oneslo[0:64, :], 1.0))
            t5 = V(lambda e: e.memset(oneshi[64:128, :], 1.0))
            aff = lambda tens, cm, pat, cmp: (lambda e: e.affine_select(
                out=tens[:], in_=tens[:], pattern=[[pat, 128]], compare_op=cmp, fill=(1.0 if cmp == ALU.not_equal else 0.0),
                base=0, channel_multiplier=cm))
            p1 = P.op("pool", aff(identb, 1, -1, ALU.not_equal), waits=[t5])
            p2 = P.op("pool", aff(identf, 1, -1, ALU.not_equal), waits=[t5, p1])
            p3 = P.op("pool", aff(tri128, -1, 1, ALU.is_ge), waits=[t5, p2])
            v = P.op("dve", lambda e: e.tensor_copy(out=tri64[:], in_=tri128[:]), waits=[p3])
            v = P.op("dve", lambda e: e.memset(tri64[0:64, 64:128], 0.0), waits=[v])
            v = P.op("dve", lambda e: e.tensor_copy(out=mask2[0:64, :], in_=tri128[0:64, 0:64]), waits=[v])
            v = P.op("dve", lambda e: e.tensor_copy(out=mask2[64:128, :], in_=tri128[64:128, 64:128]), waits=[v])
            v = P.op("dve", lambda e: e.tensor_scalar(out=maskneg[:], in0=tri128[:], scalar1=-1.0, scalar2=30000.0,
                                                      op0=ALU.add, op1=ALU.mult), waits=[v])
            for e_ in ("pe", "act", "dve", "sp"):
                P.wait_only(e_, [(cbuf.dsem, 16 * cbuf.dcnt)])
            P.barrier()

        bank_rr = [0]

        def rms_to_hT(gi, HT, hTb, ph, groups=range(NG), out_fn=None):
            if out_fn is None:
                with ExitStack() as tmp:
                    _rms(gi, HT, hTb, tmp, groups, None)
                    P.barrier()
            else:
                _rms(gi, HT, hTb, ph, groups, out_fn)

        def _rms(gi, HT, hTb, ph, groups, out_fn):
            sqt = sbt(ph, "sqt", [128, 2, NDC, 512], BF16)
            sq = [sqt[:, j, :, :] for j in range(2)]
            sqb = [P.buf("sq") for _ in range(2)]
            rst = sbt(ph, "rst", [128, 2, 512], F32)
            rs = [rst[:, j, :] for j in range(2)]
            rsb = [P.buf("rs") for _ in range(2)]
            for g in groups:
                j = g % 2
                tok = slice(g * 512, (g + 1) * 512)
                P.op("act", lambda e, j=j, tok=tok: e.activation(out=sq[j][:], in_=XT[:, :, tok], func=AF.Square),
                     reads=[XTb[g]], writes=[sqb[j]])
                bk = bank_rr[0] % 8
                bank_rr[0] += 1
                for dc in range(NDC):
                    P.op("pe", lambda e, j=j, dc=dc, bk=bk: e.matmul(banks[bk][:, :], lhsT=onesms[:], rhs=sq[j][:, dc, :],
                                                                        start=(dc == 0), stop=(dc == NDC - 1)),
                         reads=[sqb[j]], writes=[bankb[bk]], signal=(dc == NDC - 1))
                P.op("act", lambda e, j=j, bk=bk: e.activation(out=rs[j][:], in_=banks[bk][:, :], func=AF.Sqrt, bias=epsc[:]),
                     reads=[bankb[bk]], writes=[rsb[j]])
                P.op("dve", lambda e, j=j: e.reciprocal(out=rs[j][:], in_=rs[j][:]), reads=[rsb[j]], writes=[rsb[j]])
                if out_fn is not None:
                    out_fn(g, rs[j], rsb[j])
                    continue
                for dc in range(NDC):
                    P.op("dve", lambda e, j=j, dc=dc, tok=tok: e.scalar_tensor_tensor(
                        out=HT[:, dc, tok], in0=XT[:, dc, tok], scalar=gcols[:, gi, dc:dc + 1], in1=rs[j][:],
                        op0=ALU.mult, op1=ALU.mult), reads=[XTb[g], rsb[j]], writes=[hTb[g]])

        def phase_load():
            with ExitStack() as ph:
                xs = [sbt(ph, f"xs{j}", [128, D], F32) for j in range(4)]
                xsb = [P.buf("xs") for _ in range(4)]
                for i in range(NTI):
                    j = i % 4
                    P.dma("sp", xs[j][:], x_d[i * 128:(i + 1) * 128, :], owner=xsb[j], writes=[xsb[j]])
                    for h in range(2):
                        bk = bank_rr[0] % 8
                        bank_rr[0] += 1
                        for k in range(4):
                            dc = 4 * h + k
                            P.op("pe", lambda e, j=j, dc=dc, k=k, bk=bk: e.transpose(
                                out=banks[bk][:, k * 128:(k + 1) * 128], in_=xs[j][:, dc * 128:(dc + 1) * 128], identity=identf[:]),
                                reads=[xsb[j]], writes=[bankb[bk]], signal=(k == 3))
                        eng = "act" if (2 * i + h) % 2 == 0 else "dve"
                        src = lambda bk=bk: banks[bk][:, :].rearrange("p (c n) -> p c n", n=128)
                        dst = lambda i=i, h=h: XT[:, 4 * h:4 * h + 4, i * 128:(i + 1) * 128]
                        if eng == "act":
                            P.op("act", lambda e, src=src, dst=dst: e.copy(out=dst(), in_=src()), reads=[bankb[bk]], writes=[XTb[i // 4]])
                        else:
                            P.op("dve", lambda e, src=src, dst=dst: e.tensor_copy(out=dst(), in_=src()), reads=[bankb[bk]], writes=[XTb[i // 4]])
                P.barrier()

        def phase_ffn(gi, wg_d, wu_d, wd_d):
            with ExitStack() as ph:
                HT = sbt(ph, "ffn_hT", [128, NDC, T], BF16)
                hTb = [P.buf("hT") for _ in range(NG)]
                rms_to_hT(gi, HT, hTb, ph)
                TH = T // 2
                AT = sbt(ph, "ffn_aT", [128, NFC, TH], BF16)
                aTb = [P.buf("aT") for _ in range(2)]
                sgt = sbt(ph, "sgt", [128, 2, 512], BF16)
                sg = [sgt[:, j, :] for j in range(2)]
                sgb = [P.buf("sg") for _ in range(2)]
                cnt = 0
                for th in range(2):
                    for fp in range(NFC // 2):
                        tg, tgb = ring_next()
                        wgs = load_slab(wg_d, 0, NDC, fp * 256, 256, tg, tgb)
                        tu, tub = ring_next()
                        wus = load_slab(wu_d, 0, NDC, fp * 256, 256, tu, tub)
                        for fi in range(2):
                            fc = 2 * fp + fi
                            fcol = slice(fi * 128, (fi + 1) * 128)
                            bs = 4 * (cnt % 2)
                            cnt += 1
                            for (ws, wb, boff) in ((wgs, tgb, 0), (wus, tub, 2)):
                                for dc in range(NDC):
                                    for q in range(2):
                                        g = 2 * th + q
                                        bk = bs + boff + q
                                        P.op("pe", lambda e, ws=ws, dc=dc, fcol=fcol, g=g, bk=bk: e.matmul(
                                            banks[bk][:, :], lhsT=ws[:, dc, fcol], rhs=HT[:, dc, g * 512:(g + 1) * 512],
                                            start=(dc == 0), stop=(dc == NDC - 1)),
                                            reads=[wb, hTb[g]], writes=[bankb[bk]], signal=(dc == NDC - 1))
                            for q in range(2):
                                P.op("act", lambda e, q=q, bk=bs + q: e.activation(out=sg[q][:], in_=banks[bk][:, :], func=AF.Silu),
                                     reads=[bankb[bs + q]], writes=[sgb[q]])
                                P.op("dve", lambda e, q=q, bk=bs + 2 + q, fc=fc: e.tensor_tensor(
                                    out=AT[:, fc, q * 512:(q + 1) * 512], in0=sg[q][:], in1=banks[bk][:, :], op=ALU.mult),
                                    reads=[sgb[q], bankb[bs + 2 + q]], writes=[aTb[q]])
                    for dc in range(NDC):
                        t0, t0b = ring_next()
                        s0 = load_slab(wd_d, 0, 11, dc * 128, 128, t0, t0b)
                        t1, t1b = ring_next()
                        s1 = load_slab(wd_d, 11 * 128, 11, dc * 128, 128, t1, t1b)
                        bs = 2 * (dc % 4)
                        for fc in range(NFC):
                            ws, wb = (s0, t0b) if fc < 11 else (s1, t1b)
                            for q in range(2):
                                bk = bs + q
                                P.op("pe", lambda e, ws=ws, fc=fc, q=q, bk=bk: e.matmul(
                                    banks[bk][:, :], lhsT=ws[:, fc % 11, :], rhs=AT[:, fc, q * 512:(q + 1) * 512],
                                    start=(fc == 0), stop=(fc == NFC - 1)),
                                    reads=[wb, aTb[q]], writes=[bankb[bk]], signal=(fc == NFC - 1))
                        for q in range(2):
                            bk = bs + q
                            g = 2 * th + q
                            P.op("dve", lambda e, dc=dc, g=g, bk=bk: e.scalar_tensor_tensor(
                                out=XT[:, dc, g * 512:(g + 1) * 512], in0=banks[bk][:, :], scalar=0.5,
                                in1=XT[:, dc, g * 512:(g + 1) * 512], op0=ALU.mult, op1=ALU.add),
                                reads=[bankb[bk], XTb[g]], writes=[XTb[g]])
                P.barrier()

        def phase_final():
            with ExitStack() as ph:
                FT = sbt(ph, "fT", [128, NDC, 512], F32)
                FTb = P.buf("fT")
                ost = [sbt(ph, f"ost{j}", [128, D], F32) for j in range(2)]
                ostb = [P.buf("ost") for _ in range(2)]
                stores = []

                def out_fn(g, rs, rsb):
                    tok = slice(g * 512, (g + 1) * 512)
                    for dc in range(NDC):
                        P.op("dve", lambda e, dc=dc, tok=tok, rs=rs: e.scalar_tensor_tensor(
                            out=FT[:, dc, :], in0=XT[:, dc, tok], scalar=gcols[:, 3, dc:dc + 1], in1=rs[:],
                            op0=ALU.mult, op1=ALU.mult), reads=[XTb[g], rsb], writes=[FTb])
                    for it in range(4):
                        i = 4 * g + it
                        j = i % 2
                        for h in range(2):
                            bk = bank_rr[0] % 8
                            bank_rr[0] += 1
                            for k in range(4):
                                dc = 4 * h + k
                                P.op("pe", lambda e, dc=dc, k=k, bk=bk, it=it: e.transpose(
                                    out=banks[bk][:, k * 128:(k + 1) * 128], in_=FT[:, dc, it * 128:(it + 1) * 128], identity=identf[:]),
                                    reads=[FTb], writes=[bankb[bk]], signal=(k == 3))
                            if h == 0:
                                P.op("act", lambda e, j=j, bk=bk: e.copy(out=ost[j][:, 0:512], in_=banks[bk][:, :]),
                                     reads=[bankb[bk]], writes=[ostb[j]])
                            else:
                                P.op("dve", lambda e, j=j, bk=bk: e.tensor_copy(out=ost[j][:, 512:1024], in_=banks[bk][:, :]),
                                     reads=[bankb[bk]], writes=[ostb[j]])
                        stores.append(P.dma("sp", out_d[i * 128:(i + 1) * 128, :], ost[j][:], owner=ostb[j], reads=[ostb[j]]))

                rms_to_hT(3, None, None, ph, out_fn=out_fn)
                P.wait_only("sp", stores)

        def phase_mixer():
            with ExitStack() as ph:
                HT = sbt(ph, "mx_hT", [128, NDC, T], BF16)
                hTb = [P.buf("hT") for _ in range(NG)]
                Y = sbt(ph, "mx_Y", [128, NTI, D], BF16)
                Yb = [P.buf("Y") for _ in range(NTI)]
                rms_to_hT(1, HT, hTb, ph)
                GF = sbt(ph, "GF", [128, 2592], F32)
                GFW = 2592
                gpre = GF[:, 0:384].rearrange("p (j i) -> p j i", i=NTI)
                lf = GF[:, 384:704].rearrange("p (j i) -> p j i", i=NTI)
                tmpa = GF[:, 704:1024].rearrange("p (j i) -> p j i", i=NTI)
                tmpb = GF[:, 1024:1344].rearrange("p (j i) -> p j i", i=NTI)
                bsb = GF[:, 1344:1408]
                wk = GF[:, 1408:1472]
                wv = GF[:, 1472:1536]
                eb = GF[:, 1536:1600]
                ebL = GF[:, 1600:1728]
                tot = GF[:, 1728:1984].rearrange("p (h i) -> p h i", i=NTI)
                OFF_O = 1984
                off = GF[:, 1984:2240].rearrange("p (h i) -> p h i", i=NTI)
                cc_ = GF[:, 2240:2496].rearrange("p (h i) -> p h i", i=NTI)
                GB_O = 2496
                gbias = GF[:, 2496:2520]
                ieb = GF[:, 2528:2592]
                gb = P.buf("gates")

                def gates_block():
                    tg, tgb = ring_next()
                    gsl = tg[:, 0:NDC * 24].rearrange("p (c n) -> p c n", n=24)
                    P.dma("pool", gsl[:, :, 0:8], win_d[:, O_MLI:O_MLI + 8].rearrange("(c p) n -> p c n", p=128), owner=tgb, writes=[tgb])
                    P.dma("pool", gsl[:, :, 8:24], win_d[:, O_FXF:O_FXF + 16].rearrange("(c p) n -> p c n", p=128), owner=tgb, writes=[tgb])
                    P.dma("sp", gbias[:, 0:8], bin_d[O_MLI:O_MLI + 8].partition_broadcast(128), owner=gb, writes=[gb])
                    P.dma("sp", gbias[:, 8:24], bin_d[O_FXF:O_FXF + 16].partition_broadcast(128), owner=gb, writes=[gb])
                    bk = 0
                    for i in range(NTI):
                        for dc in range(NDC):
                            last = (i == NTI - 1 and dc == NDC - 1)
                            P.op("pe", lambda e, i=i, dc=dc: e.matmul(
                                banks[0][:, i * 24:(i + 1) * 24], lhsT=HT[:, dc, i * 128:(i + 1) * 128], rhs=gsl[:, dc, :],
                                start=(i == 0 and dc == 0), stop=(dc == NDC - 1), skip_group_check=True),
                                reads=[tgb, hTb[i // 4]], writes=[bankb[0]], signal=last)
                    P.op("dve", lambda e: e.tensor_tensor(
                        out=gpre.rearrange("p j i -> p i j"), in0=banks[0][:, 0:NTI * 24].rearrange("p (i j) -> p i j", j=24),
                        in1=bass.AP(GF, GB_O, [[GFW, 128], [0, NTI], [1, 24]]), op=ALU.add), reads=[bankb[0], gb], writes=[gb])
                    xg = gpre[:, 4:24, :]
                    DV = lambda fn: P.op("dve", fn, reads=[gb], writes=[gb])
                    AC = lambda fn: P.op("act", fn, reads=[gb], writes=[gb])
                    DV(lambda e: e.tensor_scalar(out=tmpa[:], in0=xg, scalar1=-1.0, scalar2=None, op0=ALU.mult))
                    DV(lambda e: e.tensor_tensor(out=tmpa[:], in0=tmpa[:], in1=xg, op=ALU.max))
                    AC(lambda e: e.activation(out=tmpa[:], in_=tmpa[:], func=AF.Exp, scale=-1.0))
                    AC(lambda e: e.activation(out=tmpa[:], in_=tmpa[:], func=AF.Ln, bias=1.0))
                    DV(lambda e: e.tensor_scalar(out=tmpb[:], in0=xg, scalar1=0.0, scalar2=None, op0=ALU.min))
                    DV(lambda e: e.tensor_tensor(out=lf[:], in0=tmpb[:], in1=tmpa[:], op=ALU.subtract))
                    lfa = GF[:, 384:448]
                    lfb = GF[:, 448:704]
                    li = GF[:, 0:64]
                    for k, m in enumerate((tri64, ones64, oneslo, oneshi)):
                        P.op("pe", lambda e, k=k, m=m: e.matmul(banks[1][:, k * 64:(k + 1) * 64], lhsT=m[:], rhs=lfa,
                                                                 start=(k == 0), stop=True, skip_group_check=True),
                             reads=[gb], writes=[bankb[1]], signal=(k == 3))
                    P.op("pe", lambda e: e.matmul(banks[2][:, 0:256], lhsT=tri128[:], rhs=lfb, start=True, stop=True, skip_group_check=True),
                         reads=[gb], writes=[bankb[2]], signal=False)
                    P.op("pe", lambda e: e.matmul(banks[2][:, 256:512], lhsT=allones[:], rhs=lfb, start=False, stop=True, skip_group_check=True),
                         reads=[gb], writes=[bankb[2]], signal=True)
                    A1 = lambda fn: P.op("act", fn, reads=[gb, bankb[1]], writes=[gb])
                    D1 = lambda fn: P.op("dve", fn, reads=[gb, bankb[1]], writes=[gb])
                    A1(lambda e: e.copy(out=bsb[:], in_=banks[1][:, 0:64]))
                    D1(lambda e: e.tensor_tensor(out=wk[:], in0=li, in1=bsb[:], op=ALU.subtract))
                    D1(lambda e: e.tensor_tensor(out=wv[:], in0=banks[1][:, 64:128], in1=bsb[:], op=ALU.subtract))
                    D1(lambda e: e.tensor_tensor(out=wv[:], in0=wv[:], in1=li, op=ALU.add))
                    A1(lambda e: e.activation(out=wk[:], in_=wk[:], func=AF.Exp, bias=-LN16))
                    A1(lambda e: e.activation(out=wv[:], in_=wv[:], func=AF.Exp, bias=-LN16))
                    A1(lambda e: e.activation(out=eb[:], in_=bsb[:], func=AF.Exp))
                    A1(lambda e: e.activation(out=ieb[:], in_=bsb[:], func=AF.Exp, scale=-1.0))
                    A1(lambda e: e.activation(out=ebL[:], in_=banks[1][:, 128:256], func=AF.Exp))
                    A2 = lambda fn: P.op("act", fn, reads=[gb, bankb[2]], writes=[gb])
                    D2 = lambda fn: P.op("dve", fn, reads=[gb, bankb[2]], writes=[gb])
                    A2(lambda e: e.copy(out=GF[:, 1728:1984], in_=banks[2][:, 256:512]))
                    D2(lambda e: e.memset(off[:, :, 0:1], 0.0))
                    for i in range(1, NTI):
                        D2(lambda e, i=i: e.tensor_tensor(out=off[:, :, i:i + 1], in0=off[:, :, i - 1:i], in1=tot[:, :, i - 1:i], op=ALU.add))
                    D2(lambda e: e.tensor_tensor(out=GF[:, 2240:2496], in0=banks[2][:, 0:256], in1=GF[:, 1984:2240], op=ALU.add))


                if not do_fox:
                    gates_block()

                if not do_fox:
                    for i in range(NTI):
                        P.op("dve", lambda e, i=i: e.memset(Y[:, i, :], 0.0), writes=[Yb[i]])
                with ExitStack() as pf:
                    qh = [sbt(pf, f"qh{j}", [128, T], BF16) for j in range(2)]
                    kh = [sbt(pf, f"kh{j}", [128, T], BF16) for j in range(2)]
                    qkb = [P.buf("qk2a"), P.buf("qk2b")]
                    VAf = sbt(pf, "VAf", [128, NTI, 2, 65], BF16)
                    VAb = P.buf("VA")
                    sgbt = sbt(pf, "sgb", [128, NTI, 128], BF16)
                    sgbb = P.buf("sgb")
                    BRf = sbt(pf, "BRf", [1, 512], BF16)
                    brb = P.buf("brow")
                    ones512 = sbt(pf, "ones512", [1, 512], BF16)
                    pTt = sbt(pf, "pTft", [128, 4, 512], BF16)
                    pTf = [pTt[:, j, :] for j in range(4)]
                    pTb = [P.buf("pT") for _ in range(4)]
                    rect = sbt(pf, "rect", [128, 2, 4], F32)
                    recb = [P.buf("rec") for _ in range(2)]
                    RW = sbt(pf, "RW", [128, 6, 256], F32)
                    PCS = sbt(pf, "PCS", [128, 6, 256], BF16)
                    rwb = P.buf("rw")
                    P.op("dve", lambda e: e.memset(VAf[:, :, :, 64:65], 1.0), writes=[VAb])
                    P.op("dve", lambda e: e.memset(ones512[:], 1.0), writes=[brb])
                    if do_fox:
                        for j in range(2):
                            P.op("dve", lambda e, j=j: e.memset(qh[j][64:70, :], 1.0), writes=[qkb[j]])
                            P.op("dve", lambda e, j=j: e.memset(kh[j][64:70, :], 1.0), writes=[qkb[j]])
                    def fox_rows():
                        for (k, src) in ((0, cc_), (1, off)):
                            for par in range(2):
                                P.op("dve", lambda e, k=k, src=src, par=par: e.tensor_copy(
                                    out=RW[:, 2 + k, par * 128:(par + 1) * 128].rearrange("p (h g) -> p h g", g=8),
                                    in_=src[:, :, par::2]), reads=[gb], writes=[rwb])
                            for par in range(2):
                                P.op("pe", lambda e, k=k, par=par: e.transpose(
                                    out=banks[6][:, k * 256 + par * 128:k * 256 + (par + 1) * 128],
                                    in_=RW[:, 2 + k, par * 128:(par + 1) * 128], identity=identf),
                                    reads=[rwb], writes=[bankb[6]], signal=(par == 1))
                            P.op("act", lambda e, k=k: e.copy(out=RW[:, k, :], in_=banks[6][:, k * 256:(k + 1) * 256]),
                                 reads=[bankb[6]], writes=[rwb])
                        DR = lambda fn: P.op("dve", fn, reads=[rwb], writes=[rwb])
                        for (k, pc0, mul) in ((1, 0, 8.0), (0, 3, -8.0)):
                            x8 = RW[:, 2, :]
                            r1 = RW[:, 3, :]
                            DR(lambda e, k=k, mul=mul, x8=x8: e.tensor_scalar(out=x8, in0=RW[:, k, :], scalar1=mul, scalar2=None, op0=ALU.mult))
                            DR(lambda e, pc0=pc0, x8=x8: e.tensor_copy(out=PCS[:, pc0, :], in_=x8))
                            DR(lambda e, pc0=pc0, x8=x8, r1=r1: e.tensor_tensor(out=r1, in0=x8, in1=PCS[:, pc0, :], op=ALU.subtract))
                            DR(lambda e, pc0=pc0, r1=r1: e.tensor_copy(out=PCS[:, pc0 + 1, :], in_=r1))
                            DR(lambda e, pc0=pc0, r1=r1: e.tensor_tensor(out=r1, in0=r1, in1=PCS[:, pc0 + 1, :], op=ALU.subtract))
                            DR(lambda e, pc0=pc0, r1=r1: e.tensor_copy(out=PCS[:, pc0 + 2, :], in_=r1))
                        for j in range(6):
                            P.dma("sp", scr_d[j].rearrange("h (g n) -> (h g) n", n=256), PCS[:, j, :], owner=rwb, reads=[rwb], writes=[scrb])
                    scnt = 0
                    for p in (range(8) if do_fox else ()):
                        tq, tqb = ring_next()
                        wq = load_slab(win_d, 0, NDC, O_FXQ + p * 128, 128, tq, tqb, 0)
                        wkk = load_slab(win_d, 0, NDC, O_FXK + p * 128, 128, tq, tqb, NDC * 128)
                        tv, tvb = ring_next()
                        wvg = tv[:, 0:NDC * 256].rearrange("p (c n) -> p c n", n=256)
                        P.dma("pool", wvg[:, :, 0:128], win_d[:, O_FXV + p * 128:O_FXV + (p + 1) * 128].rearrange("(c p) n -> p c n", p=128),
                              owner=tvb, writes=[tvb])
                        P.dma("pool", wvg[:, :, 128:256], win_d[:, O_GB + p * 128:O_GB + (p + 1) * 128].rearrange("(c p) n -> p c n", p=128),
                              owner=tvb, writes=[tvb])
                        for k, o in enumerate((O_FXV, O_GB, O_FXQ, O_FXK)):
                            P.dma("pool", BRf[:, k * 128:(k + 1) * 128], bin_d[o + p * 128:o + (p + 1) * 128].rearrange("(o n) -> o n", o=1),
                                  owner=brb, writes=[brb])
                        for (ws, dst, bseg) in ((wq, qh, 2), (wkk, kh, 3)):
                            for g in range(NG):
                                bk = 6 + (scnt % 2)
                                scnt += 1
                                for dc in range(NDC):
                                    P.op("pe", lambda e, ws=ws, dc=dc, g=g, bk=bk: e.matmul(
                                        banks[bk][:, :], lhsT=ws[:, dc, :], rhs=HT[:, dc, g * 512:(g + 1) * 512],
                                        start=(dc == 0), stop=False),
                                        reads=[tqb, hTb[g]], writes=[bankb[bk]], signal=False)
                                P.op("pe", lambda e, bk=bk, bseg=bseg: e.matmul(
                                    banks[bk][:, :], lhsT=BRf[:, bseg * 128:(bseg + 1) * 128], rhs=ones512[:], start=False, stop=True),
                                    reads=[brb], writes=[bankb[bk]], signal=True)
                                P.op("act", lambda e, dst=dst, g=g, bk=bk: e.copy(out=dst[0][0:64, g * 512:(g + 1) * 512], in_=banks[bk][0:64, :]),
                                     reads=[bankb[bk]], writes=[qkb[0]])
                                P.op("dve", lambda e, dst=dst, g=g, bk=bk: e.tensor_copy(out=dst[1][0:64, g * 512:(g + 1) * 512], in_=banks[bk][64:128, :]),
                                     reads=[bankb[bk]], writes=[qkb[1]])
                        for i in range(NTI):
                            bk = 6 + (scnt % 2)
                            scnt += 1
                            for dc in range(NDC):
                                P.op("pe", lambda e, wvg=wvg, dc=dc, i=i, bk=bk: e.matmul(
                                    banks[bk][:, 0:256], lhsT=HT[:, dc, i * 128:(i + 1) * 128], rhs=wvg[:, dc, :],
                                    start=(dc == 0), stop=False, skip_group_check=True),
                                    reads=[tvb, hTb[i // 4]], writes=[bankb[bk]], signal=False)
                            P.op("pe", lambda e, bk=bk: e.matmul(banks[bk][:, 0:256], lhsT=onesb[0:1, :], rhs=BRf[:, 0:256], start=False, stop=True,
                                                                 skip_group_check=True), reads=[brb], writes=[bankb[bk]], signal=True)
                            P.op("act", lambda e, i=i, bk=bk: e.copy(out=VAf[:, i, :, 0:64], in_=banks[bk][:, 0:128].rearrange("p (h d) -> p h d", d=64)),
                                 reads=[bankb[bk]], writes=[VAb])
                            P.op("act", lambda e, i=i, bk=bk: e.activation(out=sgbt[:, i, :], in_=banks[bk][:, 128:256], func=AF.Sigmoid),
                                 reads=[bankb[bk]], writes=[sgbb])
                        if p == 0:
                            gates_block()
                            fox_rows()
                        for hl in range(2):
                            h = 2 * p + hl
                            P.dma("sp", qh[hl][67:70, :], scr_d[0:3, h, :], owner=qkb[hl], reads=[scrb], writes=[qkb[hl]])
                            P.dma("sp", kh[hl][64:67, :], scr_d[3:6, h, :], owner=qkb[hl], reads=[scrb], writes=[qkb[hl]])
                        items = [(hl, tg, kb) for hl in range(2) for tg in range(NG) for kb in range(4 * tg + 4)]
                        LOOK = 3

                        def emit_s(n):
                            hl, tg, kb = items[n]
                            sb_ = n % 4
                            t0 = max(kb * 128, tg * 512)
                            t1 = (tg + 1) * 512
                            N = t1 - t0
                            diag = (kb >= 4 * tg)
                            P.op("pe", lambda e, hl=hl, kb=kb, t0=t0, t1=t1, N=N, sb_=sb_, diag=diag: e.matmul(
                                banks[sb_][:, 0:N], lhsT=kh[hl][0:70, kb * 128:(kb + 1) * 128], rhs=qh[hl][0:70, t0:t1],
                                start=True, stop=(not diag)), reads=[qkb[hl]], writes=[bankb[sb_]], signal=(not diag))
                            if diag:
                                P.op("pe", lambda e, sb_=sb_: e.matmul(banks[sb_][:, 0:128], lhsT=identb, rhs=maskneg,
                                                                      start=False, stop=True), writes=[bankb[sb_]], signal=True)
                            P.op("act", lambda e, sb_=sb_, N=N: e.activation(out=pTf[sb_][:, 0:N], in_=banks[sb_][:, 0:N], func=AF.Exp, scale=0.125),
                                 reads=[bankb[sb_]], writes=[pTb[sb_]])

                        def emit_pv(n):
                            hl, tg, kb = items[n]
                            h = 2 * p + hl
                            sb_ = n % 4
                            ab = 4 + (tg % 2)
                            tb0 = max(kb, 4 * tg)
                            for tb in range(tb0, 4 * tg + 4):
                                j = tb - 4 * tg
                                c0 = (tb - tb0) * 128
                                first = (kb == 0 and j == 0)
                                last = (kb == 4 * tg + 3)
                                P.op("pe", lambda e, sb_=sb_, kb=kb, hl=hl, ab=ab, j=j, c0=c0, first=first: e.matmul(
                                    banks[ab][:, j * 65:(j + 1) * 65], lhsT=pTf[sb_][:, c0:c0 + 128], rhs=VAf[:, kb, hl, :],
                                    start=first, stop=True, skip_group_check=True),
                                    reads=[pTb[sb_], VAb], writes=[bankb[ab]], signal=(tb == 4 * tg + 3))
                            if kb == 4 * tg + 3:
                                rj = tg % 2
                                P.op("dve", lambda e, rj=rj, ab=ab: e.reciprocal(
                                    out=rect[:, rj, :], in_=banks[ab][:, 0:260].rearrange("p (j c) -> p j c", c=65)[:, :, 64]),
                                    reads=[bankb[ab]], writes=[recb[rj]])
                                for j in range(4):
                                    tb = 4 * tg + j
                                    P.op("dve", lambda e, rj=rj, ab=ab, tb=tb, j=j, hl=hl, h=h: e.scalar_tensor_tensor(
                                        out=Y[:, tb, h * 64:(h + 1) * 64], in0=banks[ab][:, j * 65:j * 65 + 64], scalar=rect[:, rj, j:j + 1],
                                        in1=sgbt[:, tb, hl * 64:(hl + 1) * 64], op0=ALU.mult, op1=ALU.mult),
                                        reads=[bankb[ab], recb[rj], sgbb], writes=[Yb[tb]])

                        for n in range(min(LOOK, len(items))):
                            emit_s(n)
                        for n in range(len(items)):
                            if n + LOOK < len(items):
                                emit_s(n + LOOK)
                            emit_pv(n)
                    P.barrier()


                with ExitStack() as pm:
                    qT = sbt(pm, "qT", [128, 2, T], BF16)
                    kT = sbt(pm, "kT", [128, 2, T], BF16)
                    qkb = P.buf("qk")
                    VA = sbt(pm, "VAm", [128, NTI, 258], BF16)
                    VAb = P.buf("VAm")
                    GAt = sbt(pm, "GAt", [128, 3, 256], BF16)
                    GA = [GAt[:, j, :] for j in range(3)]
                    GAb = [P.buf("GA") for _ in range(3)]
                    gn = sbt(pm, "gn", [128, 256], F32)
                    gnb = P.buf("gn")
                    BR = sbt(pm, "BRm", [1, 768], BF16)
                    brb = P.buf("browm")
                    stgt = sbt(pm, "stgt", [128, 2, 516], F32)
                    stg = [stgt[:, j, :] for j in range(2)]
                    stgb = [P.buf("stg") for _ in range(2)]
                    cacct = sbt(pm, "cacc", [128, 2, 512], F32)
                    cacc = [cacct[:, j, :] for j in range(2)]
                    caccb = [P.buf("cacc") for _ in range(2)]
                    sig = sbt(pm, "sig", [128, 512], F32)
                    sigb = P.buf("sig")
                    CT = sbt(pm, "CT", [128, 2, 257], F32)
                    CTb = P.buf("CT")
                    CTht = sbt(pm, "CTht", [128, 2, 2, 258], BF16)
                    CTh = [CTht[:, j, :, 0:257] for j in range(2)]
                    CThb = [P.buf("CTh") for _ in range(2)]
                    ktokt = sbt(pm, "ktokt", [128, 2, 256], BF16)
                    ktok = [ktokt[:, j, :] for j in range(2)]
                    ktokb = [P.buf("ktok") for _ in range(2)]
                    pTt = sbt(pm, "pTmt", [128, 2, 64], BF16)
                    pT = [pTt[:, j, :] for j in range(2)]
                    pTb = [P.buf("pTm") for _ in range(2)]
                    hnnt = sbt(pm, "hnnt", [128, 2, 256], BF16)
                    hnn = [hnnt[:, j, :] for j in range(2)]
                    hnnb = [P.buf("hnn") for _ in range(2)]
                    smt_ = sbt(pm, "smt", [128, 2, 16], F32)
                    sm = [smt_[:, j, :] for j in range(2)]
                    smb = [P.buf("sm") for _ in range(2)]
                    scnt = 0
                    for hd in (range(4) if do_mlstm else ()):
                        tq, tqb = ring_next()
                        wq = load_slab(win_d, 0, NDC, O_MLQ + hd * 256, 256, tq, tqb)
                        tk, tkb = ring_next()
                        wkk = load_slab(win_d, 0, NDC, O_MLK + hd * 256, 256, tk, tkb)
                        tv, tvb = ring_next()
                        wvv = load_slab(win_d, 0, NDC, O_MLV + hd * 256, 256, tv, tvb)
                        for k, o in enumerate((O_MLV, O_MLO, O_GA)):
                            P.dma("pool", BR[:, k * 256:(k + 1) * 256], bin_d[o + hd * 256:o + (hd + 1) * 256].rearrange("(o n) -> o n", o=1),
                                  owner=brb, writes=[brb])
                        P.dma("sp", gn[:], hn_d[hd * 256:(hd + 1) * 256].partition_broadcast(128), owner=gnb, writes=[gnb])
                        pend = [None]

                        def flush_silu():
                            if pend[0] is not None:
                                dstv, cj = pend[0]
                                P.op("act", lambda e, dstv=dstv, cj=cj: e.activation(out=dstv, in_=cacc[cj], func=AF.Silu),
                                     reads=[caccb[cj]], writes=[qkb])
                                pend[0] = None

                        for (ws, wb, dstT, bcol, cofs) in ((wq, tqb, qT, 0, 0), (wkk, tkb, kT, 1, 8)):
                            for cc in range(2):
                                ch = hd * 2 + cc
                                cch = cofs + ch
                                for g in range(NG):
                                    bk = (scnt % 2)
                                    j = scnt % 2
                                    scnt += 1
                                    for dc in range(NDC):
                                        P.op("pe", lambda e, ws=ws, dc=dc, cc=cc, g=g, bk=bk: e.matmul(
                                            banks[bk][:, :], lhsT=ws[:, dc, cc * 128:(cc + 1) * 128], rhs=HT[:, dc, g * 512:(g + 1) * 512],
                                            start=(dc == 0), stop=(dc == NDC - 1)),
                                            reads=[wb, hTb[g]], writes=[bankb[bk]], signal=(dc == NDC - 1))
                                    if g == 0:
                                        P.op("dve", lambda e, j=j: e.memset(stg[j][:, 0:3], 0.0), writes=[stgb[j]])
                                    else:
                                        P.op("dve", lambda e, j=j: e.tensor_copy(out=stg[j][:, 0:3], in_=stg[1 - j][:, 512:515]),
                                             reads=[stgb[1 - j]], writes=[stgb[j]])
                                    P.op("act", lambda e, j=j, bk=bk, bcol=bcol, ch=ch: e.activation(
                                        out=stg[j][:, 3:515], in_=banks[bk][:, :], func=AF.Identity, bias=bfm[:, bcol, ch:ch + 1]),
                                        reads=[bankb[bk]], writes=[stgb[j]])
                                    flush_silu()
                                    P.op("dve", lambda e, j=j, cch=cch: e.tensor_scalar(
                                        out=cacc[j], in0=stg[j][:, 3:515], scalar1=cwc[:, 3, cch:cch + 1], scalar2=cbc[:, cch:cch + 1],
                                        op0=ALU.mult, op1=ALU.add), reads=[stgb[j]], writes=[caccb[j]])
                                    for jj in range(3):
                                        P.op("dve", lambda e, j=j, cch=cch, jj=jj: e.scalar_tensor_tensor(
                                            out=cacc[j], in0=stg[j][:, jj:jj + 512], scalar=cwc[:, jj, cch:cch + 1], in1=cacc[j],
                                            op0=ALU.mult, op1=ALU.add), reads=[stgb[j], caccb[j]], writes=[caccb[j]])
                                    pend[0] = (dstT[:, cc, g * 512:(g + 1) * 512], j)
                        flush_silu()
                        to, tob = ring_next()
                        woo = load_slab(win_d, 0, NDC, O_MLO + hd * 256, 256, to, tob)
                        ta, tab = ring_next()
                        wga = load_slab(win_d, 0, NDC, O_GA + hd * 256, 256, ta, tab)
                        P.op("dve", lambda e: e.memset(VA[:, :, 256:257], 1.0), writes=[VAb])
                        for i in range(NTI):
                            bk = (scnt % 2)
                            scnt += 1
                            for dc in range(NDC):
                                P.op("pe", lambda e, dc=dc, i=i, bk=bk, wvv=wvv: e.matmul(
                                    banks[bk][:, 0:256], lhsT=HT[:, dc, i * 128:(i + 1) * 128], rhs=wvv[:, dc, :],
                                    start=(dc == 0), stop=False, skip_group_check=True),
                                    reads=[tvb, hTb[i // 4]], writes=[bankb[bk]], signal=False)
                            P.op("pe", lambda e, bk=bk: e.matmul(banks[bk][:, 0:256], lhsT=onesb[0:1, :], rhs=BR[:, 0:256], start=False, stop=True,
                                                                 skip_group_check=True), reads=[brb], writes=[bankb[bk]], signal=True)
                            P.op("act", lambda e, i=i, bk=bk: e.copy(out=VA[:, i, 0:256], in_=banks[bk][:, 0:256]),
                                 reads=[bankb[bk]], writes=[VAb])
                        P.op("dve", lambda e: e.memset(CT[:], 0.0), writes=[CTb])
                        P.op("dve", lambda e: e.memset(CTh[0], 0.0), writes=[CThb[0]])
                        b3bf = banks[3].bitcast(BF16)
                        UB = ((6, 7), (6, 7))

                        def gates(i):
                            s3 = i % 3
                            for dc in range(NDC):
                                for (ws, wb2, c0) in ((woo, tob, 0), (wga, tab, 256)):
                                    P.op("pe", lambda e, ws=ws, dc=dc, i=i, c0=c0: e.matmul(
                                        banks[0][:, c0:c0 + 256], lhsT=HT[:, dc, i * 128:(i + 1) * 128], rhs=ws[:, dc, :],
                                        start=(dc == 0 and c0 == 0), stop=False, skip_group_check=True),
                                        reads=[wb2, hTb[i // 4]], writes=[bankb[0]], signal=False)
                            P.op("pe", lambda e: e.matmul(banks[0][:, :], lhsT=onesb[0:1, :], rhs=BR[:, 256:768], start=False, stop=True,
                                                          skip_group_check=True), reads=[brb], writes=[bankb[0]], signal=True)
                            P.op("act", lambda e: e.activation(out=sig[:], in_=banks[0][:, :], func=AF.Exp, scale=-1.0),
                                 reads=[bankb[0]], writes=[sigb])
                            P.op("act", lambda e: e.activation(out=sig[:], in_=sig[:], func=AF.Ln, bias=1.0), reads=[sigb], writes=[sigb])
                            P.op("dve", lambda e: e.tensor_tensor(out=sig[:, 0:256], in0=sig[:, 0:256], in1=sig[:, 256:512], op=ALU.add),
                                 reads=[sigb], writes=[sigb])

                        def gatesB(i):
                            s3 = i % 3
                            P.op("act", lambda e: e.activation(out=sig[:, 256:512], in_=sig[:, 0:256], func=AF.Exp, scale=-1.0),
                                 reads=[sigb], writes=[sigb])
                            P.op("dve", lambda e, s3=s3: e.tensor_tensor(out=GA[s3], in0=sig[:, 256:512], in1=gn[:], op=ALU.mult),
                                 reads=[sigb, gnb], writes=[GAb[s3]])

                        def pre(c):
                            i, hf = c // 2, c % 2
                            rows = slice(hf * 64, hf * 64 + 64)
                            tok = slice(c * 64, (c + 1) * 64)
                            s2 = c % 2
                            gcol = hd * NTI + i
                            for cc in range(2):
                                P.op("pe", lambda e, rows=rows, cc=cc, tok=tok: e.transpose(
                                    out=b3bf[rows, cc * 128:(cc + 1) * 128], in_=kT[:, cc, tok], identity=identb),
                                    reads=[qkb], writes=[bankb[3]], signal=(cc == 1))
                            P.op("act", lambda e, rows=rows, s2=s2, gcol=gcol: e.activation(
                                out=ktok[s2][rows, :], in_=b3bf[rows, 0:256], func=AF.Copy, scale=wv[rows, gcol:gcol + 1]),
                                reads=[bankb[3], gb], writes=[ktokb[s2]])
                            for cc in range(2):
                                P.op("pe", lambda e, rows=rows, cc=cc, tok=tok: e.matmul(
                                    banks[2][rows, 0:64], lhsT=kT[:, cc, tok], rhs=qT[:, cc, tok], start=(cc == 0), stop=(cc == 1)),
                                    reads=[qkb], writes=[bankb[2]], signal=(cc == 1))
                            P.op("dve", lambda e, rows=rows, s2=s2, gcol=gcol: e.scalar_tensor_tensor(
                                out=pT[s2][rows, :], in0=banks[2][rows, 0:64], scalar=wk[rows, gcol:gcol + 1], in1=mask2[rows, :],
                                op0=ALU.mult, op1=ALU.mult), reads=[bankb[2], gb], writes=[pTb[s2]])

                        def main(c):
                            i, hf = c // 2, c % 2
                            rows = slice(hf * 64, hf * 64 + 64)
                            tok = slice(c * 64, (c + 1) * 64)
                            s2 = c % 2
                            s3 = i % 2
                            gcol = hd * NTI + i
                            ob = 4 + (i % 2)
                            cur = c % 2
                            ub = UB[c % 2]
                            for cc in range(2):
                                P.op("pe", lambda e, rows=rows, cc=cc, s2=s2, ub=ub, i=i: e.matmul(
                                    banks[ub[cc]][:, 0:257], lhsT=ktok[s2][rows, cc * 128:(cc + 1) * 128], rhs=VA[rows, i, 0:257],
                                    start=True, stop=True), reads=[ktokb[s2], VAb], writes=[bankb[ub[cc]]], signal=True)
                            P.op("pe", lambda e, rows=rows, s2=s2, i=i, ob=ob: e.matmul(
                                banks[ob][rows, 0:257], lhsT=pT[s2][rows, :], rhs=VA[rows, i, 0:257], start=True, stop=False),
                                reads=[pTb[s2], VAb], writes=[bankb[ob]], signal=False)
                            for cc in range(2):
                                P.op("pe", lambda e, rows=rows, cc=cc, tok=tok, ob=ob, cur=cur: e.matmul(
                                    banks[ob][rows, 0:257], lhsT=qT[:, cc, tok], rhs=CTh[cur][:, cc, :], start=False, stop=(cc == 1)),
                                    reads=[qkb, CThb[cur]], writes=[bankb[ob]], signal=(cc == 1))
                            ecol = hf * 64 + gcol
                            P.op("dve", lambda e, ecol=ecol: e.scalar_tensor_tensor(
                                out=CT[:], in0=CT[:], scalar=ebL[:, ecol:ecol + 1],
                                in1=bank67[:, :].rearrange("p (c n) -> p c n", n=512)[:, :, 0:257],
                                op0=ALU.mult, op1=ALU.add), reads=[CTb, bankb[6], bankb[7], gb], writes=[CTb])
                            P.op("act", lambda e, cur=cur: e.copy(out=CTh[1 - cur], in_=CT[:]), reads=[CTb], writes=[CThb[1 - cur]])

                        def postA(i):
                            s3 = i % 2
                            ob = 4 + (i % 2)
                            gcol = hd * NTI + i
                            smt = sm[s3]
                            SD = lambda fn: P.op("dve", fn, reads=[smb[s3], bankb[ob], gb], writes=[smb[s3]])
                            SD(lambda e: e.tensor_scalar(out=smt[:, 0:1], in0=banks[ob][:, 256:257], scalar1=-1.0, scalar2=None, op0=ALU.mult))
                            SD(lambda e: e.tensor_tensor(out=smt[:, 0:1], in0=smt[:, 0:1], in1=banks[ob][:, 256:257], op=ALU.max))
                            SD(lambda e: e.tensor_tensor(out=smt[:, 1:2], in0=smt[:, 0:1], in1=ieb[:, gcol:gcol + 1], op=ALU.max))
                            SD(lambda e: e.reciprocal(out=smt[:, 2:3], in_=smt[:, 1:2]))
                            SD(lambda e: e.bn_stats(smt[:, 4:10], banks[ob][:, 0:256]))
                            SD(lambda e: e.bn_aggr(smt[:, 10:12], smt[:, 4:10]))
                            SD(lambda e: e.scalar_tensor_tensor(out=smt[:, 3:4], in0=smt[:, 2:3], scalar=smt[:, 2:3], in1=smt[:, 11:12],
                                                                op0=ALU.mult, op1=ALU.mult))

                        def postB(i):
                            s3 = i % 2
                            smt = sm[s3]
                            SA = lambda fn: P.op("act", fn, reads=[smb[s3]], writes=[smb[s3]])
                            SA(lambda e: e.activation(out=smt[:, 12:13], in_=smt[:, 3:4], func=AF.Ln, bias=epsc))
                            SA(lambda e: e.activation(out=smt[:, 13:14], in_=smt[:, 12:13], func=AF.Exp, scale=-0.5))

                        def postC(i):
                            s3 = i % 2
                            ob = 4 + (i % 2)
                            smt = sm[s3]
                            SD = lambda fn: P.op("dve", fn, reads=[smb[s3]], writes=[smb[s3]])
                            SD(lambda e: e.tensor_tensor(out=smt[:, 14:15], in0=smt[:, 2:3], in1=smt[:, 13:14], op=ALU.mult))
                            SD(lambda e: e.scalar_tensor_tensor(out=smt[:, 15:16], in0=smt[:, 10:11], scalar=-1.0, in1=smt[:, 14:15],
                                                                op0=ALU.mult, op1=ALU.mult))
                            P.op("act", lambda e: e.activation(out=hnn[s3], in_=banks[ob][:, 0:256], func=AF.Identity,
                                                               scale=smt[:, 14:15], bias=smt[:, 15:16]),
                                 reads=[bankb[ob], smb[s3]], writes=[hnnb[s3]])

                        def postE(i):
                            s3 = i % 2
                            P.op("dve", lambda e: e.tensor_tensor(out=hnn[s3], in0=hnn[s3], in1=GA[i % 3], op=ALU.mult),
                                 reads=[hnnb[s3], GAb[i % 3]], writes=[hnnb[s3]])
                            ydst = Y[:, i, hd * 256:(hd + 1) * 256]
                            P.op("dve", lambda e: e.tensor_tensor(out=ydst, in0=ydst, in1=hnn[s3], op=ALU.add),
                                 reads=[hnnb[s3], Yb[i]], writes=[Yb[i]])

                        NCH = T // 64
                        gates(0)
                        gatesB(0)
                        pre(0)
                        for c in range(NCH):
                            if c + 1 < NCH:
                                pre(c + 1)
                            if c % 2 == 0 and c >= 2:
                                postB(c // 2 - 1)
                            main(c)
                            if c % 2 == 1:
                                postA(c // 2)
                                if c >= 3:
                                    postE((c - 3) // 2)
                                if c // 2 + 1 < NTI:
                                    gatesB(c // 2 + 1)
                            else:
                                if c >= 2:
                                    postC(c // 2 - 1)
                                if c // 2 + 1 < NTI:
                                    gates(c // 2 + 1)
                        postB(NTI - 1)
                        postC(NTI - 1)
                        postE(NTI - 1)
                    P.barrier()

                for i in range(NTI):
                    for h in range(2):
                        bk = bank_rr[0] % 8
                        bank_rr[0] += 1
                        bbf = banks[bk].bitcast(BF16)
                        for k in range(4):
                            dc = 4 * h + k
                            P.op("pe", lambda e, dc=dc, k=k, i=i, bbf=bbf: e.transpose(
                                out=bbf[:, k * 128:(k + 1) * 128], in_=Y[:, i, dc * 128:(dc + 1) * 128], identity=identb[:]),
                                reads=[Yb[i]], writes=[bankb[bk]], signal=(k == 3))
                        src = bbf[:, 0:512].rearrange("p (c n) -> p c n", n=128)
                        dst = HT[:, 4 * h:4 * h + 4, i * 128:(i + 1) * 128]
                        if (2 * i + h) % 2 == 0:
                            P.op("act", lambda e, src=src, dst=dst: e.copy(out=dst, in_=src), reads=[bankb[bk]], writes=[hTb[i // 4]])
                        else:
                            P.op("dve", lambda e, src=src, dst=dst: e.tensor_copy(out=dst, in_=src), reads=[bankb[bk]], writes=[hTb[i // 4]])
                for nn in range(4):
                    tw, twb = ring_next()
                    wo = load_slab(wo_d, 0, NDC, nn * 256, 256, tw, twb)
                    for hh in range(2):
                        ncx = 2 * nn + hh
                        bs = 4 * (ncx % 2)
                        for dc in range(NDC):
                            for g in range(NG):
                                P.op("pe", lambda e, dc=dc, g=g, hh=hh, bs=bs, wo=wo: e.matmul(
                                    banks[bs + g][:, :], lhsT=wo[:, dc, hh * 128:(hh + 1) * 128], rhs=HT[:, dc, g * 512:(g + 1) * 512],
                                    start=(dc == 0), stop=(dc == NDC - 1)),
                                    reads=[twb, hTb[g]], writes=[bankb[bs + g]], signal=(dc == NDC - 1))
                        for g in range(NG):
                            P.op("dve", lambda e, ncx=ncx, g=g, bs=bs: e.tensor_tensor(
                                out=XT[:, ncx, g * 512:(g + 1) * 512], in0=banks[bs + g][:, :], in1=XT[:, ncx, g * 512:(g + 1) * 512], op=ALU.add),
                                reads=[bankb[bs + g], XTb[g]], writes=[XTb[g]])
                P.barrier()
            return None

        setup()
        phase_load()
        if stop_after not in ("load",):
            phase_ffn(0, wg1_d, wu1_d, wd1_d)
        if stop_after not in ("load", "ffn1"):
            phase_mixer()
        if stop_after not in ("load", "ffn1", "mixer"):
            phase_ffn(2, wg2_d, wu2_d, wd2_d)
        phase_final()
        P.replay()
    return nc


_NC_CACHE = {}
_NAMES = ("ffn1_norm", "ffn1_w_gate", "ffn1_w_up", "ffn1_w_down", "mix_norm", "w_in", "b_in", "conv_w", "conv_b",
          "ml_head_norm", "w_out", "ffn2_norm", "ffn2_w_gate", "ffn2_w_up", "ffn2_w_down")


def kernel(**inputs):
    stop_after = inputs.pop("_stop_after", "all")
    do_fox = inputs.pop("_do_fox", True)
    do_mlstm = inputs.pop("_do_mlstm", True)
    x = np.ascontiguousarray(np.asarray(inputs["x"], dtype=np.float32))
    shared = {}
    for n in _NAMES:
        a = np.asarray(inputs[n], dtype=np.float32)
        shared[n] = np.ascontiguousarray(a.reshape(a.shape[1:]))
    shared["final_norm"] = np.ascontiguousarray(np.asarray(inputs["final_norm"], dtype=np.float32))
    key = (stop_after, do_fox, do_mlstm)
    if key not in _NC_CACHE:
        _NC_CACHE[key] = build(stop_after, do_fox, do_mlstm)
    nc = _NC_CACHE[key]
    in_maps = []
    for b in range(8):
        m = {"x": np.ascontiguousarray(x[b])}
        m.update(shared)
        in_maps.append(m)
    res = run_bass_kernel_spmd(nc, in_maps, core_ids=list(range(8)))
    return np.stack([np.asarray(r["out"], dtype=np.float32) for r in res.results], axis=0)
```

```python
import numpy as np
from contextlib import ExitStack
import concourse.bass as bass
import concourse.mybir as mybir
from concourse.bass_utils import run_bass_kernel_spmd

F32 = mybir.dt.float32
BF16 = mybir.dt.bfloat16
AF = mybir.ActivationFunctionType
ALU = mybir.AluOpType

D = 1024
T = 2048
DFF = 2816
NIN = 9240
NDC = 8
NFC = 22
NG = 4
NTI = 16
EPS = 1e-6
O_MLQ, O_MLK, O_MLV, O_MLO, O_MLI, O_MLF = 0, 1024, 2048, 3072, 4096, 4100
O_FXQ, O_FXK, O_FXV, O_FXF, O_GA, O_GB = 4104, 5128, 6152, 7176, 7192, 8216
LN16 = 2.772588722239781


class Buf:
    __slots__ = ("name", "w", "r", "dsem", "dcnt")

    def __init__(self, name):
        self.name = name
        self.w = None
        self.r = {}
        self.dsem = None
        self.dcnt = 0


class Prog:
    ENG = ("pe", "act", "dve", "pool", "sp")

    def __init__(self, nc, stack):
        self.nc = nc
        self.stack = stack
        self.ops = {e: [] for e in self.ENG}
        self.tick = {e: 0 for e in self.ENG}
        self.waited = {e: {} for e in self.ENG}
        self.sems = {}
        self.nbuf = 0

    def buf(self, name):
        self.nbuf += 1
        return Buf(f"{name}_{self.nbuf}")

    def sem(self, name):
        if name not in self.sems:
            self.sems[name] = self.stack.enter_context(self.nc.semaphore(name))
        return self.sems[name]

    def _waits(self, eng, reads, writes, extra):
        ws = list(extra)
        for b in reads:
            if b.w is not None:
                ws.append(b.w)
        for b in writes:
            if b.w is not None:
                ws.append(b.w)
            ws.extend(b.r.items())
        out = []
        for (s, v) in ws:
            if eng == "pe" and s == "pe":
                continue
            if self.waited[eng].get(s, 0) >= v:
                continue
            self.waited[eng][s] = v
            out.append((s, v))
        return out

    def _update(self, tick, reads, writes):
        for b in reads:
            b.r[tick[0]] = max(b.r.get(tick[0], 0), tick[1])
        for b in writes:
            b.w = tick
            b.r = {}

    def op(self, eng, fn, reads=(), writes=(), signal=True, waits=()):
        w = self._waits(eng, reads, writes, waits)
        if signal:
            self.tick[eng] += 1
            tick = (eng, self.tick[eng])
            self.ops[eng].append((w, fn, ("inc", eng)))
        else:
            assert eng == "pe"
            tick = (eng, self.tick[eng] + 1)
            self.ops[eng].append((w, fn, None))
        self._update(tick, reads, writes)
        return tick

    def dma(self, eng, out, in_, owner, reads=(), writes=(), **kw):
        w = self._waits(eng, reads, writes, ())
        if owner.dsem is None:
            owner.dsem = "d_" + owner.name
            self.sem(owner.dsem)
        owner.dcnt += 1
        tick = (owner.dsem, 16 * owner.dcnt)
        self.ops[eng].append((w, (lambda e: e.dma_start(out=out, in_=in_, **kw)), ("dma", owner.dsem)))
        self._update(tick, reads, writes)
        return tick

    def wait_only(self, eng, ticks):
        w = self._waits(eng, (), (), ticks)
        if w:
            self.ops[eng].append((w, None, None))

    def barrier(self, engines=("pe", "act", "dve", "sp", "pool")):
        ticks = [(e, self.tick[e]) for e in ("pe", "act", "dve", "pool") if self.tick[e] > 0]
        for e in engines:
            self.wait_only(e, [t for t in ticks if t[0] != e])

    def replay(self):
        nc = self.nc
        for e in self.ENG:
            self.sem(e)
        sems = self.sems
        ops = self.ops

        def run(engname, engobj):
            for (w, fn, sig) in ops[engname]:
                for (s, v) in w:
                    engobj.wait_ge(sems[s], v)
                if fn is None:
                    continue
                inst = fn(engobj)
                if sig is not None:
                    inst.then_inc(sems[sig[1]], 16 if sig[0] == "dma" else 1)

        with nc.Block() as block:
            @block.tensor
            def _(e):
                run("pe", e)

            @block.scalar
            def _(e):
                run("act", e)

            @block.vector
            def _(e):
                run("dve", e)

            @block.gpsimd
            def _(e):
                run("pool", e)

            @block.sync
            def _(e):
                run("sp", e)


def build(stop_after="all", do_fox=True, do_mlstm=True):
    nc = bass.Bass("TRN2", target_bir_lowering=False)
    dt_in = lambda name, shape: nc.dram_tensor(name, shape, F32, kind="ExternalInput").ap()
    x_d = dt_in("x", [T, D])
    g1_d = dt_in("ffn1_norm", [D])
    wg1_d = dt_in("ffn1_w_gate", [D, DFF])
    wu1_d = dt_in("ffn1_w_up", [D, DFF])
    wd1_d = dt_in("ffn1_w_down", [DFF, D])
    gm_d = dt_in("mix_norm", [D])
    win_d = dt_in("w_in", [D, NIN])
    bin_d = dt_in("b_in", [NIN])
    cw_d = dt_in("conv_w", [4, 2 * D])
    cb_d = dt_in("conv_b", [2 * D])
    hn_d = dt_in("ml_head_norm", [D])
    wo_d = dt_in("w_out", [D, D])
    g2_d = dt_in("ffn2_norm", [D])
    wg2_d = dt_in("ffn2_w_gate", [D, DFF])
    wu2_d = dt_in("ffn2_w_up", [D, DFF])
    wd2_d = dt_in("ffn2_w_down", [DFF, D])
    gf_d = dt_in("final_norm", [D])
    out_d = nc.dram_tensor("out", [T, D], F32, kind="ExternalOutput").ap()
    scr_d = nc.dram_tensor("fox_scr", [6, 16, T], BF16, kind="Internal").ap()

    with ExitStack() as st:
        P = Prog(nc, st)
        _uid = [0]

        def sbt(stack, name, shape, dt):
            _uid[0] += 1
            return stack.enter_context(nc.sbuf_tensor(f"{name}_{_uid[0]}", shape, dt))

        XT = sbt(st, "XT", [128, NDC, T], F32)
        XTb = [P.buf(f"xt{g}") for g in range(NG)]
        CF = sbt(st, "CF", [128, 1232], F32)
        CB = sbt(st, "CB", [128, 512], BF16)
        identf = CF[:, 0:128]
        tri128 = CF[:, 128:256]
        tri64 = CF[:, 256:384]
        ones64 = CF[:, 384:512]
        oneslo = CF[:, 512:640]
        oneshi = CF[:, 640:768]
        allones = CF[:, 768:896]
        mask2 = CF[:, 896:960]
        gcols = CF[:, 960:992].rearrange("p (a c) -> p a c", c=NDC)
        bfm = CF[:, 992:1024].rearrange("p (a c) -> p a c", c=NDC)
        cwc = CF[:, 1024:1088].rearrange("p (a c) -> p a c", c=16)
        cbc = CF[:, 1088:1104]
        epsc = CF[:, 1104:1105]
        identb = CB[:, 0:128]
        onesms = CB[:, 128:256]
        onesb = CB[:, 256:384]
        maskneg = CB[:, 384:512]
        RSL = 2048
        NRING = 4
        ring_t = [sbt(st, f"ring{i}", [128, RSL], BF16) for i in range(NRING)]
        ring_b = [P.buf(f"ring{i}") for i in range(NRING)]
        ring_i = [0]
        banks = [st.enter_context(nc.psum_tensor(f"bank{i}", [128, 512], F32)) for i in range(6)]
        bank67 = st.enter_context(nc.psum_tensor("bank67", [128, 1024], F32))
        banks.append(bank67[:, 0:512])
        banks.append(bank67[:, 512:1024])
        bankb = [P.buf(f"bank{i}") for i in range(8)]
        cbuf = P.buf("consts")
        scrb = P.buf("scr")

        def ring_next():
            i = ring_i[0] % NRING
            ring_i[0] += 1
            return ring_t[i], ring_b[i]

        def load_slab(w_d, r0, nrow_chunks, c0, ncols, tens, tb, dst_off=0):
            src = w_d[r0:r0 + nrow_chunks * 128, c0:c0 + ncols].rearrange("(c p) n -> p c n", p=128)
            dst = tens[:, dst_off:dst_off + nrow_chunks * ncols].rearrange("p (c n) -> p c n", n=ncols)
            P.dma("pool", dst, src, owner=tb, writes=[tb])
            return dst

        def setup():
            P.dma("sp", gcols[:, 0, :], g1_d.rearrange("(c p) -> p c", p=128), owner=cbuf, allow_slow_non_contiguous=True)
            P.dma("sp", gcols[:, 1, :], gm_d.rearrange("(c p) -> p c", p=128), owner=cbuf, allow_slow_non_contiguous=True)
            P.dma("sp", gcols[:, 2, :], g2_d.rearrange("(c p) -> p c", p=128), owner=cbuf, allow_slow_non_contiguous=True)
            P.dma("sp", gcols[:, 3, :], gf_d.rearrange("(c p) -> p c", p=128), owner=cbuf, allow_slow_non_contiguous=True)
            for k, o in enumerate((O_MLQ, O_MLK, O_FXQ, O_FXK)):
                P.dma("sp", bfm[:, k, :], bin_d[o:o + 1024].rearrange("(c p) -> p c", p=128), owner=cbuf, allow_slow_non_contiguous=True)
            for j in range(4):
                P.dma("sp", cwc[:, j, :], cw_d[j, :].rearrange("(c p) -> p c", p=128), owner=cbuf, allow_slow_non_contiguous=True)
            P.dma("sp", cbc[:, :], cb_d.rearrange("(c p) -> p c", p=128), owner=cbuf, allow_slow_non_contiguous=True)
            cfb = P.buf("cfill")

            def V(fn):
                return P.op("dve", fn, reads=[cfb], writes=[cfb])
            t = V(lambda e: e.memset(identb[:], 0.0))
            t2 = V(lambda e: e.memset(identf[:], 0.0))
            t3 = V(lambda e: e.memset(tri128[:], 1.0))
            V(lambda e: e.memset(onesms[:], 1.0 / D))
            V(lambda e: e.memset(onesb[:], 1.0))
            V(lambda e: e.memset(allones[:], 1.0))
            V(lambda e: e.memset(epsc[:], EPS))
            V(lambda e: e.memset(ones64[:], 0.0))
            V(lambda e: e.memset(oneslo[:], 0.0))
            V(lambda e: e.memset(oneshi[:], 0.0))
            t4 = V(lambda e: e.memset(ones64[0:64, 0:64], 1.0))
            t4 = V(lambda e: e.memset(ones64[64:128, 64:128], 1.0))
            V(lambda e: e.memset(oneslo[0:64, :], 1.0))
            t5 = V(lambda e: e.memset(oneshi[64:128, :], 1.0))
            aff = lambda tens, cm, pat, cmp: (lambda e: e.affine_select(
                out=tens[:], in_=tens[:], pattern=[[pat, 128]], compare_op=cmp, fill=(1.0 if cmp == ALU.not_equal else 0.0),
                base=0, channel_multiplier=cm))
            p1 = P.op("pool", aff(identb, 1, -1, ALU.not_equal), waits=[t5])
            p2 = P.op("pool", aff(identf, 1, -1, ALU.not_equal), waits=[t5, p1])
            p3 = P.op("pool", aff(tri128, -1, 1, ALU.is_ge), waits=[t5, p2])
            v = P.op("dve", lambda e: e.tensor_copy(out=tri64[:], in_=tri128[:]), waits=[p3])
            v = P.op("dve", lambda e: e.memset(tri64[0:64, 64:128], 0.0), waits=[v])
            v = P.op("dve", lambda e: e.tensor_copy(out=mask2[0:64, :], in_=tri128[0:64, 0:64]), waits=[v])
            v = P.op("dve", lambda e: e.tensor_copy(out=mask2[64:128, :], in_=tri128[64:128, 64:128]), waits=[v])
            v = P.op("dve", lambda e: e.tensor_scalar(out=maskneg[:], in0=tri128[:], scalar1=-1.0, scalar2=30000.0,
                                                      op0=ALU.add, op1=ALU.mult), waits=[v])
            for e_ in ("pe", "act", "dve", "sp"):
                P.wait_only(e_, [(cbuf.dsem, 16 * cbuf.dcnt)])
            P.barrier()

        bank_rr = [0]

        def rms_to_hT(gi, HT, hTb, ph, groups=range(NG), out_fn=None):
            if out_fn is None:
                with ExitStack() as tmp:
                    _rms(gi, HT, hTb, tmp, groups, None)
                    P.barrier()
            else:
                _rms(gi, HT, hTb, ph, groups, out_fn)

        def _rms(gi, HT, hTb, ph, groups, out_fn):
            sqt = sbt(ph, "sqt", [128, 2, NDC, 512], BF16)
            sq = [sqt[:, j, :, :] for j in range(2)]
            sqb = [P.buf("sq") for _ in range(2)]
            rst = sbt(ph, "rst", [128, 2, 512], F32)
            rs = [rst[:, j, :] for j in range(2)]
            rsb = [P.buf("rs") for _ in range(2)]
            for g in groups:
                j = g % 2
                tok = slice(g * 512, (g + 1) * 512)
                P.op("act", lambda e, j=j, tok=tok: e.activation(out=sq[j][:], in_=XT[:, :, tok], func=AF.Square),
                     reads=[XTb[g]], writes=[sqb[j]])
                bk = bank_rr[0] % 8
                bank_rr[0] += 1
                for dc in range(NDC):
                    P.op("pe", lambda e, j=j, dc=dc, bk=bk: e.matmul(banks[bk][:, :], lhsT=onesms[:], rhs=sq[j][:, dc, :],
                                                                        start=(dc == 0), stop=(dc == NDC - 1)),
                         reads=[sqb[j]], writes=[bankb[bk]], signal=(dc == NDC - 1))
                P.op("act", lambda e, j=j, bk=bk: e.activation(out=rs[j][:], in_=banks[bk][:, :], func=AF.Sqrt, bias=epsc[:]),
                     reads=[bankb[bk]], writes=[rsb[j]])
                P.op("dve", lambda e, j=j: e.reciprocal(out=rs[j][:], in_=rs[j][:]), reads=[rsb[j]], writes=[rsb[j]])
                if out_fn is not None:
                    out_fn(g, rs[j], rsb[j])
                    continue
                for dc in range(NDC):
                    P.op("dve", lambda e, j=j, dc=dc, tok=tok: e.scalar_tensor_tensor(
                        out=HT[:, dc, tok], in0=XT[:, dc, tok], scalar=gcols[:, gi, dc:dc + 1], in1=rs[j][:],
                        op0=ALU.mult, op1=ALU.mult), reads=[XTb[g], rsb[j]], writes=[hTb[g]])

        def phase_load():
            with ExitStack() as ph:
                xs = [sbt(ph, f"xs{j}", [128, D], F32) for j in range(4)]
                xsb = [P.buf("xs") for _ in range(4)]
                for i in range(NTI):
                    j = i % 4
                    P.dma("sp", xs[j][:], x_d[i * 128:(i + 1) * 128, :], owner=xsb[j], writes=[xsb[j]])
                    for h in range(2):
                        bk = bank_rr[0] % 8
                        bank_rr[0] += 1
                        for k in range(4):
                            dc = 4 * h + k
                            P.op("pe", lambda e, j=j, dc=dc, k=k, bk=bk: e.transpose(
                                out=banks[bk][:, k * 128:(k + 1) * 128], in_=xs[j][:, dc * 128:(dc + 1) * 128], identity=identf[:]),
                                reads=[xsb[j]], writes=[bankb[bk]], signal=(k == 3))
                        eng = "act" if (2 * i + h) % 2 == 0 else "dve"
                        src = lambda bk=bk: banks[bk][:, :].rearrange("p (c n) -> p c n", n=128)
                        dst = lambda i=i, h=h: XT[:, 4 * h:4 * h + 4, i * 128:(i + 1) * 128]
                        if eng == "act":
                            P.op("act", lambda e, src=src, dst=dst: e.copy(out=dst(), in_=src()), reads=[bankb[bk]], writes=[XTb[i // 4]])
                        else:
                            P.op("dve", lambda e, src=src, dst=dst: e.tensor_copy(out=dst(), in_=src()), reads=[bankb[bk]], writes=[XTb[i // 4]])
                P.barrier()

        def phase_ffn(gi, wg_d, wu_d, wd_d):
            with ExitStack() as ph:
                HT = sbt(ph, "ffn_hT", [128, NDC, T], BF16)
                hTb = [P.buf("hT") for _ in range(NG)]
                rms_to_hT(gi, HT, hTb, ph)
                TH = T // 2
                AT = sbt(ph, "ffn_aT", [128, NFC, TH], BF16)
                aTb = [P.buf("aT") for _ in range(2)]
                sgt = sbt(ph, "sgt", [128, 2, 512], BF16)
                sg = [sgt[:, j, :] for j in range(2)]
                sgb = [P.buf("sg") for _ in range(2)]
                cnt = 0
                for th in range(2):
                    for fp in range(NFC // 2):
                        tg, tgb = ring_next()
                        wgs = load_slab(wg_d, 0, NDC, fp * 256, 256, tg, tgb)
                        tu, tub = ring_next()
                        wus = load_slab(wu_d, 0, NDC, fp * 256, 256, tu, tub)
                        for fi in range(2):
                            fc = 2 * fp + fi
                            fcol = slice(fi * 128, (fi + 1) * 128)
                            bs = 4 * (cnt % 2)
                            cnt += 1
                            for (ws, wb, boff) in ((wgs, tgb, 0), (wus, tub, 2)):
                                for dc in range(NDC):
                                    for q in range(2):
                                        g = 2 * th + q
                                        bk = bs + boff + q
                                        P.op("pe", lambda e, ws=ws, dc=dc, fcol=fcol, g=g, bk=bk: e.matmul(
                                            banks[bk][:, :], lhsT=ws[:, dc, fcol], rhs=HT[:, dc, g * 512:(g + 1) * 512],
                                            start=(dc == 0), stop=(dc == NDC - 1)),
                                            reads=[wb, hTb[g]], writes=[bankb[bk]], signal=(dc == NDC - 1))
                            for q in range(2):
                                P.op("act", lambda e, q=q, bk=bs + q: e.activation(out=sg[q][:], in_=banks[bk][:, :], func=AF.Silu),
                                     reads=[bankb[bs + q]], writes=[sgb[q]])
                                P.op("dve", lambda e, q=q, bk=bs + 2 + q, fc=fc: e.tensor_tensor(
                                    out=AT[:, fc, q * 512:(q + 1) * 512], in0=sg[q][:], in1=banks[bk][:, :], op=ALU.mult),
                                    reads=[sgb[q], bankb[bs + 2 + q]], writes=[aTb[q]])
                    for dc in range(NDC):
                        t0, t0b = ring_next()
                        s0 = load_slab(wd_d, 0, 11, dc * 128, 128, t0, t0b)
                        t1, t1b = ring_next()
                        s1 = load_slab(wd_d, 11 * 128, 11, dc * 128, 128, t1, t1b)
                        bs = 2 * (dc % 4)
                        for fc in range(NFC):
                            ws, wb = (s0, t0b) if fc < 11 else (s1, t1b)
                            for q in range(2):
                                bk = bs + q
                                P.op("pe", lambda e, ws=ws, fc=fc, q=q, bk=bk: e.matmul(
                                    banks[bk][:, :], lhsT=ws[:, fc % 11, :], rhs=AT[:, fc, q * 512:(q + 1) * 512],
                                    start=(fc == 0), stop=(fc == NFC - 1)),
                                    reads=[wb, aTb[q]], writes=[bankb[bk]], signal=(fc == NFC - 1))
                        for q in range(2):
                            bk = bs + q
                            g = 2 * th + q
                            P.op("dve", lambda e, dc=dc, g=g, bk=bk: e.scalar_tensor_tensor(
                                out=XT[:, dc, g * 512:(g + 1) * 512], in0=banks[bk][:, :], scalar=0.5,
                                in1=XT[:, dc, g * 512:(g + 1) * 512], op0=ALU.mult, op1=ALU.add),
                                reads=[bankb[bk], XTb[g]], writes=[XTb[g]])
                P.barrier()

        def phase_final():
            with ExitStack() as ph:
                FTt = sbt(ph, "fT", [128, 2, NDC, 512], F32)
                FTs = [FTt[:, j, :, :] for j in range(2)]
                FTbs = [P.buf("fT") for _ in range(2)]
                ost = [sbt(ph, f"ost{j}", [128, D], F32) for j in range(2)]
                ostb = [P.buf("ost") for _ in range(2)]
                stores = []

                def out_fn(g, rs, rsb):
                    tok = slice(g * 512, (g + 1) * 512)
                    FT = FTs[g % 2]
                    FTb = FTbs[g % 2]
                    for dc in range(NDC):
                        P.op("dve", lambda e, dc=dc, tok=tok, rs=rs, FT=FT: e.scalar_tensor_tensor(
                            out=FT[:, dc, :], in0=XT[:, dc, tok], scalar=gcols[:, 3, dc:dc + 1], in1=rs[:],
                            op0=ALU.mult, op1=ALU.mult), reads=[XTb[g], rsb], writes=[FTb])
                    for it in range(4):
                        i = 4 * g + it
                        j = i % 2
                        for h in range(2):
                            bk = bank_rr[0] % 8
                            bank_rr[0] += 1
                            for k in range(4):
                                dc = 4 * h + k
                                P.op("pe", lambda e, dc=dc, k=k, bk=bk, it=it, FT=FT: e.transpose(
                                    out=banks[bk][:, k * 128:(k + 1) * 128], in_=FT[:, dc, it * 128:(it + 1) * 128], identity=identf[:]),
                                    reads=[FTb], writes=[bankb[bk]], signal=(k == 3))
                            if h == 0:
                                P.op("act", lambda e, j=j, bk=bk: e.copy(out=ost[j][:, 0:512], in_=banks[bk][:, :]),
                                     reads=[bankb[bk]], writes=[ostb[j]])
                            else:
                                P.op("dve", lambda e, j=j, bk=bk: e.tensor_copy(out=ost[j][:, 512:1024], in_=banks[bk][:, :]),
                                     reads=[bankb[bk]], writes=[ostb[j]])
                        stores.append(P.dma("sp", out_d[i * 128:(i + 1) * 128, :], ost[j][:], owner=ostb[j], reads=[ostb[j]]))

                rms_to_hT(3, None, None, ph, out_fn=out_fn)
                P.wait_only("sp", stores)

        def phase_mixer():
            with ExitStack() as ph:
                HT = sbt(ph, "mx_hT", [128, NDC, T], BF16)
                hTb = [P.buf("hT") for _ in range(NG)]
                Y = sbt(ph, "mx_Y", [128, NTI, D], BF16)
                Yb = [P.buf("Y") for _ in range(NTI)]
                rms_to_hT(1, HT, hTb, ph)
                GF = sbt(ph, "GF", [128, 2592], F32)
                GFW = 2592
                gpre = GF[:, 0:384].rearrange("p (j i) -> p j i", i=NTI)
                lf = GF[:, 384:704].rearrange("p (j i) -> p j i", i=NTI)
                tmpa = GF[:, 704:1024].rearrange("p (j i) -> p j i", i=NTI)
                tmpb = GF[:, 1024:1344].rearrange("p (j i) -> p j i", i=NTI)
                bsb = GF[:, 1344:1408]
                wk = GF[:, 1408:1472]
                wv = GF[:, 1472:1536]
                eb = GF[:, 1536:1600]
                ebL = GF[:, 1600:1728]
                tot = GF[:, 1728:1984].rearrange("p (h i) -> p h i", i=NTI)
                OFF_O = 1984
                off = GF[:, 1984:2240].rearrange("p (h i) -> p h i", i=NTI)
                cc_ = GF[:, 2240:2496].rearrange("p (h i) -> p h i", i=NTI)
                GB_O = 2496
                gbias = GF[:, 2496:2520]
                ieb = GF[:, 2528:2592]
                gb = P.buf("gates")

                def gates_block():
                    tg, tgb = ring_next()
                    gsl = tg[:, 0:NDC * 24].rearrange("p (c n) -> p c n", n=24)
                    P.dma("pool", gsl[:, :, 0:8], win_d[:, O_MLI:O_MLI + 8].rearrange("(c p) n -> p c n", p=128), owner=tgb, writes=[tgb])
                    P.dma("pool", gsl[:, :, 8:24], win_d[:, O_FXF:O_FXF + 16].rearrange("(c p) n -> p c n", p=128), owner=tgb, writes=[tgb])
                    P.dma("sp", gbias[:, 0:8], bin_d[O_MLI:O_MLI + 8].partition_broadcast(128), owner=gb, writes=[gb])
                    P.dma("sp", gbias[:, 8:24], bin_d[O_FXF:O_FXF + 16].partition_broadcast(128), owner=gb, writes=[gb])
                    bk = 0
                    for i in range(NTI):
                        for dc in range(NDC):
                            last = (i == NTI - 1 and dc == NDC - 1)
                            P.op("pe", lambda e, i=i, dc=dc: e.matmul(
                                banks[0][:, i * 24:(i + 1) * 24], lhsT=HT[:, dc, i * 128:(i + 1) * 128], rhs=gsl[:, dc, :],
                                start=(i == 0 and dc == 0), stop=(dc == NDC - 1), skip_group_check=True),
                                reads=[tgb, hTb[i // 4]], writes=[bankb[0]], signal=last)
                    P.op("dve", lambda e: e.tensor_tensor(
                        out=gpre.rearrange("p j i -> p i j"), in0=banks[0][:, 0:NTI * 24].rearrange("p (i j) -> p i j", j=24),
                        in1=bass.AP(GF, GB_O, [[GFW, 128], [0, NTI], [1, 24]]), op=ALU.add), reads=[bankb[0], gb], writes=[gb])
                    xg = gpre[:, 4:24, :]
                    DV = lambda fn: P.op("dve", fn, reads=[gb], writes=[gb])
                    AC = lambda fn: P.op("act", fn, reads=[gb], writes=[gb])
                    DV(lambda e: e.tensor_scalar(out=tmpa[:], in0=xg, scalar1=-1.0, scalar2=None, op0=ALU.mult))
                    DV(lambda e: e.tensor_tensor(out=tmpa[:], in0=tmpa[:], in1=xg, op=ALU.max))
                    AC(lambda e: e.activation(out=tmpa[:], in_=tmpa[:], func=AF.Exp, scale=-1.0))
                    AC(lambda e: e.activation(out=tmpa[:], in_=tmpa[:], func=AF.Ln, bias=1.0))
                    DV(lambda e: e.tensor_scalar(out=tmpb[:], in0=xg, scalar1=0.0, scalar2=None, op0=ALU.min))
                    DV(lambda e: e.tensor_tensor(out=lf[:], in0=tmpb[:], in1=tmpa[:], op=ALU.subtract))
                    lfa = GF[:, 384:448]
                    lfb = GF[:, 448:704]
                    li = GF[:, 0:64]
                    for k, m in enumerate((tri64, ones64, oneslo, oneshi)):
                        P.op("pe", lambda e, k=k, m=m: e.matmul(banks[1][:, k * 64:(k + 1) * 64], lhsT=m[:], rhs=lfa,
                                                                 start=(k == 0), stop=True, skip_group_check=True),
                             reads=[gb], writes=[bankb[1]], signal=(k == 3))
                    P.op("pe", lambda e: e.matmul(banks[2][:, 0:256], lhsT=tri128[:], rhs=lfb, start=True, stop=True, skip_group_check=True),
                         reads=[gb], writes=[bankb[2]], signal=False)
                    P.op("pe", lambda e: e.matmul(banks[2][:, 256:512], lhsT=allones[:], rhs=lfb, start=False, stop=True, skip_group_check=True),
                         reads=[gb], writes=[bankb[2]], signal=True)
                    A1 = lambda fn: P.op("act", fn, reads=[gb, bankb[1]], writes=[gb])
                    D1 = lambda fn: P.op("dve", fn, reads=[gb, bankb[1]], writes=[gb])
                    A1(lambda e: e.copy(out=bsb[:], in_=banks[1][:, 0:64]))
                    D1(lambda e: e.tensor_tensor(out=wk[:], in0=li, in1=bsb[:], op=ALU.subtract))
                    D1(lambda e: e.tensor_tensor(out=wv[:], in0=banks[1][:, 64:128], in1=bsb[:], op=ALU.subtract))
                    D1(lambda e: e.tensor_tensor(out=wv[:], in0=wv[:], in1=li, op=ALU.add))
                    A1(lambda e: e.activation(out=wk[:], in_=wk[:], func=AF.Exp, bias=-LN16))
                    A1(lambda e: e.activation(out=wv[:], in_=wv[:], func=AF.Exp, bias=-LN16))
                    A1(lambda e: e.activation(out=eb[:], in_=bsb[:], func=AF.Exp))
                    A1(lambda e: e.activation(out=ieb[:], in_=bsb[:], func=AF.Exp, scale=-1.0))
                    A1(lambda e: e.activation(out=ebL[:], in_=banks[1][:, 128:256], func=AF.Exp))
                    A2 = lambda fn: P.op("act", fn, reads=[gb, bankb[2]], writes=[gb])
                    D2 = lambda fn: P.op("dve", fn, reads=[gb, bankb[2]], writes=[gb])
                    A2(lambda e: e.copy(out=GF[:, 1728:1984], in_=banks[2][:, 256:512]))
                    D2(lambda e: e.memset(off[:, :, 0:1], 0.0))
                    for i in range(1, NTI):
                        D2(lambda e, i=i: e.tensor_tensor(out=off[:, :, i:i + 1], in0=off[:, :, i - 1:i], in1=tot[:, :, i - 1:i], op=ALU.add))
                    D2(lambda e: e.tensor_tensor(out=GF[:, 2240:2496], in0=banks[2][:, 0:256], in1=GF[:, 1984:2240], op=ALU.add))


                if not do_fox:
                    gates_block()

                if not do_fox:
                    for i in range(NTI):
                        P.op("dve", lambda e, i=i: e.memset(Y[:, i, :], 0.0), writes=[Yb[i]])
                with ExitStack() as pf:
                    qh = [sbt(pf, f"qh{j}", [128, T], BF16) for j in range(2)]
                    kh = [sbt(pf, f"kh{j}", [128, T], BF16) for j in range(2)]
                    qkb = [P.buf("qk2a"), P.buf("qk2b")]
                    VAf = sbt(pf, "VAf", [128, NTI, 2, 65], BF16)
                    VAb = P.buf("VA")
                    sgbt = sbt(pf, "sgb", [128, NTI, 128], BF16)
                    sgbb = P.buf("sgb")
                    BRf = sbt(pf, "BRf", [1, 512], BF16)
                    brb = P.buf("brow")
                    ones512 = sbt(pf, "ones512", [1, 512], BF16)
                    pTt = sbt(pf, "pTft", [128, 4, 512], BF16)
                    pTf = [pTt[:, j, :] for j in range(4)]
                    pTb = [P.buf("pT") for _ in range(4)]
                    rect = sbt(pf, "rect", [128, 2, 4], F32)
                    recb = [P.buf("rec") for _ in range(2)]
                    RW = sbt(pf, "RW", [128, 6, 256], F32)
                    PCS = sbt(pf, "PCS", [128, 6, 256], BF16)
                    rwb = P.buf("rw")
                    P.op("dve", lambda e: e.memset(VAf[:, :, :, 64:65], 1.0), writes=[VAb])
                    P.op("dve", lambda e: e.memset(ones512[:], 1.0), writes=[brb])
                    if do_fox:
                        for j in range(2):
                            P.op("dve", lambda e, j=j: e.memset(qh[j][64:70, :], 1.0), writes=[qkb[j]])
                            P.op("dve", lambda e, j=j: e.memset(kh[j][64:70, :], 1.0), writes=[qkb[j]])
                    def fox_rows():
                        for (k, src) in ((0, cc_), (1, off)):
                            for par in range(2):
                                P.op("dve", lambda e, k=k, src=src, par=par: e.tensor_copy(
                                    out=RW[:, 2 + k, par * 128:(par + 1) * 128].rearrange("p (h g) -> p h g", g=8),
                                    in_=src[:, :, par::2]), reads=[gb], writes=[rwb])
                            for par in range(2):
                                P.op("pe", lambda e, k=k, par=par: e.transpose(
                                    out=banks[6][:, k * 256 + par * 128:k * 256 + (par + 1) * 128],
                                    in_=RW[:, 2 + k, par * 128:(par + 1) * 128], identity=identf),
                                    reads=[rwb], writes=[bankb[6]], signal=(par == 1))
                            P.op("act", lambda e, k=k: e.copy(out=RW[:, k, :], in_=banks[6][:, k * 256:(k + 1) * 256]),
                                 reads=[bankb[6]], writes=[rwb])
                        DR = lambda fn: P.op("dve", fn, reads=[rwb], writes=[rwb])
                        for (k, pc0, mul) in ((1, 0, 8.0), (0, 3, -8.0)):
                            x8 = RW[:, 2, :]
                            r1 = RW[:, 3, :]
                            DR(lambda e, k=k, mul=mul, x8=x8: e.tensor_scalar(out=x8, in0=RW[:, k, :], scalar1=mul, scalar2=None, op0=ALU.mult))
                            DR(lambda e, pc0=pc0, x8=x8: e.tensor_copy(out=PCS[:, pc0, :], in_=x8))
                            DR(lambda e, pc0=pc0, x8=x8, r1=r1: e.tensor_tensor(out=r1, in0=x8, in1=PCS[:, pc0, :], op=ALU.subtract))
                            DR(lambda e, pc0=pc0, r1=r1: e.tensor_copy(out=PCS[:, pc0 + 1, :], in_=r1))
                            DR(lambda e, pc0=pc0, r1=r1: e.tensor_tensor(out=r1, in0=r1, in1=PCS[:, pc0 + 1, :], op=ALU.subtract))
                            DR(lambda e, pc0=pc0, r1=r1: e.tensor_copy(out=PCS[:, pc0 + 2, :], in_=r1))
                        for j in range(6):
                            P.dma("sp", scr_d[j].rearrange("h (g n) -> (h g) n", n=256), PCS[:, j, :], owner=rwb, reads=[rwb], writes=[scrb])
                    scnt = 0
                    for p in (range(8) if do_fox else ()):
                        tq, tqb = ring_next()
                        wq = load_slab(win_d, 0, NDC, O_FXQ + p * 128, 128, tq, tqb, 0)
                        wkk = load_slab(win_d, 0, NDC, O_FXK + p * 128, 128, tq, tqb, NDC * 128)
                        tv, tvb = ring_next()
                        wvg = tv[:, 0:NDC * 256].rearrange("p (c n) -> p c n", n=256)
                        P.dma("pool", wvg[:, :, 0:128], win_d[:, O_FXV + p * 128:O_FXV + (p + 1) * 128].rearrange("(c p) n -> p c n", p=128),
                              owner=tvb, writes=[tvb])
                        P.dma("pool", wvg[:, :, 128:256], win_d[:, O_GB + p * 128:O_GB + (p + 1) * 128].rearrange("(c p) n -> p c n", p=128),
                              owner=tvb, writes=[tvb])
                        for k, o in enumerate((O_FXV, O_GB, O_FXQ, O_FXK)):
                            P.dma("pool", BRf[:, k * 128:(k + 1) * 128], bin_d[o + p * 128:o + (p + 1) * 128].rearrange("(o n) -> o n", o=1),
                                  owner=brb, writes=[brb])
                        for (ws, dst, bseg) in ((wq, qh, 2), (wkk, kh, 3)):
                            for g in range(NG):
                                bk = 6 + (scnt % 2)
                                scnt += 1
                                for dc in range(NDC):
                                    P.op("pe", lambda e, ws=ws, dc=dc, g=g, bk=bk: e.matmul(
                                        banks[bk][:, :], lhsT=ws[:, dc, :], rhs=HT[:, dc, g * 512:(g + 1) * 512],
                                        start=(dc == 0), stop=False),
                                        reads=[tqb, hTb[g]], writes=[bankb[bk]], signal=False)
                                P.op("pe", lambda e, bk=bk, bseg=bseg: e.matmul(
                                    banks[bk][:, :], lhsT=BRf[:, bseg * 128:(bseg + 1) * 128], rhs=ones512[:], start=False, stop=True),
                                    reads=[brb], writes=[bankb[bk]], signal=True)
                                P.op("act", lambda e, dst=dst, g=g, bk=bk: e.copy(out=dst[0][0:64, g * 512:(g + 1) * 512], in_=banks[bk][0:64, :]),
                                     reads=[bankb[bk]], writes=[qkb[0]])
                                P.op("dve", lambda e, dst=dst, g=g, bk=bk: e.tensor_copy(out=dst[1][0:64, g * 512:(g + 1) * 512], in_=banks[bk][64:128, :]),
                                     reads=[bankb[bk]], writes=[qkb[1]])
                        for i in range(NTI):
                            bk = 6 + (scnt % 2)
                            scnt += 1
                            for dc in range(NDC):
                                P.op("pe", lambda e, wvg=wvg, dc=dc, i=i, bk=bk: e.matmul(
                                    banks[bk][:, 0:256], lhsT=HT[:, dc, i * 128:(i + 1) * 128], rhs=wvg[:, dc, :],
                                    start=(dc == 0), stop=False, skip_group_check=True),
                                    reads=[tvb, hTb[i // 4]], writes=[bankb[bk]], signal=False)
                            P.op("pe", lambda e, bk=bk: e.matmul(banks[bk][:, 0:256], lhsT=onesb[0:1, :], rhs=BRf[:, 0:256], start=False, stop=True,
                                                                 skip_group_check=True), reads=[brb], writes=[bankb[bk]], signal=True)
                            P.op("act", lambda e, i=i, bk=bk: e.copy(out=VAf[:, i, :, 0:64], in_=banks[bk][:, 0:128].rearrange("p (h d) -> p h d", d=64)),
                                 reads=[bankb[bk]], writes=[VAb])
                            P.op("act", lambda e, i=i, bk=bk: e.activation(out=sgbt[:, i, :], in_=banks[bk][:, 128:256], func=AF.Sigmoid),
                                 reads=[bankb[bk]], writes=[sgbb])
                        if p == 0:
                            gates_block()
                            fox_rows()
                        for hl in range(2):
                            h = 2 * p + hl
                            P.dma("sp", qh[hl][67:70, :], scr_d[0:3, h, :], owner=qkb[hl], reads=[scrb], writes=[qkb[hl]])
                            P.dma("sp", kh[hl][64:67, :], scr_d[3:6, h, :], owner=qkb[hl], reads=[scrb], writes=[qkb[hl]])
                        items = [(hl, tg, kb) for hl in range(2) for tg in range(NG) for kb in range(4 * tg + 4)]
                        LOOK = 3

                        def emit_s(n):
                            hl, tg, kb = items[n]
                            sb_ = n % 4
                            t0 = max(kb * 128, tg * 512)
                            t1 = (tg + 1) * 512
                            N = t1 - t0
                            diag = (kb >= 4 * tg)
                            P.op("pe", lambda e, hl=hl, kb=kb, t0=t0, t1=t1, N=N, sb_=sb_, diag=diag: e.matmul(
                                banks[sb_][:, 0:N], lhsT=kh[hl][0:70, kb * 128:(kb + 1) * 128], rhs=qh[hl][0:70, t0:t1],
                                start=True, stop=(not diag)), reads=[qkb[hl]], writes=[bankb[sb_]], signal=(not diag))
                            if diag:
                                P.op("pe", lambda e, sb_=sb_: e.matmul(banks[sb_][:, 0:128], lhsT=identb, rhs=maskneg,
                                                                      start=False, stop=True), writes=[bankb[sb_]], signal=True)
                            P.op("act", lambda e, sb_=sb_, N=N: e.activation(out=pTf[sb_][:, 0:N], in_=banks[sb_][:, 0:N], func=AF.Exp, scale=0.125),
                                 reads=[bankb[sb_]], writes=[pTb[sb_]])

                        def emit_pv(n):
                            hl, tg, kb = items[n]
                            h = 2 * p + hl
                            sb_ = n % 4
                            ab = 4 + (tg % 2)
                            tb0 = max(kb, 4 * tg)
                            for tb in range(tb0, 4 * tg + 4):
                                j = tb - 4 * tg
                                c0 = (tb - tb0) * 128
                                first = (kb == 0 and j == 0)
                                last = (kb == 4 * tg + 3)
                                P.op("pe", lambda e, sb_=sb_, kb=kb, hl=hl, ab=ab, j=j, c0=c0, first=first: e.matmul(
                                    banks[ab][:, j * 65:(j + 1) * 65], lhsT=pTf[sb_][:, c0:c0 + 128], rhs=VAf[:, kb, hl, :],
                                    start=first, stop=True, skip_group_check=True),
                                    reads=[pTb[sb_], VAb], writes=[bankb[ab]], signal=(tb == 4 * tg + 3))
                            if kb == 4 * tg + 3:
                                rj = tg % 2
                                P.op("dve", lambda e, rj=rj, ab=ab: e.reciprocal(
                                    out=rect[:, rj, :], in_=banks[ab][:, 0:260].rearrange("p (j c) -> p j c", c=65)[:, :, 64]),
                                    reads=[bankb[ab]], writes=[recb[rj]])
                                for j in range(4):
                                    tb = 4 * tg + j
                                    P.op("dve", lambda e, rj=rj, ab=ab, tb=tb, j=j, hl=hl, h=h: e.scalar_tensor_tensor(
                                        out=Y[:, tb, h * 64:(h + 1) * 64], in0=banks[ab][:, j * 65:j * 65 + 64], scalar=rect[:, rj, j:j + 1],
                                        in1=sgbt[:, tb, hl * 64:(hl + 1) * 64], op0=ALU.mult, op1=ALU.mult),
                                        reads=[bankb[ab], recb[rj], sgbb], writes=[Yb[tb]])

                        for n in range(min(LOOK, len(items))):
                            emit_s(n)
                        for n in range(len(items)):
                            if n + LOOK < len(items):
                                emit_s(n + LOOK)
                            emit_pv(n)
                    P.barrier()


                with ExitStack() as pm:
                    qT = sbt(pm, "qT", [128, 2, T], BF16)
                    kT = sbt(pm, "kT", [128, 2, T], BF16)
                    qkb = P.buf("qk")
                    VA = sbt(pm, "VAm", [128, NTI, 258], BF16)
                    VAb = P.buf("VAm")
                    GAt = sbt(pm, "GAt", [128, 3, 256], BF16)
                    GA = [GAt[:, j, :] for j in range(3)]
                    GAb = [P.buf("GA") for _ in range(3)]
                    gn = sbt(pm, "gn", [128, 256], F32)
                    gnb = P.buf("gn")
                    BR = sbt(pm, "BRm", [1, 768], BF16)
                    brb = P.buf("browm")
                    stgt = sbt(pm, "stgt", [128, 2, 516], F32)
                    stg = [stgt[:, j, :] for j in range(2)]
                    stgb = [P.buf("stg") for _ in range(2)]
                    cacct = sbt(pm, "cacc", [128, 2, 512], F32)
                    cacc = [cacct[:, j, :] for j in range(2)]
                    caccb = [P.buf("cacc") for _ in range(2)]
                    sig = sbt(pm, "sig", [128, 512], F32)
                    sigb = P.buf("sig")
                    CT = sbt(pm, "CT", [128, 2, 257], F32)
                    CTb = P.buf("CT")
                    CTht = sbt(pm, "CTht", [128, 2, 2, 258], BF16)
                    CTh = [CTht[:, j, :, 0:257] for j in range(2)]
                    CThb = [P.buf("CTh") for _ in range(2)]
                    ktokt = sbt(pm, "ktokt", [128, 2, 256], BF16)
                    ktok = [ktokt[:, j, :] for j in range(2)]
                    ktokb = [P.buf("ktok") for _ in range(2)]
                    pTt = sbt(pm, "pTmt", [128, 2, 64], BF16)
                    pT = [pTt[:, j, :] for j in range(2)]
                    pTb = [P.buf("pTm") for _ in range(2)]
                    hnnt = sbt(pm, "hnnt", [128, 2, 256], BF16)
                    hnn = [hnnt[:, j, :] for j in range(2)]
                    hnnb = [P.buf("hnn") for _ in range(2)]
                    smt_ = sbt(pm, "smt", [128, 2, 16], F32)
                    sm = [smt_[:, j, :] for j in range(2)]
                    smb = [P.buf("sm") for _ in range(2)]
                    scnt = 0
                    for hd in (range(4) if do_mlstm else ()):
                        tq, tqb = ring_next()
                        wq = load_slab(win_d, 0, NDC, O_MLQ + hd * 256, 256, tq, tqb)
                        tk, tkb = ring_next()
                        wkk = load_slab(win_d, 0, NDC, O_MLK + hd * 256, 256, tk, tkb)
                        tv, tvb = ring_next()
                        wvv = load_slab(win_d, 0, NDC, O_MLV + hd * 256, 256, tv, tvb)
                        for k, o in enumerate((O_MLV, O_MLO, O_GA)):
                            P.dma("pool", BR[:, k * 256:(k + 1) * 256], bin_d[o + hd * 256:o + (hd + 1) * 256].rearrange("(o n) -> o n", o=1),
                                  owner=brb, writes=[brb])
                        P.dma("sp", gn[:], hn_d[hd * 256:(hd + 1) * 256].partition_broadcast(128), owner=gnb, writes=[gnb])
                        pend = [None]

                        def flush_silu():
                            if pend[0] is not None:
                                dstv, cj = pend[0]
                                P.op("act", lambda e, dstv=dstv, cj=cj: e.activation(out=dstv, in_=cacc[cj], func=AF.Silu),
                                     reads=[caccb[cj]], writes=[qkb])
                                pend[0] = None

                        for (ws, wb, dstT, bcol, cofs) in ((wq, tqb, qT, 0, 0), (wkk, tkb, kT, 1, 8)):
                            for cc in range(2):
                                ch = hd * 2 + cc
                                cch = cofs + ch
                                for g in range(NG):
                                    bk = (scnt % 2)
                                    j = scnt % 2
                                    scnt += 1
                                    for dc in range(NDC):
                                        P.op("pe", lambda e, ws=ws, dc=dc, cc=cc, g=g, bk=bk: e.matmul(
                                            banks[bk][:, :], lhsT=ws[:, dc, cc * 128:(cc + 1) * 128], rhs=HT[:, dc, g * 512:(g + 1) * 512],
                                            start=(dc == 0), stop=(dc == NDC - 1)),
                                            reads=[wb, hTb[g]], writes=[bankb[bk]], signal=(dc == NDC - 1))
                                    if g == 0:
                                        P.op("dve", lambda e, j=j: e.memset(stg[j][:, 0:3], 0.0), writes=[stgb[j]])
                                    else:
                                        P.op("dve", lambda e, j=j: e.tensor_copy(out=stg[j][:, 0:3], in_=stg[1 - j][:, 512:515]),
                                             reads=[stgb[1 - j]], writes=[stgb[j]])
                                    P.op("act", lambda e, j=j, bk=bk, bcol=bcol, ch=ch: e.activation(
                                        out=stg[j][:, 3:515], in_=banks[bk][:, :], func=AF.Identity, bias=bfm[:, bcol, ch:ch + 1]),
                                        reads=[bankb[bk]], writes=[stgb[j]])
                                    flush_silu()
                                    P.op("dve", lambda e, j=j, cch=cch: e.tensor_scalar(
                                        out=cacc[j], in0=stg[j][:, 3:515], scalar1=cwc[:, 3, cch:cch + 1], scalar2=cbc[:, cch:cch + 1],
                                        op0=ALU.mult, op1=ALU.add), reads=[stgb[j]], writes=[caccb[j]])
                                    for jj in range(3):
                                        P.op("dve", lambda e, j=j, cch=cch, jj=jj: e.scalar_tensor_tensor(
                                            out=cacc[j], in0=stg[j][:, jj:jj + 512], scalar=cwc[:, jj, cch:cch + 1], in1=cacc[j],
                                            op0=ALU.mult, op1=ALU.add), reads=[stgb[j], caccb[j]], writes=[caccb[j]])
                                    pend[0] = (dstT[:, cc, g * 512:(g + 1) * 512], j)
                        flush_silu()
                        to, tob = ring_next()
                        woo = load_slab(win_d, 0, NDC, O_MLO + hd * 256, 256, to, tob)
                        ta, tab = ring_next()
                        wga = load_slab(win_d, 0, NDC, O_GA + hd * 256, 256, ta, tab)
                        P.op("dve", lambda e: e.memset(VA[:, :, 256:257], 1.0), writes=[VAb])
                        for i in range(NTI):
                            bk = (scnt % 2)
                            scnt += 1
                            for dc in range(NDC):
                                P.op("pe", lambda e, dc=dc, i=i, bk=bk, wvv=wvv: e.matmul(
                                    banks[bk][:, 0:256], lhsT=HT[:, dc, i * 128:(i + 1) * 128], rhs=wvv[:, dc, :],
                                    start=(dc == 0), stop=False, skip_group_check=True),
                                    reads=[tvb, hTb[i // 4]], writes=[bankb[bk]], signal=False)
                            P.op("pe", lambda e, bk=bk: e.matmul(banks[bk][:, 0:256], lhsT=onesb[0:1, :], rhs=BR[:, 0:256], start=False, stop=True,
                                                                 skip_group_check=True), reads=[brb], writes=[bankb[bk]], signal=True)
                            P.op("act", lambda e, i=i, bk=bk: e.copy(out=VA[:, i, 0:256], in_=banks[bk][:, 0:256]),
                                 reads=[bankb[bk]], writes=[VAb])
                        P.op("dve", lambda e: e.memset(CT[:], 0.0), writes=[CTb])
                        P.op("dve", lambda e: e.memset(CTh[0], 0.0), writes=[CThb[0]])
                        b3bf = banks[3].bitcast(BF16)
                        UB = ((6, 7), (6, 7))

                        def gates(i):
                            s3 = i % 3
                            for dc in range(NDC):
                                for (ws, wb2, c0) in ((woo, tob, 0), (wga, tab, 256)):
                                    P.op("pe", lambda e, ws=ws, dc=dc, i=i, c0=c0: e.matmul(
                                        banks[0][:, c0:c0 + 256], lhsT=HT[:, dc, i * 128:(i + 1) * 128], rhs=ws[:, dc, :],
                                        start=(dc == 0 and c0 == 0), stop=False, skip_group_check=True),
                                        reads=[wb2, hTb[i // 4]], writes=[bankb[0]], signal=False)
                            P.op("pe", lambda e: e.matmul(banks[0][:, :], lhsT=onesb[0:1, :], rhs=BR[:, 256:768], start=False, stop=True,
                                                          skip_group_check=True), reads=[brb], writes=[bankb[0]], signal=True)
                            P.op("act", lambda e: e.activation(out=sig[:], in_=banks[0][:, :], func=AF.Exp, scale=-1.0),
                                 reads=[bankb[0]], writes=[sigb])
                            P.op("act", lambda e: e.activation(out=sig[:], in_=sig[:], func=AF.Ln, bias=1.0), reads=[sigb], writes=[sigb])
                            P.op("dve", lambda e: e.tensor_tensor(out=sig[:, 0:256], in0=sig[:, 0:256], in1=sig[:, 256:512], op=ALU.add),
                                 reads=[sigb], writes=[sigb])

                        def gatesB(i):
                            s3 = i % 3
                            P.op("act", lambda e: e.activation(out=sig[:, 256:512], in_=sig[:, 0:256], func=AF.Exp, scale=-1.0),
                                 reads=[sigb], writes=[sigb])
                            P.op("dve", lambda e, s3=s3: e.tensor_tensor(out=GA[s3], in0=sig[:, 256:512], in1=gn[:], op=ALU.mult),
                                 reads=[sigb, gnb], writes=[GAb[s3]])

                        def pre(c):
                            i, hf = c // 2, c % 2
                            rows = slice(hf * 64, hf * 64 + 64)
                            tok = slice(c * 64, (c + 1) * 64)
                            s2 = c % 2
                            gcol = hd * NTI + i
                            for cc in range(2):
                                P.op("pe", lambda e, rows=rows, cc=cc, tok=tok: e.transpose(
                                    out=b3bf[rows, cc * 128:(cc + 1) * 128], in_=kT[:, cc, tok], identity=identb),
                                    reads=[qkb], writes=[bankb[3]], signal=(cc == 1))
                            P.op("act", lambda e, rows=rows, s2=s2, gcol=gcol: e.activation(
                                out=ktok[s2][rows, :], in_=b3bf[rows, 0:256], func=AF.Copy, scale=wv[rows, gcol:gcol + 1]),
                                reads=[bankb[3], gb], writes=[ktokb[s2]])
                            for cc in range(2):
                                P.op("pe", lambda e, rows=rows, cc=cc, tok=tok: e.matmul(
                                    banks[2][rows, 0:64], lhsT=kT[:, cc, tok], rhs=qT[:, cc, tok], start=(cc == 0), stop=(cc == 1)),
                                    reads=[qkb], writes=[bankb[2]], signal=(cc == 1))
                            P.op("dve", lambda e, rows=rows, s2=s2, gcol=gcol: e.scalar_tensor_tensor(
                                out=pT[s2][rows, :], in0=banks[2][rows, 0:64], scalar=wk[rows, gcol:gcol + 1], in1=mask2[rows, :],
                                op0=ALU.mult, op1=ALU.mult), reads=[bankb[2], gb], writes=[pTb[s2]])

                        def main(c):
                            i, hf = c // 2, c % 2
                            rows = slice(hf * 64, hf * 64 + 64)
                            tok = slice(c * 64, (c + 1) * 64)
                            s2 = c % 2
                            s3 = i % 2
                            gcol = hd * NTI + i
                            ob = 4 + (i % 2)
                            cur = c % 2
                            ub = UB[c % 2]
                            for cc in range(2):
                                P.op("pe", lambda e, rows=rows, cc=cc, s2=s2, ub=ub, i=i: e.matmul(
                                    banks[ub[cc]][:, 0:257], lhsT=ktok[s2][rows, cc * 128:(cc + 1) * 128], rhs=VA[rows, i, 0:257],
                                    start=True, stop=True), reads=[ktokb[s2], VAb], writes=[bankb[ub[cc]]], signal=True)
                            P.op("pe", lambda e, rows=rows, s2=s2, i=i, ob=ob: e.matmul(
                                banks[ob][rows, 0:257], lhsT=pT[s2][rows, :], rhs=VA[rows, i, 0:257], start=True, stop=False),
                                reads=[pTb[s2], VAb], writes=[bankb[ob]], signal=False)
                            for cc in range(2):
                                P.op("pe", lambda e, rows=rows, cc=cc, tok=tok, ob=ob, cur=cur: e.matmul(
                                    banks[ob][rows, 0:257], lhsT=qT[:, cc, tok], rhs=CTh[cur][:, cc, :], start=False, stop=(cc == 1)),
                                    reads=[qkb, CThb[cur]], writes=[bankb[ob]], signal=(cc == 1))
                            ecol = hf * 64 + gcol
                            P.op("dve", lambda e, ecol=ecol: e.scalar_tensor_tensor(
                                out=CT[:], in0=CT[:], scalar=ebL[:, ecol:ecol + 1],
                                in1=bank67[:, :].rearrange("p (c n) -> p c n", n=512)[:, :, 0:257],
                                op0=ALU.mult, op1=ALU.add), reads=[CTb, bankb[6], bankb[7], gb], writes=[CTb])
                            P.op("act", lambda e, cur=cur: e.copy(out=CTh[1 - cur], in_=CT[:]), reads=[CTb], writes=[CThb[1 - cur]])

                        def postA(i):
                            s3 = i % 2
                            ob = 4 + (i % 2)
                            gcol = hd * NTI + i
                            smt = sm[s3]
                            SD = lambda fn: P.op("dve", fn, reads=[smb[s3], bankb[ob], gb], writes=[smb[s3]])
                            SD(lambda e: e.tensor_scalar(out=smt[:, 0:1], in0=banks[ob][:, 256:257], scalar1=-1.0, scalar2=None, op0=ALU.mult))
                            SD(lambda e: e.tensor_tensor(out=smt[:, 0:1], in0=smt[:, 0:1], in1=banks[ob][:, 256:257], op=ALU.max))
                            SD(lambda e: e.tensor_tensor(out=smt[:, 1:2], in0=smt[:, 0:1], in1=ieb[:, gcol:gcol + 1], op=ALU.max))
                            SD(lambda e: e.reciprocal(out=smt[:, 2:3], in_=smt[:, 1:2]))
                            SD(lambda e: e.bn_stats(smt[:, 4:10], banks[ob][:, 0:256]))
                            SD(lambda e: e.bn_aggr(smt[:, 10:12], smt[:, 4:10]))
                            SD(lambda e: e.scalar_tensor_tensor(out=smt[:, 3:4], in0=smt[:, 2:3], scalar=smt[:, 2:3], in1=smt[:, 11:12],
                                                                op0=ALU.mult, op1=ALU.mult))

                        def postB(i):
                            s3 = i % 2
                            smt = sm[s3]
                            SA = lambda fn: P.op("act", fn, reads=[smb[s3]], writes=[smb[s3]])
                            SA(lambda e: e.activation(out=smt[:, 12:13], in_=smt[:, 3:4], func=AF.Ln, bias=epsc))
                            SA(lambda e: e.activation(out=smt[:, 13:14], in_=smt[:, 12:13], func=AF.Exp, scale=-0.5))

                        def postC(i):
                            s3 = i % 2
                            ob = 4 + (i % 2)
                            smt = sm[s3]
                            SD = lambda fn: P.op("dve", fn, reads=[smb[s3]], writes=[smb[s3]])
                            SD(lambda e: e.tensor_tensor(out=smt[:, 14:15], in0=smt[:, 2:3], in1=smt[:, 13:14], op=ALU.mult))
                            SD(lambda e: e.scalar_tensor_tensor(out=smt[:, 15:16], in0=smt[:, 10:11], scalar=-1.0, in1=smt[:, 14:15],
                                                                op0=ALU.mult, op1=ALU.mult))
                            P.op("act", lambda e: e.activation(out=hnn[s3], in_=banks[ob][:, 0:256], func=AF.Identity,
                                                               scale=smt[:, 14:15], bias=smt[:, 15:16]),
                                 reads=[bankb[ob], smb[s3]], writes=[hnnb[s3]])

                        def postE(i):
                            s3 = i % 2
                            P.op("dve", lambda e: e.tensor_tensor(out=hnn[s3], in0=hnn[s3], in1=GA[i % 3], op=ALU.mult),
                                 reads=[hnnb[s3], GAb[i % 3]], writes=[hnnb[s3]])
                            ydst = Y[:, i, hd * 256:(hd + 1) * 256]
                            P.op("dve", lambda e: e.tensor_tensor(out=ydst, in0=ydst, in1=hnn[s3], op=ALU.add),
                                 reads=[hnnb[s3], Yb[i]], writes=[Yb[i]])

                        NCH = T // 64
                        gates(0)
                        gatesB(0)
                        pre(0)
                        for c in range(NCH):
                            if c + 1 < NCH:
                                pre(c + 1)
                            if c % 2 == 0 and c >= 2:
                                postB(c // 2 - 1)
                            main(c)
                            if c % 2 == 1:
                                postA(c // 2)
                                if c >= 3:
                                    postE((c - 3) // 2)
                                if c // 2 + 1 < NTI:
                                    gatesB(c // 2 + 1)
                            else:
                                if c >= 2:
                                    postC(c // 2 - 1)
                                if c // 2 + 1 < NTI:
                                    gates(c // 2 + 1)
                        postB(NTI - 1)
                        postC(NTI - 1)
                        postE(NTI - 1)
                    P.barrier()

                for i in range(NTI):
                    for h in range(2):
                        bk = bank_rr[0] % 8
                        bank_rr[0] += 1
                        bbf = banks[bk].bitcast(BF16)
                        for k in range(4):
                            dc = 4 * h + k
                            P.op("pe", lambda e, dc=dc, k=k, i=i, bbf=bbf: e.transpose(
                                out=bbf[:, k * 128:(k + 1) * 128], in_=Y[:, i, dc * 128:(dc + 1) * 128], identity=identb[:]),
                                reads=[Yb[i]], writes=[bankb[bk]], signal=(k == 3))
                        src = bbf[:, 0:512].rearrange("p (c n) -> p c n", n=128)
                        dst = HT[:, 4 * h:4 * h + 4, i * 128:(i + 1) * 128]
                        if (2 * i + h) % 2 == 0:
                            P.op("act", lambda e, src=src, dst=dst: e.copy(out=dst, in_=src), reads=[bankb[bk]], writes=[hTb[i // 4]])
                        else:
                            P.op("dve", lambda e, src=src, dst=dst: e.tensor_copy(out=dst, in_=src), reads=[bankb[bk]], writes=[hTb[i // 4]])
                for nn in range(4):
                    tw, twb = ring_next()
                    wo = load_slab(wo_d, 0, NDC, nn * 256, 256, tw, twb)
                    for hh in range(2):
                        ncx = 2 * nn + hh
                        bs = 4 * (ncx % 2)
                        for dc in range(NDC):
                            for g in range(NG):
                                P.op("pe", lambda e, dc=dc, g=g, hh=hh, bs=bs, wo=wo: e.matmul(
                                    banks[bs + g][:, :], lhsT=wo[:, dc, hh * 128:(hh + 1) * 128], rhs=HT[:, dc, g * 512:(g + 1) * 512],
                                    start=(dc == 0), stop=(dc == NDC - 1)),
                                    reads=[twb, hTb[g]], writes=[bankb[bs + g]], signal=(dc == NDC - 1))
                        for g in range(NG):
                            P.op("dve", lambda e, ncx=ncx, g=g, bs=bs: e.tensor_tensor(
                                out=XT[:, ncx, g * 512:(g + 1) * 512], in0=banks[bs + g][:, :], in1=XT[:, ncx, g * 512:(g + 1) * 512], op=ALU.add),
                                reads=[bankb[bs + g], XTb[g]], writes=[XTb[g]])
                P.barrier()
            return None

        setup()
        phase_load()
        if stop_after not in ("load",):
            phase_ffn(0, wg1_d, wu1_d, wd1_d)
        if stop_after not in ("load", "ffn1"):
            phase_mixer()
        if stop_after not in ("load", "ffn1", "mixer"):
            phase_ffn(2, wg2_d, wu2_d, wd2_d)
        phase_final()
        P.replay()
    return nc


_NC_CACHE = {}
_NAMES = ("ffn1_norm", "ffn1_w_gate", "ffn1_w_up", "ffn1_w_down", "mix_norm", "w_in", "b_in", "conv_w", "conv_b",
          "ml_head_norm", "w_out", "ffn2_norm", "ffn2_w_gate", "ffn2_w_up", "ffn2_w_down")


def kernel(**inputs):
    stop_after = inputs.pop("_stop_after", "all")
    do_fox = inputs.pop("_do_fox", True)
    do_mlstm = inputs.pop("_do_mlstm", True)
    x = np.ascontiguousarray(np.asarray(inputs["x"], dtype=np.float32))
    shared = {}
    for n in _NAMES:
        a = np.asarray(inputs[n], dtype=np.float32)
        shared[n] = np.ascontiguousarray(a.reshape(a.shape[1:]))
    shared["final_norm"] = np.ascontiguousarray(np.asarray(inputs["final_norm"], dtype=np.float32))
    key = (stop_after, do_fox, do_mlstm)
    if key not in _NC_CACHE:
        _NC_CACHE[key] = build(stop_after, do_fox, do_mlstm)
    nc = _NC_CACHE[key]
    in_maps = []
    for b in range(8):
        m = {"x": np.ascontiguousarray(x[b])}
        m.update(shared)
        in_maps.append(m)
    res = run_bass_kernel_spmd(nc, in_maps, core_ids=list(range(8)))
    return np.stack([np.asarray(r["out"], dtype=np.float32) for r in res.results], axis=0)
```
